# Optimizing a Trainium2 kernel written in Bass

```python
import jax, jax.numpy as jnp
from jax import lax
import numpy as np

D_MODEL = 1024
BATCH = 4
SEQ = 4096
DEPTH = 1

CONV_DIM = D_MODEL
CONV_GROUPS = 8
CONV_K = 3
MLSTM_HEADS = 4
MLSTM_DQK = 128
MLSTM_DV = 256
MLSTM_DIM = MLSTM_HEADS * MLSTM_DV
CHUNK = 128
D_FF = 2816
N_MOD = 6
EPS = 1e-6
IN_SIZES = (CONV_DIM, CONV_DIM, CONV_DIM,
            MLSTM_HEADS * MLSTM_DQK, MLSTM_HEADS * MLSTM_DQK,
            MLSTM_DIM, MLSTM_DIM, MLSTM_HEADS, MLSTM_HEADS,
            D_MODEL, D_MODEL)
N_IN = 3 * CONV_DIM + 2 * MLSTM_HEADS * MLSTM_DQK + 2 * MLSTM_DIM + 2 * MLSTM_HEADS + 2 * D_MODEL
F_GATE_OFFSET = 3 * CONV_DIM + 2 * MLSTM_HEADS * MLSTM_DQK + 2 * MLSTM_DIM + MLSTM_HEADS

kernel_name = "hybrid_conv_mlstm_adaln_block"


def _split_points():
    pts, acc = [], 0
    for s in IN_SIZES[:-1]:
        acc += s
        pts.append(acc)
    return tuple(pts)


def rmsnorm(x, g):
    xf = x.astype(jnp.float32)
    y = xf * lax.rsqrt(jnp.mean(xf * xf, axis=-1, keepdims=True) + EPS)
    return (y * g.astype(jnp.float32)).astype(x.dtype)


def causal_dwconv(u, w):
    K = w.shape[0]
    S = u.shape[1]
    up = jnp.pad(u, ((0, 0), (K - 1, 0), (0, 0)))
    y = up[:, 0:S] * w[0]
    for j in range(1, K):
        y = y + up[:, j:j + S] * w[j]
    return y


def mlstm_chunkwise(q, k, v, i_raw, f_raw):
    B, H, S, DK = q.shape
    DV = v.shape[-1]
    L = CHUNK
    NC = S // L
    f32 = jnp.float32
    q = q.astype(f32).reshape(B, H, NC, L, DK) * (DK ** -0.5)
    k = k.astype(f32).reshape(B, H, NC, L, DK)
    v = v.astype(f32).reshape(B, H, NC, L, DV)
    log_i = i_raw.astype(f32).reshape(B, H, NC, L)
    log_f = jax.nn.log_sigmoid(f_raw.astype(f32)).reshape(B, H, NC, L)
    b = jnp.cumsum(log_f, axis=-1)
    g = b[..., -1]
    a = g[..., None] - b + log_i
    m_loc = jnp.max(a, axis=-1)
    w_loc = jnp.exp(a - m_loc[..., None])
    C_loc = jnp.einsum('bhcsk,bhcsv->bhckv', w_loc[..., None] * k, v)
    n_loc = jnp.einsum('bhcs,bhcsk->bhck', w_loc, k)

    def step(carry, xs):
        C, n, m = carry
        g_c, m_c, C_c, n_c = xs
        m_new = jnp.maximum(g_c + m, m_c)
        s_old = jnp.exp(g_c + m - m_new)
        s_new = jnp.exp(m_c - m_new)
        C_next = s_old[..., None, None] * C + s_new[..., None, None] * C_c
        n_next = s_old[..., None] * n + s_new[..., None] * n_c
        return (C_next, n_next, m_new), (C, n, m)

    init = (jnp.zeros((B, H, DK, DV), f32), jnp.zeros((B, H, DK), f32), jnp.zeros((B, H), f32))
    xs = (jnp.moveaxis(g, 2, 0), jnp.moveaxis(m_loc, 2, 0),
          jnp.moveaxis(C_loc, 2, 0), jnp.moveaxis(n_loc, 2, 0))
    _, (C_prev, n_prev, m_prev) = lax.scan(step, init, xs)
    C_prev = jnp.moveaxis(C_prev, 0, 2)
    n_prev = jnp.moveaxis(n_prev, 0, 2)
    m_prev = jnp.moveaxis(m_prev, 0, 2)

    tri = jnp.tril(jnp.ones((L, L), dtype=bool))
    Dlog = b[..., :, None] - b[..., None, :] + log_i[..., None, :]
    Dlog = jnp.where(tri, Dlog, -jnp.inf)
    inter = b + m_prev[..., None]
    m_t = jnp.maximum(inter, jnp.max(Dlog, axis=-1))
    wts = jnp.exp(Dlog - m_t[..., None]) * jnp.einsum('bhctk,bhcsk->bhcts', q, k)
    s_inter = jnp.exp(inter - m_t)
    num = jnp.einsum('bhcts,bhcsv->bhctv', wts, v) \
        + s_inter[..., None] * jnp.einsum('bhctk,bhckv->bhctv', q, C_prev)
    den = jnp.sum(wts, axis=-1) + s_inter * jnp.einsum('bhctk,bhck->bhct', q, n_prev)
    h = num / jnp.maximum(jnp.abs(den), jnp.exp(-m_t))[..., None]
    return h.reshape(B, H, S, DV)


def mixer(h, w_in, b_in, conv_mix_w, mlstm_head_g, w_proj_conv, w_proj_mlstm, w_out):
    B, S, _ = h.shape
    proj = h @ w_in + b_in
    xin, bg, cg, q, k, v, o, ig, fg, gate_c, gate_m = jnp.split(proj, _split_points(), axis=-1)
    y_conv = bg * causal_dwconv(cg * xin, conv_mix_w)
    p_conv = y_conv @ w_proj_conv
    H = MLSTM_HEADS
    to_heads = lambda t, d: t.reshape(B, S, H, d).transpose(0, 2, 1, 3)
    hm = mlstm_chunkwise(to_heads(q, MLSTM_DQK), to_heads(k, MLSTM_DQK), to_heads(v, MLSTM_DV),
                         ig.transpose(0, 2, 1), fg.transpose(0, 2, 1))
    hm = hm.transpose(0, 2, 1, 3).astype(h.dtype)
    hm = rmsnorm(hm, mlstm_head_g)
    hm = jax.nn.sigmoid(o).reshape(B, S, H, MLSTM_DV) * hm
    p_mlstm = hm.reshape(B, S, MLSTM_DIM) @ w_proj_mlstm
    z = jax.nn.sigmoid(gate_c) * p_conv + jax.nn.sigmoid(gate_m) * p_mlstm
    return z @ w_out


def conv_ffn(h, w_up, conv_ffn_w, w_down):
    up = h @ w_up
    a, gt = jnp.split(up, 2, axis=-1)
    a = causal_dwconv(a, conv_ffn_w)
    return (jax.nn.silu(a) * gt) @ w_down


def setup_inputs(seed: int = 0) -> dict:
    key = jax.random.key(seed)
    ks = jax.random.split(key, 20)
    f32 = jnp.float32
    nrm = lambda k, shape, scale: jax.random.normal(k, shape, f32) * scale
    b_in = nrm(ks[6], (DEPTH, N_IN), 0.01)
    f_bias = jnp.linspace(3.0, 6.0, MLSTM_HEADS, dtype=f32)
    b_in = b_in.at[:, F_GATE_OFFSET:F_GATE_OFFSET + MLSTM_HEADS].add(f_bias)
    return {
        "x": nrm(ks[0], (BATCH, SEQ, D_MODEL), 1.0),
        "c": nrm(ks[1], (BATCH, D_MODEL), 1.0),
        "w_ada": nrm(ks[2], (DEPTH, D_MODEL, N_MOD * D_MODEL), 0.5 * D_MODEL ** -0.5),
        "b_ada": nrm(ks[3], (DEPTH, N_MOD * D_MODEL), 0.01),
        "g_norm_mix": 1.0 + nrm(ks[4], (DEPTH, D_MODEL), 0.05),
        "w_in": nrm(ks[5], (DEPTH, D_MODEL, N_IN), D_MODEL ** -0.5),
        "b_in": b_in,
        "conv_mix_w": nrm(ks[7], (DEPTH, CONV_K, CONV_DIM), CONV_K ** -0.5),
        "mlstm_head_g": 1.0 + nrm(ks[8], (DEPTH, MLSTM_HEADS, MLSTM_DV), 0.05),
        "w_proj_conv": nrm(ks[9], (DEPTH, CONV_DIM, D_MODEL), CONV_DIM ** -0.5),
        "w_proj_mlstm": nrm(ks[10], (DEPTH, MLSTM_DIM, D_MODEL), MLSTM_DIM ** -0.5),
        "w_out": nrm(ks[11], (DEPTH, D_MODEL, D_MODEL), D_MODEL ** -0.5),
        "g_norm_ffn": 1.0 + nrm(ks[12], (DEPTH, D_MODEL), 0.05),
        "w_up": nrm(ks[13], (DEPTH, D_MODEL, 2 * D_FF), D_MODEL ** -0.5),
        "conv_ffn_w": nrm(ks[14], (DEPTH, CONV_K, D_FF), CONV_K ** -0.5),
        "w_down": nrm(ks[15], (DEPTH, D_FF, D_MODEL), D_FF ** -0.5),
        "g_final": 1.0 + nrm(ks[16], (D_MODEL,), 0.05),
    }


def reference(x, c, w_ada, b_ada, g_norm_mix, w_in, b_in, conv_mix_w, mlstm_head_g,
              w_proj_conv, w_proj_mlstm, w_out, g_norm_ffn, w_up, conv_ffn_w, w_down, g_final):
    for l in range(DEPTH):
        mod = jax.nn.silu(c) @ w_ada[l] + b_ada[l]
        sh1, sc1, gt1, sh2, sc2, gt2 = jnp.split(mod[:, None, :], N_MOD, axis=-1)
        h = rmsnorm(x, g_norm_mix[l]) * (1.0 + sc1) + sh1
        x = x + gt1 * mixer(h, w_in[l], b_in[l], conv_mix_w[l], mlstm_head_g[l],
                            w_proj_conv[l], w_proj_mlstm[l], w_out[l])
        h = rmsnorm(x, g_norm_ffn[l]) * (1.0 + sc2) + sh2
        x = x + gt2 * conv_ffn(h, w_up[l], conv_ffn_w[l], w_down[l])
    return rmsnorm(x, g_final)
```

```python
import numpy as np
import concourse.bass as bass
import concourse.mybir as mybir
from concourse.bass_utils import run_bass_kernel_spmd
from contextlib import ExitStack

F32 = mybir.dt.float32
BF16 = mybir.dt.bfloat16
AF = mybir.ActivationFunctionType
ALU = mybir.AluOpType
AX = mybir.AxisListType

D = 1024
NIN = 8200
DFF = 2816
EPS = 1e-6
O_XIN, O_BG, O_CG, O_Q, O_K, O_V, O_O, O_IG, O_FG, O_GC, O_GM = 0, 1024, 2048, 3072, 3584, 4096, 5120, 6144, 6148, 6152, 7176
NSLOT = 6
NFF = 22


class Res:
    __slots__ = ("name", "w", "r")

    def __init__(self, name):
        self.name = name
        self.w = None
        self.r = {}


def RL(name, n):
    return [Res(f"{name}{i}") for i in range(n)]


class Sched:
    def __init__(self, nc, es):
        self.nc = nc
        self.engs = {"pe": nc.tensor, "act": nc.scalar, "dve": nc.vector, "pool": nc.gpsimd, "sp": nc.sync}
        self.sem = {k: es.enter_context(nc.semaphore("s_" + k)) for k in self.engs}
        self.cnt = {k: 0 for k in self.engs}
        self.pending = {k: False for k in self.engs}
        self.seen = {k: {} for k in self.engs}
        self.dq = {}
        for q, n in (("sp", 8), ("pool", 10)):
            self.dq[q] = {"sems": [es.enter_context(nc.semaphore(f"d_{q}{i}")) for i in range(n)],
                          "vals": [0] * n, "next": 0}
        self.semobj = dict(self.sem)
        for q in self.dq:
            for i, s in enumerate(self.dq[q]["sems"]):
                self.semobj[(q, i)] = s
        self.ninstr = 0

    def _deps(self, e, reads, writes):
        deps = {}

        def add(ev, same_ok):
            if ev is None:
                return
            k, v = ev
            if k == e and not same_ok:
                return
            if deps.get(k, 0) < v:
                deps[k] = v

        same_raw = e in ("act", "dve", "pool")
        for r in reads:
            add(r.w, same_raw)
        for w in writes:
            add(w.w, False)
            for k, v in w.r.items():
                add((k, v), False)
        return deps

    def _wait(self, e, deps):
        eng = self.engs[e]
        seen = self.seen[e]
        for k, v in deps.items():
            if seen.get(k, 0) >= v:
                continue
            eng.wait_ge(self.semobj[k], v)
            seen[k] = v
            self.ninstr += 1

    def _mark(self, ev, reads, writes):
        k, v = ev
        for r in reads:
            if r.r.get(k, 0) < v:
                r.r[k] = v
        for w in writes:
            w.w = ev
            w.r = {}

    def op(self, e, fn, reads=(), writes=(), inc=True):
        deps = self._deps(e, reads, writes)
        self._wait(e, deps)
        ins = fn(self.engs[e])
        self.ninstr += 1
        if inc:
            ins.then_inc(self.sem[e], 1)
            self.cnt[e] += 1
            self.pending[e] = False
            ev = (e, self.cnt[e])
        else:
            self.pending[e] = True
            ev = (e, self.cnt[e] + 1)
        self._mark(ev, reads, writes)
        return ins

    def dma(self, q, out, in_, reads=(), writes=()):
        dq = self.dq[q]
        i = dq["next"]
        dq["next"] = (i + 1) % len(dq["sems"])
        deps = self._deps(q, reads, writes)
        if dq["vals"][i] > 0:
            k = (q, i)
            if deps.get(k, 0) < dq["vals"][i]:
                deps[k] = dq["vals"][i]
        self._wait(q, deps)
        ins = self.engs[q].dma_start(out=out, in_=in_)
        ins.then_inc(dq["sems"][i], 16)
        dq["vals"][i] += 16
        self.ninstr += 1
        ev = ((q, i), dq["vals"][i])
        self._mark(ev, reads, writes)
        return ins

    def flush_pe(self):
        if self.pending["pe"]:
            self.nc.tensor.nop().then_inc(self.sem["pe"], 1)
            self.cnt["pe"] += 1
            self.pending["pe"] = False

    def barrier(self, engines=("pe", "act", "dve"), queues=("sp",)):
        self.flush_pe()
        deps = {k: self.cnt[k] for k in self.engs if self.cnt[k] > 0}
        for q in queues:
            for i, v in enumerate(self.dq[q]["vals"]):
                if v > 0:
                    deps[(q, i)] = v
        for e in engines:
            d = {k: v for k, v in deps.items() if k != e or e in ("act", "dve")}
            self._wait(e, d)


def build_nc(debug=False):
    rec = []
    marks = []
    _build(None, rec, marks)
    return _build(rec, None, marks)


def _build(plan, rec, marks):
    nc = bass.Bass("TRN2", target_bir_lowering=False)
    dram = lambda name, shape, kind="ExternalInput": nc.dram_tensor(name, list(shape), F32, kind=kind).ap()
    x_d = dram("xin", [4096, D])
    ccol_d = dram("ccol", [128, 8])
    flag_d = dram("flag", [128, 1])
    wada_d = dram("w_ada", [D, 6 * D])
    bada_d = dram("b_ada", [1, 6 * D])
    gmix_d = dram("g_norm_mix", [1, D])
    win_d = dram("w_in", [D, NIN])
    bcol_d = dram("b_col", [128, 56])
    brow_d = dram("b_row", [1, 512 + 1024])
    bgate_d = dram("b_gate", [1, 8])
    cw_d = dram("cw_col", [128, 24])
    ghead_d = dram("g_head", [1, D])
    wpc_d = dram("w_proj_conv", [D, D])
    wpm_d = dram("w_proj_mlstm", [D, D])
    wout_d = dram("w_out", [D, D])
    gffn_d = dram("g_norm_ffn", [1, D])
    wup_d = dram("w_up", [D, 2 * DFF])
    cwf_d = dram("cwf_col", [128, 3 * NFF])
    wdn_d = dram("w_down", [DFF, D])
    gfin_d = dram("g_final", [1, D])
    y_d = dram("y", [2048, D], kind="ExternalOutput")

    with ExitStack() as es:
        S = Sched(nc, es)
        uid = [0]

        def sb(name, shape, dt=F32, st=es):
            uid[0] += 1
            return st.enter_context(nc.sbuf_tensor(f"{name}_{uid[0]}", list(shape), dt))

        identb = sb("identb", [128, 128], BF16)
        identf = sb("identf", [128, 128])
        onesf = sb("onesf", [128, 128])
        onesb = sb("onesb", [1, 512], BF16)
        maskf = sb("maskf", [128, 128])
        flag_t = sb("flag_t", [128, 1])
        ccol = sb("ccol", [128, 8])
        scol = sb("scol", [128, 8], BF16)
        G1bc = sb("G1bc", [128, D])
        G2bc = sb("G2bc", [128, D])
        gt1bc = sb("gt1bc", [128, D], BF16)
        gt2bc = sb("gt2bc", [128, D], BF16)
        sh1row = sb("sh1col", [128, 8])
        sh2row = sb("sh2col", [128, 8])
        gheadbc = sb("gheadbc", [128, D])
        gfinbc = sb("gfinbc", [128, D])
        bcol = sb("bcol", [128, 56])
        bqs = sb("bqs", [128, 4])
        browbc = sb("browbc", [128, 1536], BF16)
        bgatebc = sb("bgatebc", [128, 8])
        cwcol = sb("cwcol", [128, 24])
        cwfcol = sb("cwfcol", [128, 3 * NFF])
        wgate = sb("wgate", [128, 8, 8], BF16)
        C32 = sb("C32", [128, 4, 257])
        Cb = sb("Cb", [128, 4, 257], BF16)
        mcarry = sb("mcarry", [1, 4])
        uhalo = sb("uhalo", [128, 8, 2])
        ahalo = sb("ahalo", [128, NFF, 2])
        wslot = [sb(f"wslot{i}", [128, 8, 512], BF16) for i in range(NSLOT)]
        xt = [sb(f"xt{i}", [128, D]) for i in range(2)]
        xnb = [sb(f"xnb{i}", [128, D], BF16) for i in range(2)]
        sqj = sb("sqj", [128, D], BF16)
        stat = sb("stat", [128, 64])
        psum = es.enter_context(nc.psum_tensor("psum", [128, 8, 512], F32))

        r_wslot = RL("wslot", NSLOT)
        r_xt = RL("xt", 2)
        r_xnb = RL("xnb", 2)
        r_bank = RL("bank", 8)
        r_const = Res("const")
        r_par = Res("par")
        r_par2 = Res("par2")
        r_mod = Res("modrow")
        r_sqj = Res("sqj")
        r_C32 = RL("C32_", 4)
        r_Cb = RL("Cb_", 4)
        r_mc = Res("mcarry")
        r_uh = Res("uhalo")
        r_ah = Res("ahalo")
        r_y = Res("ydram")
        r_in = Res("indram")
        r_wgate = Res("wgate")

        state = {"slot": 0, "bank": 0, "xt": 0, "xnb": 0, "stat": 0, "resv": set()}

        def next_bank():
            b = state["bank"]
            while b in state["resv"]:
                b = (b + 1) % 8
            state["bank"] = (b + 1) % 8
            return b

        def stat_col(n=1):
            c = state["stat"]
            if c + n > 64:
                c = 0
            state["stat"] = c + n
            return stat[:, c:c + n], r_stat[c]

        r_stat = RL("stat", 64)

        dr = {"w_ada": wada_d, "w_in": win_d, "w_proj_conv": wpc_d, "w_proj_mlstm": wpm_d, "w_out": wout_d, "w_up": wup_d, "w_down": wdn_d}
        W = {"issue": 0, "use": 0, "free": list(range(NSLOT)), "slot_of": {}, "fence": 0, "stage": 0}

        def slot_view(i, kc, ncol):
            if ncol <= 512:
                return wslot[i][:, 0:kc, 0:ncol]
            return wslot[i][:].rearrange("p k n -> p (k n)")[:, 0:kc * ncol].rearrange("p (k n) -> p k n", n=ncol)

        def issue_w(i, spec):
            name, r0, c0, kc, ncol = spec
            extra = state.pop("w_gate", [])
            S.dma("pool", slot_view(i, kc, ncol), dr[name][r0:r0 + kc * 128, c0:c0 + ncol].rearrange("(k p) n -> p k n", p=128),
                  reads=[r_in] + extra, writes=[r_wslot[i]])

        if plan is not None:
            W["fence"] = marks[0]

        def pump_w():
            while W["free"] and W["issue"] < min(len(plan), W["fence"]) and W["issue"] - W["use"] < 3:
                i = W["free"].pop(0)
                issue_w(i, plan[W["issue"]])
                W["slot_of"][W["issue"]] = i
                W["issue"] += 1

        def load_w(name, c0, kc=8, ncol=512, r0=0):
            spec = (name, r0, c0, kc, ncol)
            if plan is None:
                rec.append(spec)
                i = state["slot"]
                state["slot"] = (i + 1) % NSLOT
                issue_w(i, spec)
                return i
            pump_w()
            k = W["use"]
            assert plan[k] == spec, (k, plan[k], spec)
            assert k in W["slot_of"], f"weight tile {k} {spec} not issued: missing free_w()?"
            W["use"] += 1
            return W["slot_of"][k]

        def free_w(*slots):
            if plan is None:
                return
            for i in slots:
                assert i not in W["free"]
                W["free"].append(i)
            pump_w()

        S.op("pool", lambda e: e.memset(identf[:], 1.0), writes=[r_const])
        S.op("pool", lambda e: e.affine_select(out=identf[:], in_=identf[:], pattern=[[-1, 128]], compare_op=ALU.is_equal,
                                               fill=0.0, base=0, channel_multiplier=1), reads=[r_const], writes=[r_const])
        S.op("pool", lambda e: e.memset(maskf[:], 1.0), writes=[r_const])
        S.op("pool", lambda e: e.affine_select(out=maskf[:], in_=maskf[:], pattern=[[1, 128]], compare_op=ALU.is_ge,
                                               fill=0.0, base=0, channel_multiplier=-1), reads=[r_const], writes=[r_const])
        S.op("dve", lambda e: e.memset(onesf[:], 1.0), writes=[r_const])
        S.op("dve", lambda e: e.memset(onesb[:], 1.0), writes=[r_const])
        S.op("dve", lambda e: e.tensor_copy(out=identb[:], in_=identf[:]), reads=[r_const], writes=[r_const])
        S.op("dve", lambda e: e.memset(C32[:], 0.0), writes=r_C32)
        S.op("dve", lambda e: e.memset(mcarry[:], 0.0), writes=[r_mc])
        S.op("dve", lambda e: e.memset(uhalo[:], 0.0), writes=[r_uh])
        S.op("dve", lambda e: e.memset(ahalo[:], 0.0), writes=[r_ah])

        def pload(dst, src):
            S.dma("sp", dst, src, reads=[r_in], writes=[r_par])

        r_crit = Res("crit")
        ada = ExitStack()
        modrow = sb("modrow", [1, 6 * D], F32, ada)
        growA = sb("growA", [1, D], F32, ada)
        growB = sb("growB", [1, D], F32, ada)
        browst = sb("browst", [128, 1536], F32, ada)
        S.dma("sp", ccol[:], ccol_d, reads=[r_in], writes=[r_crit])
        S.dma("sp", modrow[:], bada_d, reads=[r_in], writes=[r_mod])
        S.dma("sp", growA[:], gmix_d, reads=[r_in], writes=[r_crit])
        pload(flag_t[:], flag_d)
        pload(growB[:], gffn_d)
        pload(gheadbc[:], ghead_d.partition_broadcast(128))
        pload(gfinbc[:], gfin_d.partition_broadcast(128))
        pload(bcol[:], bcol_d)
        pload(browst[:], brow_d.partition_broadcast(128))
        S.op("dve", lambda e: e.tensor_copy(out=browbc[:], in_=browst[:]), reads=[r_par], writes=[r_par])
        pload(bgatebc[:], bgate_d.partition_broadcast(128))
        pload(cwcol[:], cw_d)
        pload(cwfcol[:], cwf_d)
        state["w_gate"] = [r_crit, r_mod]
        S.dma("pool", wgate[:], win_d[:, O_IG:O_IG + 8].rearrange("(k p) n -> p k n", p=128), reads=[r_in], writes=[r_wgate])

        S.op("act", lambda e: e.activation(out=scol[:], in_=ccol[:], func=AF.Silu), reads=[r_crit], writes=[r_crit])
        QS = 128.0 ** -0.5
        S.op("dve", lambda e: e.tensor_scalar(out=bqs[:], in0=bcol[:, 24:28], scalar1=QS, scalar2=None, op0=ALU.mult),
             reads=[r_par], writes=[r_par])

        def ada_blocks(nlist):
            for n in nlist:
                si = load_w("w_ada", n * 512)
                b = next_bank()
                for k in range(8):
                    S.op("pe", lambda e, k=k: e.matmul(psum[0:1, b, :], lhsT=scol[:, k:k + 1], rhs=wslot[si][:, k, :],
                                                      start=(k == 0), stop=(k == 7)),
                         reads=[r_crit, r_wslot[si]], writes=[r_bank[b]], inc=(k == 7))
                S.op("dve", lambda e: e.tensor_tensor(out=modrow[0:1, n * 512:(n + 1) * 512], in0=psum[0:1, b, :],
                                                      in1=modrow[0:1, n * 512:(n + 1) * 512], op=ALU.add),
                     reads=[r_bank[b], r_mod], writes=[r_mod])
                free_w(si)

        def bcast_row(row_ap, dst, r_dst):
            for h in range(2):
                b = next_bank()
                S.op("pe", lambda e: e.matmul(psum[:, b, :], lhsT=onesf[0:1, :], rhs=row_ap[0:1, h * 512:(h + 1) * 512], start=True, stop=True),
                     reads=[r_mod, r_const], writes=[r_bank[b]])
                S.op("act", lambda e: e.activation(out=dst[:, h * 512:(h + 1) * 512], in_=psum[:, b, :], func=AF.Copy),
                     reads=[r_bank[b]], writes=[r_dst])

        def row_to_col(off, dst, r_dst):
            b = next_bank()
            for k in range(8):
                S.op("pe", lambda e, k=k: e.matmul(psum[:, b, k:k + 1], lhsT=modrow[0:1, off + k * 128:off + (k + 1) * 128], rhs=onesf[0:1, 0:1],
                                                  start=True, stop=True),
                     reads=[r_mod, r_const], writes=[r_bank[b]], inc=(k == 7))
            S.op("act", lambda e: e.activation(out=dst[:], in_=psum[:, b, 0:8], func=AF.Copy), reads=[r_bank[b]], writes=[r_dst])

        def ada_part1():
            ada_blocks([2, 3, 0, 1])
            S.op("dve", lambda e: e.scalar_tensor_tensor(out=growA[:], in0=modrow[0:1, D:2 * D], scalar=1.0, in1=growA[:],
                                                         op0=ALU.add, op1=ALU.mult), reads=[r_mod, r_crit], writes=[r_mod])
            bcast_row(growA, G1bc, r_par)
            row_to_col(0, sh1row, r_par)

        def ada_part2():
            S.op("dve", lambda e: e.scalar_tensor_tensor(out=growB[:], in0=modrow[0:1, 4 * D:5 * D], scalar=1.0, in1=growB[:],
                                                         op0=ALU.add, op1=ALU.mult), reads=[r_mod, r_par], writes=[r_mod])
            bcast_row(growB, G2bc, r_par2)
            bcast_row(modrow[0:1, 2 * D:3 * D], gt1bc, r_par2)
            bcast_row(modrow[0:1, 5 * D:6 * D], gt2bc, r_par2)
            row_to_col(3 * D, sh2row, r_par2)

        ada_part1()

        def rms_rstd(src_ap, src_res, n_feat):
            ssq, r_ssq = stat_col()
            S.op("act", lambda e: e.activation(out=sqj[:, 0:n_feat], in_=src_ap, func=AF.Square, accum_out=ssq),
                 reads=src_res, writes=[r_sqj, r_ssq])
            sd, r_sd = stat_col()
            S.op("act", lambda e: e.activation(out=sd, in_=ssq, func=AF.Sqrt, scale=1.0 / n_feat, bias=EPS),
                 reads=[r_ssq], writes=[r_sd])
            rs, r_rs = stat_col()
            S.op("dve", lambda e: e.reciprocal(out=rs, in_=sd), reads=[r_sd], writes=[r_rs])
            return rs, r_rs

        def norm_front(src_ap, src_res, Gbc):
            rs, r_rs = rms_rstd(src_ap, src_res, D)
            i = state["xnb"]
            state["xnb"] = 1 - i
            S.op("dve", lambda e: e.scalar_tensor_tensor(out=xnb[i][:], in0=src_ap, scalar=rs, in1=Gbc[:], op0=ALU.mult, op1=ALU.mult),
                 reads=list(src_res) + [r_rs, r_par, r_par2], writes=[r_xnb[i]])
            return i

        def norm_back(i, shrow, dstT, col0, dst_res):
            for half in range(2):
                b = next_bank()
                for kk in range(4):
                    k = half * 4 + kk
                    S.op("pe", lambda e, k=k, kk=kk: e.matmul(psum[:, b, kk * 128:(kk + 1) * 128], lhsT=xnb[i][:, k * 128:(k + 1) * 128],
                                                              rhs=identb[:], start=True, stop=True),
                         reads=[r_xnb[i], r_const], writes=[r_bank[b]], inc=(kk == 3))
                for kk in range(4):
                    k = half * 4 + kk
                    if half == 0:
                        S.op("act", lambda e, k=k, kk=kk: e.activation(out=dstT[:, k, col0:col0 + 128], in_=psum[:, b, kk * 128:(kk + 1) * 128],
                                                                       func=AF.Identity, bias=shrow[:, k:k + 1]),
                             reads=[r_bank[b], r_par, r_par2], writes=[dst_res])
                    else:
                        S.op("dve", lambda e, k=k, kk=kk: e.tensor_scalar(out=dstT[:, k, col0:col0 + 128], in0=psum[:, b, kk * 128:(kk + 1) * 128],
                                                                          scalar1=shrow[:, k:k + 1], scalar2=None, op0=ALU.add),
                             reads=[r_bank[b], r_par, r_par2], writes=[dst_res])

        def load_x_tile(row0):
            i = state["xt"]
            state["xt"] = (i + 1) % 2
            S.dma("sp", xt[i][:], x_d[row0:row0 + 128, :], reads=[r_in], writes=[r_xt[i]])
            return i

        def fm_proj(si_list, wcol0, hT, r_h, tbs, evac):
            si = si_list
            for (t0, n) in tbs:
                b = next_bank()
                for k in range(8):
                    S.op("pe", lambda e, k=k: e.matmul(psum[:, b, 0:n], lhsT=wslot[si][:, k, wcol0:wcol0 + 128], rhs=hT[:, k, t0:t0 + n],
                                                      start=(k == 0), stop=(k == 7)),
                         reads=[r_wslot[si]] + r_h, writes=[r_bank[b]], inc=(k == 7))
                evac(b, t0, n)

        def interleave(gen, units):
            for u_ in units:
                next(gen, None)
                u_()
            for _ in gen:
                pass

        def gate_math(scope, graw, r_graw, nch, first_stage_flag, want_dmin, GB, out):
            ng = nch * 4
            T = lambda name, shape: sb(name, shape, F32, scope)
            ig = graw[:, 0:nch, 0:4]
            fg = graw[:, 0:nch, 4:8]
            t_abs = T("g_abs", [128, nch, 4])
            t_e = T("g_e", [128, nch, 4])
            logf = T("g_logf", [128, nch, 4])
            alpha = T("g_alpha", [128, ng])
            b_sb = T("g_b", [128, ng])
            acol = T("g_acol", [128, 1])
            arow = T("g_arow", [1, nch, 4])
            grow = T("g_grow", [1, nch, 4])
            mb = T("g_mb", [1, nch + 1, 4])
            mp = T("g_mp", [1, nch, 4])
            sold_row = T("g_soldrow", [1, nch, 4])
            tmp = T("g_tmp", [128, ng])
            wG = T("g_wG", [128, ng])
            sold = T("g_sold", [128, ng])
            dmin = T("g_dmin", [128, ng])
            rg = Res("gm")
            S.op("dve", lambda e: e.scalar_tensor_tensor(out=t_abs[:], in0=fg, scalar=-1.0, in1=fg, op0=ALU.mult, op1=ALU.max),
                 reads=[r_graw], writes=[rg])
            S.op("act", lambda e: e.activation(out=t_e[:], in_=t_abs[:], func=AF.Exp, scale=-1.0), reads=[rg], writes=[rg])
            S.op("act", lambda e: e.activation(out=t_e[:], in_=t_e[:], func=AF.Ln, bias=1.0), reads=[rg], writes=[rg])
            S.op("dve", lambda e: e.scalar_tensor_tensor(out=logf[:], in0=fg, scalar=0.0, in1=t_e[:], op0=ALU.min, op1=ALU.subtract),
                 reads=[rg, r_graw], writes=[rg])
            logf2 = logf[:].rearrange("p c h -> p (c h)")
            b0, b1, b2, b3 = GB[0], GB[1], GB[0], GB[1]
            yield
            S.op("pe", lambda e: e.matmul(psum[:, b0, 0:ng], lhsT=maskf[:], rhs=logf2, start=True, stop=True),
                 reads=[rg, r_const], writes=[r_bank[b0]])
            S.op("pe", lambda e: e.matmul(psum[0:1, b1, 0:ng], lhsT=onesf[:, 0:1], rhs=logf2, start=True, stop=True),
                 reads=[rg, r_const], writes=[r_bank[b1]])
            S.op("dve", lambda e: e.tensor_tensor(out=alpha[:].rearrange("p (c h) -> p c h", h=4), in0=ig,
                                                  in1=psum[:, b0, 0:ng].rearrange("p (c h) -> p c h", h=4), op=ALU.subtract),
                 reads=[r_graw, r_bank[b0]], writes=[rg])
            S.op("dve", lambda e: e.tensor_copy(out=b_sb[:], in_=psum[:, b0, 0:ng]), reads=[r_bank[b0]], writes=[rg])
            S.op("act", lambda e: e.activation(out=grow[:].rearrange("p c h -> p (c h)"), in_=psum[0:1, b1, 0:ng], func=AF.Copy),
                 reads=[r_bank[b1]], writes=[rg])
            yield
            S.op("pe", lambda e: e.matmul(psum[0:ng, b2, 0:128], lhsT=alpha[:], rhs=identf[:], start=True, stop=True),
                 reads=[rg, r_const], writes=[r_bank[b2]])
            S.op("dve", lambda e: e.tensor_reduce(out=acol[0:ng, :], in_=psum[0:ng, b2, 0:128], axis=AX.X, op=ALU.max),
                 reads=[r_bank[b2]], writes=[rg])
            yield
            S.op("pe", lambda e: e.matmul(psum[0:1, b3, 0:ng], lhsT=acol[0:ng, 0:1], rhs=identf[0:ng, 0:ng], start=True, stop=True),
                 reads=[rg, r_const], writes=[r_bank[b3]])
            S.op("act", lambda e: e.activation(out=arow[:].rearrange("p c h -> p (c h)"), in_=psum[0:1, b3, 0:ng], func=AF.Copy),
                 reads=[r_bank[b3]], writes=[rg])
            yield
            S.op("dve", lambda e: e.tensor_copy(out=mb[0:1, 0, :], in_=mcarry[:]), reads=[r_mc], writes=[rg])
            for c in range(nch):
                S.op("dve", lambda e, c=c: e.tensor_tensor(out=mp[0:1, c, :], in0=mb[0:1, c, :], in1=arow[0:1, c, :], op=ALU.max),
                     reads=[rg], writes=[rg])
                S.op("dve", lambda e, c=c: e.tensor_tensor(out=mb[0:1, c + 1, :], in0=mp[0:1, c, :], in1=grow[0:1, c, :], op=ALU.add),
                     reads=[rg], writes=[rg])
                if first_stage_flag and c == 0:
                    S.op("dve", lambda e, c=c: e.tensor_scalar(out=mb[0:1, c + 1, :], in0=mb[0:1, c + 1, :], scalar1=flag_t[0:1, 0:1],
                                                               scalar2=None, op0=ALU.mult), reads=[rg, r_par], writes=[rg])
            S.op("dve", lambda e: e.tensor_copy(out=mcarry[:], in_=mb[0:1, nch, :]), reads=[rg], writes=[r_mc])
            S.op("dve", lambda e: e.tensor_tensor(out=sold_row[:], in0=mb[0:1, 0:nch, :], in1=mp[:], op=ALU.subtract),
                 reads=[rg], writes=[rg])
            S.op("act", lambda e: e.activation(out=sold_row[:], in_=sold_row[:], func=AF.Exp), reads=[rg], writes=[rg])
            if first_stage_flag:
                S.op("dve", lambda e: e.tensor_scalar(out=sold_row[0:1, 1, :], in0=sold_row[0:1, 1, :], scalar1=flag_t[0:1, 0:1],
                                                      scalar2=None, op0=ALU.mult), reads=[rg, r_par], writes=[rg])
            b4, b5 = GB[0], GB[1]
            yield
            yield
            S.op("pe", lambda e: e.matmul(psum[:, b4, 0:ng], lhsT=onesf[0:1, :], rhs=mp[:].rearrange("p c h -> p (c h)"), start=True, stop=True),
                 reads=[rg, r_const], writes=[r_bank[b4]])
            S.op("pe", lambda e: e.matmul(psum[:, b5, 0:ng], lhsT=onesf[0:1, :], rhs=sold_row[:].rearrange("p c h -> p (c h)"),
                                          start=True, stop=True), reads=[rg, r_const], writes=[r_bank[b5]])
            S.op("act", lambda e: e.activation(out=sold[:], in_=psum[:, b5, 0:ng], func=AF.Copy), reads=[r_bank[b5]], writes=[rg])
            S.op("dve", lambda e: e.tensor_tensor(out=tmp[:], in0=alpha[:], in1=psum[:, b4, 0:ng], op=ALU.subtract),
                 reads=[rg, r_bank[b4]], writes=[rg])
            S.op("act", lambda e: e.activation(out=wG[:], in_=tmp[:], func=AF.Exp), reads=[rg], writes=[rg])
            if want_dmin:
                S.op("dve", lambda e: e.tensor_tensor(out=tmp[:], in0=b_sb[:], in1=psum[:, b4, 0:ng], op=ALU.add),
                     reads=[rg, r_bank[b4]], writes=[rg])
                S.op("act", lambda e: e.activation(out=dmin[:], in_=tmp[:], func=AF.Exp, scale=-1.0), reads=[rg], writes=[rg])
            out.update(wG=wG, sold=sold, dmin=dmin, rg=rg)

        def kv_gate_proj(c, hTt, r_hT, col0, si_k, si_v, ktok, r_ktok, vaug, r_vaug, gbank, ridx, kT_src=None):
            b = next_bank()
            if kT_src is None:
                for k in range(8):
                    S.op("pe", lambda e, k=k: e.matmul(psum[:, b, :], lhsT=hTt[:, k, col0:col0 + 128], rhs=wslot[si_k][:, k, :],
                                                      start=(k == 0), stop=(k == 7)),
                         reads=[r_hT, r_wslot[si_k]], writes=[r_bank[b]], inc=(k == 7))
                S.op("dve", lambda e: e.tensor_tensor(out=ktok[:, ridx, :], in0=psum[:, b, :], in1=browbc[:, 0:512], op=ALU.add),
                     reads=[r_bank[b], r_par], writes=[r_ktok[ridx]])
            else:
                kT_, r_kT_c = kT_src
                for h in range(4):
                    S.op("pe", lambda e, h=h: e.matmul(psum[:, b, h * 128:(h + 1) * 128], lhsT=kT_[:, h, col0:col0 + 128], rhs=identb[:],
                                                      start=True, stop=True),
                         reads=[r_kT_c, r_const], writes=[r_bank[b]], inc=(h == 3))
                S.op("dve", lambda e: e.tensor_copy(out=ktok[:, ridx, :], in_=psum[:, b, :]), reads=[r_bank[b]], writes=[r_ktok[ridx]])
            for half in range(2):
                b = next_bank()
                for k in range(8):
                    S.op("pe", lambda e, k=k: e.matmul(psum[:, b, :], lhsT=hTt[:, k, col0:col0 + 128], rhs=wslot[si_v[half]][:, k, :],
                                                      start=(k == 0), stop=(k == 7)),
                         reads=[r_hT, r_wslot[si_v[half]]], writes=[r_bank[b]], inc=(k == 7))
                S.op("dve", lambda e: e.tensor_tensor(out=vaug[:, ridx, 2 * half:2 * half + 2, 0:256],
                                                      in0=psum[:, b, :].rearrange("p (h v) -> p h v", h=2),
                                                      in1=browbc[:, 512 + half * 512:512 + (half + 1) * 512].rearrange("p (h v) -> p h v", h=2),
                                                      op=ALU.add),
                     reads=[r_bank[b], r_par], writes=[r_vaug[ridx]])
            for k in range(8):
                S.op("pe", lambda e, k=k: e.matmul(psum[:, gbank, ridx * 8:ridx * 8 + 8], lhsT=hTt[:, k, col0:col0 + 128], rhs=wgate[:, k, :],
                                                  start=(k == 0), stop=(k == 7)),
                     reads=[r_hT, r_wgate], writes=[r_bank[gbank]], inc=(k == 7))

        def state_update(ridx, gi0, ktok, r_ktok, vw, r_vw, sold, rg, pbank):
            for h in range(4):
                pb = pbank[h % 2]
                S.op("pe", lambda e: e.matmul(psum[:, pb, 0:257], lhsT=ktok[:, ridx, h * 128:(h + 1) * 128], rhs=vw[:, h, :],
                                              start=True, stop=True),
                     reads=[r_ktok[ridx], r_vw], writes=[r_bank[pb]])
                S.op("dve", lambda e: e.scalar_tensor_tensor(out=C32[:, h, :], in0=C32[:, h, :], scalar=sold[:, gi0 + h:gi0 + h + 1],
                                                             in1=psum[:, pb, 0:257], op0=ALU.mult, op1=ALU.add),
                     reads=[r_C32[h], rg, r_bank[pb]], writes=[r_C32[h]])

        NPF = 15
        with ExitStack() as ps_:
            ktok_p = sb("ktok_p", [128, NPF, 512], BF16, ps_)
            vaug_p = sb("vaug_p", [128, NPF, 4, 257], BF16, ps_)
            graw_p = sb("graw_p", [128, NPF, 8], F32, ps_)
            hTp = [sb(f"hTp{i}", [128, 8, 128], BF16, ps_) for i in range(2)]
            r_ktok_p = RL("ktok_p", NPF)
            r_vaug_p = RL("vaug_p", NPF)
            r_hTp = RL("hTp", 2)
            r_graw_p = Res("graw_p")
            S.op("dve", lambda e: e.memset(vaug_p[:, :, :, 256:257], 1.0), writes=r_vaug_p)
            si_k = load_w("w_in", O_K)
            si_v = [load_w("w_in", O_V + i * 512) for i in range(2)]
            gbank = 7
            state["resv"] = {gbank, 5, 6}
            def pfront(c):
                xi = load_x_tile(c * 128)
                return norm_front(xt[xi][:], [r_xt[xi]], G1bc)

            nf = {0: pfront(0)}
            if NPF > 1:
                nf[1] = pfront(1)
            norm_back(nf[0], sh1row, hTp[0], 0, r_hTp[0])
            for c in range(NPF):
                if c + 2 < NPF:
                    nf[c + 2] = pfront(c + 2)
                if c + 1 < NPF:
                    norm_back(nf[c + 1], sh1row, hTp[(c + 1) % 2], 0, r_hTp[(c + 1) % 2])
                j = c % 2
                kv_gate_proj(c, hTp[j], r_hTp[j], 0, si_k, si_v, ktok_p, r_ktok_p, vaug_p, r_vaug_p, gbank, c)
            free_w(si_k, *si_v)
            S.op("dve", lambda e: e.tensor_tensor(out=graw_p[:], in0=psum[:, gbank, 0:NPF * 8].rearrange("p (c g) -> p c g", g=8),
                                                  in1=bgatebc[:].rearrange("p (o g) -> p o g", o=1).to_broadcast([128, NPF, 8]), op=ALU.add),
                 reads=[r_bank[gbank], r_par], writes=[r_graw_p])
            state["resv"] = {4, 5, 6, 7}
            gm = {}
            interleave(gate_math(ps_, graw_p, r_graw_p, NPF, False, False, (4, 7), gm),
                       [(lambda n=n: ada_blocks([n])) for n in range(4, 12)])
            ada_part2()
            wG, sold, dmin, rg = gm["wG"], gm["sold"], gm["dmin"], gm["rg"]
            def prescale_p(c):
                for h in range(4):
                    S.op("act", lambda e, h=h: e.activation(out=vaug_p[:, c, h, :], in_=vaug_p[:, c, h, :], func=AF.Identity,
                                                            scale=wG[:, c * 4 + h:c * 4 + h + 1]),
                         reads=[r_vaug_p[c], rg], writes=[r_vaug_p[c]])

            prescale_p(0)
            for c in range(NPF):
                if c + 1 < NPF:
                    prescale_p(c + 1)
                state_update(c, c * 4, ktok_p, r_ktok_p, vaug_p[:, c], r_vaug_p[c], sold, rg, (5, 6))
            state["resv"] = set()
            S.barrier()
        ada.close()

        prefront = {}

        def run_stage(ci0, nch, first, next_ci0=None):
            if plan is None:
                marks.append(len(rec))
            else:
                W["stage"] += 1
                W["fence"] = marks[W["stage"]] if W["stage"] < len(marks) else len(plan)
            if not first:
                state["w_gate"] = [r_y]
            NT = nch * 128
            tbs = []
            t = 0
            if first:
                tbs.append((0, 128))
                t = 128
            while t < NT:
                n = min(512, NT - t)
                tbs.append((t, n))
                t += n
            x_row0 = ci0 * 128
            with ExitStack() as st:
                x1 = sb("x1", [128, nch, D], F32, st)
                r_x1 = RL("x1_", nch)
                hT = sb("hT", [128, 8, NT], BF16, st)
                sgoT = sb("sgoT", [128, 8, NT], BF16, st)
                hmT = sgoT
                r_hT = RL("hT", nch)
                r_hmT = RL("hmT", nch)

                def blk(rl, t0, n):
                    return rl[t0 // 128:(t0 + n) // 128]

                def sfront(c):
                    xi = load_x_tile(x_row0 + c * 128)
                    return norm_front(xt[xi][:], [r_xt[xi]], G1bc)

                nf = prefront.pop("nf", None) or {0: sfront(0)}
                for c in range(nch):
                    if c + 1 < nch and (c + 1) not in nf:
                        nf[c + 1] = sfront(c + 1)
                    norm_back(nf[c], sh1row, hT, c * 128, r_hT[c])

                with ExitStack() as ms:
                    qT = sb("qT", [128, 4, NT], BF16, ms)
                    kT = sb("kT", [128, 4, NT], BF16, ms)
                    ktok = sb("ktok", [128, nch, 512], BF16, ms)
                    vaug = sb("vaug", [128, nch, 4, 257], BF16, ms)
                    graw = sb("graw", [128, nch, 8], F32, ms)
                    Sm = [sb(f"Sm{i}", [128, 4, 128], BF16, ms) for i in range(3)]
                    hm = [sb(f"hm{i}", [128, D], BF16, ms) for i in range(3)]
                    r_qT = RL("qT", nch)
                    r_kT = RL("kT", nch)
                    r_ktok = RL("ktok", nch)
                    r_vaug = RL("vaug", nch)
                    r_sgoT = r_hmT
                    r_graw = Res("graw")
                    r_Sm = RL("Sm", 3)
                    r_hm = RL("hm", 3)
                    S.op("dve", lambda e: e.memset(vaug[:, :, :, 256:257], 1.0), writes=r_vaug)
                    si_q = load_w("w_in", O_Q)
                    si_k = load_w("w_in", O_K)
                    for h in range(4):
                        def ev_q(b, t0, n, h=h):
                            S.op("act", lambda e: e.activation(out=qT[:, h, t0:t0 + n], in_=psum[:, b, 0:n], func=AF.Identity,
                                                               bias=bqs[:, h:h + 1], scale=QS),
                                 reads=[r_bank[b], r_par], writes=blk(r_qT, t0, n))
                        fm_proj(si_q, h * 128, hT, r_hT, tbs, ev_q)
                    free_w(si_q)
                    for h in range(4):
                        def ev_k(b, t0, n, h=h):
                            S.op("act", lambda e: e.activation(out=kT[:, h, t0:t0 + n], in_=psum[:, b, 0:n], func=AF.Identity,
                                                               bias=bcol[:, 28 + h:29 + h]),
                                 reads=[r_bank[b], r_par], writes=blk(r_kT, t0, n))
                        fm_proj(si_k, h * 128, hT, r_hT, tbs, ev_k)
                    free_w(si_k)
                    si_v = [load_w("w_in", O_V + i * 512) for i in range(2)]
                    gbank = 7
                    state["resv"] = {gbank}
                    for c in range(nch):
                        kv_gate_proj(c, hT, r_hT[c], c * 128, si_k, si_v, ktok, r_ktok, vaug, r_vaug, gbank, c, kT_src=(kT, r_kT[c]))
                    free_w(*si_v)
                    S.op("dve", lambda e: e.tensor_tensor(out=graw[:], in0=psum[:, gbank, 0:nch * 8].rearrange("p (c g) -> p c g", g=8),
                                                          in1=bgatebc[:].rearrange("p (o g) -> p o g", o=1).to_broadcast([128, nch, 8]), op=ALU.add),
                         reads=[r_bank[gbank], r_par], writes=[r_graw])
                    state["resv"] = {6, 7}
                    units = []
                    for i in range(2):
                        for jj in range(4):
                            def unit(i=i, jj=jj):
                                if jj == 0:
                                    state["si_o"] = load_w("w_in", O_O + i * 512)
                                j = i * 4 + jj
                                def ev_o(b, t0, n, j=j):
                                    S.op("act", lambda e: e.activation(out=sgoT[:, j, t0:t0 + n], in_=psum[:, b, 0:n], func=AF.Sigmoid,
                                                                       bias=bcol[:, 48 + j:49 + j]),
                                         reads=[r_bank[b], r_par], writes=blk(r_sgoT, t0, n))
                                fm_proj(state["si_o"], jj * 128, hT, r_hT, tbs, ev_o)
                                if jj == 3:
                                    free_w(state["si_o"])
                            units.append(unit)
                    gm = {}
                    interleave(gate_math(ms, graw, r_graw, nch, first, True, (6, 7), gm), units)
                    wG, sold, dmin, rg = gm["wG"], gm["sold"], gm["dmin"], gm["rg"]
                    state["resv"] = set()
                    S.barrier()
                    ng_ = nch * 4
                    def prescale(c):
                        for h in range(4):
                            S.op("act", lambda e, h=h: e.activation(out=vaug[:, c, h, :], in_=vaug[:, c, h, :], func=AF.Identity,
                                                                    scale=wG[:, c * 4 + h:c * 4 + h + 1]),
                                 reads=[r_vaug[c], rg], writes=[r_vaug[c]])

                    prescale(0)
                    r_X = RL("X", 4)
                    r_STb = Res("STb")
                    Xs = [sb(f"Xs{i}", [128, 4, 257], F32, ms) for i in range(3)]
                    den4 = [sb(f"den4{i}", [128, 4], F32, ms) for i in range(3)]
                    ssq4 = [sb(f"ssq4{i}", [128, 4], F32, ms) for i in range(3)]
                    fac4 = [sb(f"fac4{i}", [128, 4], F32, ms) for i in range(3)]
                    r_Xs = RL("Xs", 3)
                    r_d4 = RL("den4", 3)
                    r_s4 = RL("ssq4", 3)
                    r_f4 = RL("fac4", 3)
                    def st_A(c):
                        j = c % 3
                        cs = slice(c * 128, (c + 1) * 128)
                        for h in range(4):
                            S.op("pe", lambda e, h=h: e.matmul(psum[:, 0, h * 128:(h + 1) * 128], lhsT=kT[:, h, cs], rhs=qT[:, h, cs],
                                                               start=True, stop=True),
                                 reads=[r_kT[c], r_qT[c]], writes=[r_STb], inc=(h == 3))
                        S.op("dve", lambda e: e.tensor_tensor(out=Sm[j][:], in0=psum[:, 0, :].rearrange("p (h t) -> p h t", h=4),
                                                              in1=maskf[:].rearrange("p (o t) -> p o t", o=1).to_broadcast([128, 4, 128]), op=ALU.mult),
                             reads=[r_STb, r_const], writes=[r_Sm[j]])
                        for h in range(4):
                            S.op("act", lambda e, h=h: e.activation(out=Cb[:, h, :], in_=C32[:, h, :], func=AF.Identity,
                                                                    scale=sold[:, c * 4 + h:c * 4 + h + 1]),
                                 reads=[r_C32[h], rg], writes=[r_Cb[h]])

                    def st_B(c):
                        j = c % 3
                        cs = slice(c * 128, (c + 1) * 128)
                        for h in range(4):
                            S.op("pe", lambda e, h=h: e.matmul(psum[:, 1 + h, 0:257], lhsT=Sm[j][:, h, :], rhs=vaug[:, c, h, :], start=True, stop=False),
                                 reads=[r_Sm[j], r_vaug[c]], writes=[r_X[h]], inc=False)
                            S.op("pe", lambda e, h=h: e.matmul(psum[:, 1 + h, 0:257], lhsT=qT[:, h, cs], rhs=Cb[:, h, :], start=False, stop=True),
                                 reads=[r_qT[c], r_Cb[h]], writes=[r_X[h]], inc=(h == 3))
                        S.op("act", lambda e: e.activation(out=Xs[j][:], in_=psum[:, 1:5, 0:257], func=AF.Copy), reads=r_X, writes=[r_Xs[j]])
                        state_update(c, c * 4, ktok, r_ktok, vaug[:, c], r_vaug[c], sold, rg, (5, 6))

                    def st_C1(c):
                        j = c % 3
                        S.op("dve", lambda e: e.scalar_tensor_tensor(out=den4[j][:].rearrange("p (h o) -> p h o", o=1), in0=Xs[j][:, :, 256:257],
                                                                     scalar=-1.0, in1=Xs[j][:, :, 256:257], op0=ALU.mult, op1=ALU.max),
                             reads=[r_Xs[j]], writes=[r_d4[j]])
                        S.op("dve", lambda e: e.tensor_tensor(out=den4[j][:], in0=den4[j][:], in1=dmin[:, c * 4:c * 4 + 4], op=ALU.max),
                             reads=[r_d4[j], rg], writes=[r_d4[j]])
                        S.op("dve", lambda e: e.reciprocal(out=den4[j][:], in_=den4[j][:]), reads=[r_d4[j]], writes=[r_d4[j]])
                        for h in range(4):
                            S.op("act", lambda e, h=h: e.activation(out=sqj[:, h * 256:(h + 1) * 256], in_=Xs[j][:, h, 0:256], func=AF.Square,
                                                                    scale=den4[j][:, h:h + 1], accum_out=ssq4[j][:, h:h + 1]),
                                 reads=[r_Xs[j], r_d4[j]], writes=[r_sqj, r_s4[j]])
                        S.op("act", lambda e: e.activation(out=ssq4[j][:], in_=ssq4[j][:], func=AF.Sqrt, scale=1.0 / 256, bias=EPS),
                             reads=[r_s4[j]], writes=[r_s4[j]])

                    def st_C2(c):
                        j = c % 3
                        S.op("dve", lambda e: e.reciprocal(out=fac4[j][:], in_=ssq4[j][:]), reads=[r_s4[j]], writes=[r_f4[j]])
                        S.op("dve", lambda e: e.tensor_tensor(out=fac4[j][:], in0=fac4[j][:], in1=den4[j][:], op=ALU.mult),
                             reads=[r_f4[j], r_d4[j]], writes=[r_f4[j]])
                        for h in range(4):
                            S.op("dve", lambda e, h=h: e.scalar_tensor_tensor(out=hm[j][:, h * 256:(h + 1) * 256], in0=Xs[j][:, h, 0:256],
                                                                              scalar=fac4[j][:, h:h + 1], in1=gheadbc[:, h * 256:(h + 1) * 256],
                                                                              op0=ALU.mult, op1=ALU.mult),
                                 reads=[r_Xs[j], r_f4[j], r_par], writes=[r_hm[j]])

                    def st_D(c):
                        j = c % 3
                        cs = slice(c * 128, (c + 1) * 128)
                        for half in range(2):
                            tb_ = 7
                            for kk in range(4):
                                k = half * 4 + kk
                                S.op("pe", lambda e, k=k, kk=kk: e.matmul(psum[:, tb_, kk * 128:(kk + 1) * 128], lhsT=hm[j][:, k * 128:(k + 1) * 128],
                                                                          rhs=identb[:], start=True, stop=True),
                                     reads=[r_hm[j], r_const], writes=[r_bank[tb_]], inc=(kk == 3))
                            S.op("dve", lambda e: e.tensor_tensor(out=hmT[:, half * 4:half * 4 + 4, cs],
                                                                  in0=psum[:, tb_, :].rearrange("p (k t) -> p k t", k=4),
                                                                  in1=sgoT[:, half * 4:half * 4 + 4, cs], op=ALU.mult),
                                 reads=[r_bank[tb_], r_hmT[c]], writes=[r_hmT[c]])

                    for i in range(nch + 2):
                        if i + 1 < nch:
                            prescale(i + 1)
                        if i < nch:
                            st_A(i)
                        if 0 <= i - 2 < nch:
                            st_C2(i - 2)
                        if 0 <= i - 1 < nch:
                            st_C1(i - 1)
                        if i < nch:
                            st_B(i)
                        if 0 <= i - 2 < nch:
                            st_D(i - 2)
                    S.barrier()
                with ExitStack() as cs_:
                    ycT = sb("ycT", [128, 8, NT], BF16, cs_)
                    zT = sb("zT", [128, 8, NT], BF16, cs_)
                    xs2 = [sb(f"xs{i}", [128, NT], F32, cs_) for i in range(2)]
                    u2 = [sb(f"u{i}", [128, NT + 2], F32, cs_) for i in range(2)]
                    t32 = [sb(f"t3{i}", [128, NT], F32, cs_) for i in range(2)]
                    sg2 = [sb(f"sg{i}", [128, NT], BF16, cs_) for i in range(2)]
                    zc2 = [sb(f"zc{i}", [128, NT], F32, cs_) for i in range(2)]
                    r_ycT = Res("ycT")
                    r_zT = RL("zT", nch)
                    r_xs2, r_u2, r_t32, r_sg2, r_zc2 = RL("xs", 2), RL("u", 2), RL("t3", 2), RL("sg", 2), RL("zc", 2)
                    cv = {}

                    def conv_front(j):
                        xs, u, t3 = xs2[j % 2], u2[j % 2], t32[j % 2]
                        r_xs, r_u, r_t3 = r_xs2[j % 2], r_u2[j % 2], r_t32[j % 2]
                        if j % 4 == 0:
                            cv["x"] = load_w("w_in", O_XIN + (j // 4) * 512)
                            cv["c"] = load_w("w_in", O_CG + (j // 4) * 512)
                            cv[("b", j // 4)] = load_w("w_in", O_BG + (j // 4) * 512)
                        si_x, si_c = cv["x"], cv["c"]
                        wc = (j % 4) * 128
                        def ev_x(b, t0, n, j=j):
                            S.op("act", lambda e: e.activation(out=xs[:, t0:t0 + n], in_=psum[:, b, 0:n], func=AF.Identity, bias=bcol[:, j:j + 1]),
                                 reads=[r_bank[b], r_par], writes=[r_xs])
                        fm_proj(si_x, wc, hT, r_hT, tbs, ev_x)
                        S.op("dve", lambda e, j=j: e.tensor_copy(out=u[:, 0:2], in_=uhalo[:, j, :]), reads=[r_uh], writes=[r_u])
                        def ev_c(b, t0, n, j=j):
                            S.op("dve", lambda e: e.scalar_tensor_tensor(out=u[:, 2 + t0:2 + t0 + n], in0=psum[:, b, 0:n], scalar=bcol[:, 16 + j:17 + j],
                                                                         in1=xs[:, t0:t0 + n], op0=ALU.add, op1=ALU.mult),
                                 reads=[r_bank[b], r_par, r_xs], writes=[r_u])
                        fm_proj(si_c, wc, hT, r_hT, tbs, ev_c)
                        if first:
                            S.op("dve", lambda e: e.tensor_scalar(out=u[:, 128:130], in0=u[:, 128:130], scalar1=flag_t[:, 0:1], scalar2=None, op0=ALU.mult),
                                 reads=[r_u, r_par], writes=[r_u])
                        S.op("dve", lambda e, j=j: e.tensor_copy(out=uhalo[:, j, :], in_=u[:, NT:NT + 2]), reads=[r_u], writes=[r_uh])
                        S.op("act", lambda e, j=j: e.activation(out=t3[:], in_=u[:, 2:NT + 2], func=AF.Identity, scale=cwcol[:, j * 3 + 2:j * 3 + 3]),
                             reads=[r_u, r_par], writes=[r_t3])
                        S.op("dve", lambda e, j=j: e.scalar_tensor_tensor(out=t3[:], in0=u[:, 1:NT + 1], scalar=cwcol[:, j * 3 + 1:j * 3 + 2], in1=t3[:],
                                                                          op0=ALU.mult, op1=ALU.add), reads=[r_u, r_par, r_t3], writes=[r_t3])
                        S.op("dve", lambda e, j=j: e.scalar_tensor_tensor(out=t3[:], in0=u[:, 0:NT], scalar=cwcol[:, j * 3:j * 3 + 1], in1=t3[:],
                                                                          op0=ALU.mult, op1=ALU.add), reads=[r_u, r_par, r_t3], writes=[r_t3])
                        if j % 4 == 3:
                            free_w(si_x, si_c)

                    def conv_back(j):
                        t3, r_t3 = t32[j % 2], r_t32[j % 2]
                        si_b = cv[("b", j // 4)]
                        def ev_b(b, t0, n, j=j):
                            S.op("dve", lambda e: e.scalar_tensor_tensor(out=ycT[:, j, t0:t0 + n], in0=psum[:, b, 0:n], scalar=bcol[:, 8 + j:9 + j],
                                                                         in1=t3[:, t0:t0 + n], op0=ALU.add, op1=ALU.mult),
                                 reads=[r_bank[b], r_par, r_t3], writes=[r_ycT])
                        fm_proj(si_b, (j % 4) * 128, hT, r_hT, tbs, ev_b)
                        if j % 4 == 3:
                            free_w(si_b)

                    conv_front(0)
                    for j in range(1, 8):
                        conv_front(j)
                        conv_back(j - 1)
                    conv_back(7)
                    r_hmT_all = r_hmT
                    for j in range(8):
                        sg, zc = sg2[j % 2], zc2[j % 2]
                        r_sg, r_zc = r_sg2[j % 2], r_zc2[j % 2]
                        if j % 4 == 0:
                            si_gc = load_w("w_in", O_GC + (j // 4) * 512)
                            si_pc = load_w("w_proj_conv", (j // 4) * 512)
                            si_gm = load_w("w_in", O_GM + (j // 4) * 512)
                            si_pm = load_w("w_proj_mlstm", (j // 4) * 512)
                        wc = (j % 4) * 128
                        def ev_g(b, t0, n, col):
                            S.op("act", lambda e: e.activation(out=sg[:, t0:t0 + n], in_=psum[:, b, 0:n], func=AF.Sigmoid, bias=bcol[:, col:col + 1]),
                                 reads=[r_bank[b], r_par], writes=[r_sg])
                        fm_proj(si_gc, wc, hT, r_hT, tbs, lambda b, t0, n, j=j: ev_g(b, t0, n, 32 + j))
                        def ev_pc(b, t0, n):
                            S.op("dve", lambda e: e.tensor_tensor(out=zc[:, t0:t0 + n], in0=psum[:, b, 0:n], in1=sg[:, t0:t0 + n], op=ALU.mult),
                                 reads=[r_bank[b], r_sg], writes=[r_zc])
                        fm_proj(si_pc, wc, ycT, [r_ycT], tbs, ev_pc)
                        fm_proj(si_gm, wc, hT, r_hT, tbs, lambda b, t0, n, j=j: ev_g(b, t0, n, 40 + j))
                        def ev_pm(b, t0, n, j=j):
                            S.op("dve", lambda e: e.tensor_tensor(out=sg[:, t0:t0 + n], in0=psum[:, b, 0:n], in1=sg[:, t0:t0 + n], op=ALU.mult),
                                 reads=[r_bank[b], r_sg], writes=[r_sg])
                            S.op("dve", lambda e: e.tensor_tensor(out=zT[:, j, t0:t0 + n], in0=sg[:, t0:t0 + n], in1=zc[:, t0:t0 + n], op=ALU.add),
                                 reads=[r_sg, r_zc], writes=blk(r_zT, t0, n))
                        fm_proj(si_pm, wc, hmT, r_hmT_all, tbs, ev_pm)
                        if j % 4 == 3:
                            free_w(si_gc, si_pc, si_gm, si_pm)
                    si_o2 = [load_w("w_out", i * 512) for i in range(2)]
                    for i in range(2):
                        S.op("dve", lambda e, i=i: e.tensor_tensor(out=wslot[si_o2[i]][:], in0=wslot[si_o2[i]][:],
                                                                   in1=gt1bc[:, i * 512:(i + 1) * 512].rearrange("p (o n) -> p o n", o=1).to_broadcast([128, 8, 512]),
                                                                   op=ALU.mult),
                             reads=[r_wslot[si_o2[i]], r_par2], writes=[r_wslot[si_o2[i]]])
                    nf2 = {}
                    for c in range(nch):
                        xi = load_x_tile(x_row0 + c * 128)
                        for i in range(2):
                            b = next_bank()
                            for k in range(8):
                                S.op("pe", lambda e, k=k: e.matmul(psum[:, b, :], lhsT=zT[:, k, c * 128:(c + 1) * 128], rhs=wslot[si_o2[i]][:, k, :],
                                                                  start=(k == 0), stop=(k == 7)),
                                     reads=[r_zT[c], r_wslot[si_o2[i]]], writes=[r_bank[b]], inc=(k == 7))
                            S.op("dve", lambda e: e.tensor_tensor(out=x1[:, c, i * 512:(i + 1) * 512], in0=psum[:, b, :],
                                                                  in1=xt[xi][:, i * 512:(i + 1) * 512], op=ALU.add),
                                 reads=[r_bank[b], r_xt[xi]], writes=[r_x1[c]])
                        nf2[c] = norm_front(x1[:, c, :], [r_x1[c]], G2bc)
                        if c >= 1:
                            norm_back(nf2[c - 1], sh2row, hT, (c - 1) * 128, r_hT[c - 1])
                    norm_back(nf2[nch - 1], sh2row, hT, (nch - 1) * 128, r_hT[nch - 1])
                    free_w(*si_o2)
                    S.barrier()
                with ExitStack() as fs:
                    actT2 = [sb(f"actT{i}", [128, 4, NT], BF16, fs) for i in range(2)]
                    a_s2 = [sb(f"a_s{i}", [128, NT + 2], F32, fs) for i in range(2)]
                    t3f2 = [sb(f"t3f{i}", [128, NT], F32, fs) for i in range(2)]
                    sl2 = [sb(f"sl{i}", [128, NT], F32, fs) for i in range(2)]
                    r_actT2 = [RL("actTa", nch), RL("actTb", nch)]
                    r_as2, r_t3f2, r_sl2 = RL("a_s", 2), RL("t3f", 2), RL("sl", 2)
                    groups = [(g * 4, min(4, NFF - g * 4)) for g in range((NFF + 3) // 4)]

                    def final_norm_store(c):
                        ci = ci0 + c
                        if ci < 16:
                            return
                        rs, r_rs = rms_rstd(x1[:, c, :], [r_x1[c]], D)
                        S.op("dve", lambda e: e.scalar_tensor_tensor(out=x1[:, c, :], in0=x1[:, c, :], scalar=rs, in1=gfinbc[:], op0=ALU.mult, op1=ALU.mult),
                             reads=[r_x1[c], r_rs, r_par], writes=[r_x1[c]])
                        S.dma("sp", y_d[(ci - 16) * 128:(ci - 15) * 128, :], x1[:, c, :], reads=[r_x1[c]], writes=[r_y])

                    def emit_down(gi, gn, si_d, wd, last=False):
                        actT, r_actT = actT2[gi % 2], r_actT2[gi % 2]
                        for c in range(nch):
                            for i in range(2):
                                b = next_bank()
                                for jj in range(gn):
                                    S.op("pe", lambda e, jj=jj: e.matmul(psum[:, b, :], lhsT=actT[:, jj, c * 128:(c + 1) * 128], rhs=wd[:, jj, i * 512:(i + 1) * 512],
                                                                        start=(jj == 0), stop=(jj == gn - 1)),
                                         reads=[r_actT[c], r_wslot[si_d]], writes=[r_bank[b]], inc=(jj == gn - 1))
                                S.op("dve", lambda e: e.tensor_tensor(out=x1[:, c, i * 512:(i + 1) * 512], in0=psum[:, b, :],
                                                                      in1=x1[:, c, i * 512:(i + 1) * 512], op=ALU.add),
                                     reads=[r_bank[b], r_x1[c]], writes=[r_x1[c]])
                            if last and c >= 1:
                                final_norm_store(c - 1)
                        if last:
                            final_norm_store(nch - 1)
                        free_w(si_d)

                    pending = None
                    pend_gt = None

                    def up_a(j, jj, si_a):
                        a_s, t3, sl = a_s2[j % 2], t3f2[j % 2], sl2[j % 2]
                        r_as, r_t3, r_sl = r_as2[j % 2], r_t3f2[j % 2], r_sl2[j % 2]
                        S.op("dve", lambda e: e.tensor_copy(out=a_s[:, 0:2], in_=ahalo[:, j, :]), reads=[r_ah], writes=[r_as])
                        def ev_a(b, t0, n):
                            S.op("act", lambda e: e.activation(out=a_s[:, 2 + t0:2 + t0 + n], in_=psum[:, b, 0:n], func=AF.Copy),
                                 reads=[r_bank[b]], writes=[r_as])
                        fm_proj(si_a, jj * 128, hT, r_hT, tbs, ev_a)
                        if first:
                            S.op("dve", lambda e: e.tensor_scalar(out=a_s[:, 128:130], in0=a_s[:, 128:130], scalar1=flag_t[:, 0:1], scalar2=None,
                                                                  op0=ALU.mult), reads=[r_as, r_par], writes=[r_as])
                        S.op("dve", lambda e: e.tensor_copy(out=ahalo[:, j, :], in_=a_s[:, NT:NT + 2]), reads=[r_as], writes=[r_ah])
                        S.op("act", lambda e: e.activation(out=t3[:], in_=a_s[:, 2:NT + 2], func=AF.Identity,
                                                           scale=cwfcol[:, j * 3 + 2:j * 3 + 3]), reads=[r_as, r_par], writes=[r_t3])
                        S.op("dve", lambda e: e.scalar_tensor_tensor(out=t3[:], in0=a_s[:, 1:NT + 1], scalar=cwfcol[:, j * 3 + 1:j * 3 + 2], in1=t3[:],
                                                                     op0=ALU.mult, op1=ALU.add), reads=[r_as, r_par, r_t3], writes=[r_t3])
                        S.op("dve", lambda e: e.scalar_tensor_tensor(out=t3[:], in0=a_s[:, 0:NT], scalar=cwfcol[:, j * 3:j * 3 + 1], in1=t3[:],
                                                                     op0=ALU.mult, op1=ALU.add), reads=[r_as, r_par, r_t3], writes=[r_t3])
                        S.op("act", lambda e: e.activation(out=sl[:], in_=t3[:], func=AF.Silu), reads=[r_t3], writes=[r_sl])

                    def up_g(j, jj, si_g, actT, r_actT, release):
                        sl, r_sl = sl2[j % 2], r_sl2[j % 2]
                        def ev_gt(b, t0, n):
                            S.op("dve", lambda e: e.tensor_tensor(out=actT[:, jj, t0:t0 + n], in0=psum[:, b, 0:n], in1=sl[:, t0:t0 + n], op=ALU.mult),
                                 reads=[r_bank[b], r_sl], writes=blk(r_actT, t0, n))
                        fm_proj(si_g, jj * 128, hT, r_hT, tbs, ev_gt)
                        if release:
                            free_w(si_g)

                    for gi, (j0, gn) in enumerate(groups):
                        if next_ci0 is not None and gi == len(groups) - 1:
                            pf = {}
                            for c_ in range(2):
                                xi = load_x_tile((next_ci0 + c_) * 128)
                                pf[c_] = norm_front(xt[xi][:], [r_xt[xi]], G1bc)
                            prefront["nf"] = pf
                        actT, r_actT = actT2[gi % 2], r_actT2[gi % 2]
                        ncol = gn * 128
                        si_a = load_w("w_up", j0 * 128, 8, ncol)
                        si_g = load_w("w_up", DFF + j0 * 128, 8, ncol)
                        si_d = load_w("w_down", 0, gn, 1024, r0=j0 * 128)
                        wd = slot_view(si_d, gn, 1024)
                        S.op("dve", lambda e: e.tensor_tensor(out=wd, in0=wd,
                                                              in1=gt2bc[:].rearrange("p (o n) -> p o n", o=1).to_broadcast([128, gn, 1024]), op=ALU.mult),
                             reads=[r_wslot[si_d], r_par2], writes=[r_wslot[si_d]])
                        for jj in range(gn):
                            j = j0 + jj
                            up_a(j, jj, si_a)
                            if pend_gt is not None:
                                up_g(*pend_gt)
                            pend_gt = (j, jj, si_g, actT, r_actT, jj == gn - 1)
                            if pending is not None and jj == min(1, gn - 1):
                                emit_down(*pending)
                                pending = None
                        free_w(si_a)
                        pending = (gi, gn, si_d, wd)
                    up_g(*pend_gt)
                    emit_down(*pending, last=True)
                    S.barrier()

        run_stage(15, 5, True, next_ci0=20)
        run_stage(20, 6, False, next_ci0=26)
        run_stage(26, 6, False)
        S.barrier()
        build_nc.stats = (S.ninstr, dict(S.cnt))
        if plan is not None:
            assert W["use"] == len(plan), (W["use"], len(plan))
    return nc


def _col(v, nchunk):
    return np.ascontiguousarray(np.asarray(v, np.float32).reshape(nchunk, 128).T)


def make_in_maps(x, c, w_ada, b_ada, g_norm_mix, w_in, b_in, conv_mix_w, mlstm_head_g, w_proj_conv, w_proj_mlstm,
                 w_out, g_norm_ffn, w_up, conv_ffn_w, w_down, g_final):
    f = lambda a: np.ascontiguousarray(np.asarray(a, dtype=np.float32))
    x = f(x)
    c = f(c)
    b = f(b_in)[0]
    b_col = np.concatenate([_col(b[O_XIN:O_XIN + 1024], 8), _col(b[O_BG:O_BG + 1024], 8), _col(b[O_CG:O_CG + 1024], 8),
                            _col(b[O_Q:O_Q + 512], 4), _col(b[O_K:O_K + 512], 4), _col(b[O_GC:O_GC + 1024], 8),
                            _col(b[O_GM:O_GM + 1024], 8), _col(b[O_O:O_O + 1024], 8)], axis=1)
    b_row = np.concatenate([b[O_K:O_K + 512], b[O_V:O_V + 1024]])[None, :]
    b_gate = b[O_IG:O_IG + 8][None, :]
    cw_col = f(conv_mix_w)[0].reshape(3, 8, 128).transpose(2, 1, 0).reshape(128, 24)
    cwf_col = f(conv_ffn_w)[0].reshape(3, NFF, 128).transpose(2, 1, 0).reshape(128, 3 * NFF)
    shared = {
        "w_ada": f(w_ada)[0], "b_ada": f(b_ada)[0][None, :], "g_norm_mix": f(g_norm_mix)[0][None, :],
        "w_in": f(w_in)[0], "b_col": np.ascontiguousarray(b_col), "b_row": np.ascontiguousarray(b_row),
        "b_gate": np.ascontiguousarray(b_gate), "cw_col": np.ascontiguousarray(cw_col),
        "g_head": f(mlstm_head_g)[0].reshape(1, D), "w_proj_conv": f(w_proj_conv)[0], "w_proj_mlstm": f(w_proj_mlstm)[0],
        "w_out": f(w_out)[0], "g_norm_ffn": f(g_norm_ffn)[0][None, :], "w_up": f(w_up)[0],
        "cwf_col": np.ascontiguousarray(cwf_col), "w_down": f(w_down)[0], "g_final": f(g_final)[None, :],
    }
    in_maps = []
    for core in range(8):
        bi, hf = core // 2, core % 2
        if hf == 1:
            xin = x[bi]
        else:
            xin = np.concatenate([x[bi, 0:2048], x[bi, 0:2048]], axis=0)
        m = dict(shared)
        m["xin"] = np.ascontiguousarray(xin)
        m["ccol"] = _col(c[bi], 8)
        m["flag"] = np.full((128, 1), float(hf), np.float32)
        in_maps.append(m)
    return in_maps


_NC_CACHE = {}


def kernel(**inputs):
    if "nc" not in _NC_CACHE:
        _NC_CACHE["nc"] = build_nc()
    nc = _NC_CACHE["nc"]
    in_maps = make_in_maps(**inputs)
    res = run_bass_kernel_spmd(nc, in_maps, core_ids=list(range(8)))
    out = np.empty((4, 4096, D), np.float32)
    for core in range(8):
        bi, hf = core // 2, core % 2
        out[bi, hf * 2048:(hf + 1) * 2048] = res.results[core]["y"]
    return out
```

```python
import numpy as np
import concourse.bass as bass
import concourse.mybir as mybir
from concourse.bass_utils import run_bass_kernel_spmd
from contextlib import ExitStack

F32 = mybir.dt.float32
BF16 = mybir.dt.bfloat16
AF = mybir.ActivationFunctionType
ALU = mybir.AluOpType
AX = mybir.AxisListType

D = 1024
NIN = 8200
DFF = 2816
EPS = 1e-6
O_XIN, O_BG, O_CG, O_Q, O_K, O_V, O_O, O_IG, O_FG, O_GC, O_GM = 0, 1024, 2048, 3072, 3584, 4096, 5120, 6144, 6148, 6152, 7176
NSLOT = 6
NFF = 22


class Res:
    __slots__ = ("name", "w", "r")

    def __init__(self, name):
        self.name = name
        self.w = None
        self.r = {}


def RL(name, n):
    return [Res(f"{name}{i}") for i in range(n)]


class Sched:
    def __init__(self, nc, es):
        self.nc = nc
        self.engs = {"pe": nc.tensor, "act": nc.scalar, "dve": nc.vector, "pool": nc.gpsimd, "sp": nc.sync}
        self.sem = {k: es.enter_context(nc.semaphore("s_" + k)) for k in self.engs}
        self.cnt = {k: 0 for k in self.engs}
        self.pending = {k: False for k in self.engs}
        self.seen = {k: {} for k in self.engs}
        self.dq = {}
        for q, n in (("sp", 8), ("pool", 10)):
            self.dq[q] = {"sems": [es.enter_context(nc.semaphore(f"d_{q}{i}")) for i in range(n)],
                          "vals": [0] * n, "next": 0}
        self.semobj = dict(self.sem)
        for q in self.dq:
            for i, s in enumerate(self.dq[q]["sems"]):
                self.semobj[(q, i)] = s
        self.ninstr = 0

    def _deps(self, e, reads, writes):
        deps = {}

        def add(ev, same_ok):
            if ev is None:
                return
            k, v = ev
            if k == e and not same_ok:
                return
            if deps.get(k, 0) < v:
                deps[k] = v

        same_raw = e in ("act", "dve", "pool")
        for r in reads:
            add(r.w, same_raw)
        for w in writes:
            add(w.w, False)
            for k, v in w.r.items():
                add((k, v), False)
        return deps

    def _wait(self, e, deps):
        eng = self.engs[e]
        seen = self.seen[e]
        for k, v in deps.items():
            if seen.get(k, 0) >= v:
                continue
            eng.wait_ge(self.semobj[k], v)
            seen[k] = v
            self.ninstr += 1

    def _mark(self, ev, reads, writes):
        k, v = ev
        for r in reads:
            if r.r.get(k, 0) < v:
                r.r[k] = v
        for w in writes:
            w.w = ev
            w.r = {}

    def op(self, e, fn, reads=(), writes=(), inc=True):
        deps = self._deps(e, reads, writes)
        self._wait(e, deps)
        ins = fn(self.engs[e])
        self.ninstr += 1
        if inc:
            ins.then_inc(self.sem[e], 1)
            self.cnt[e] += 1
            self.pending[e] = False
            ev = (e, self.cnt[e])
        else:
            self.pending[e] = True
            ev = (e, self.cnt[e] + 1)
        self._mark(ev, reads, writes)
        return ins

    def dma(self, q, out, in_, reads=(), writes=()):
        dq = self.dq[q]
        i = dq["next"]
        dq["next"] = (i + 1) % len(dq["sems"])
        deps = self._deps(q, reads, writes)
        if dq["vals"][i] > 0:
            k = (q, i)
            if deps.get(k, 0) < dq["vals"][i]:
                deps[k] = dq["vals"][i]
        self._wait(q, deps)
        ins = self.engs[q].dma_start(out=out, in_=in_)
        ins.then_inc(dq["sems"][i], 16)
        dq["vals"][i] += 16
        self.ninstr += 1
        ev = ((q, i), dq["vals"][i])
        self._mark(ev, reads, writes)
        return ins

    def flush_pe(self):
        if self.pending["pe"]:
            self.nc.tensor.nop().then_inc(self.sem["pe"], 1)
            self.cnt["pe"] += 1
            self.pending["pe"] = False

    def barrier(self, engines=("pe", "act", "dve"), queues=("sp",)):
        self.flush_pe()
        deps = {k: self.cnt[k] for k in self.engs if self.cnt[k] > 0}
        for q in queues:
            for i, v in enumerate(self.dq[q]["vals"]):
                if v > 0:
                    deps[(q, i)] = v
        for e in engines:
            d = {k: v for k, v in deps.items() if k != e or e in ("act", "dve")}
            self._wait(e, d)


def build_nc(debug=False):
    rec = []
    marks = []
    _build(None, rec, marks)
    return _build(rec, None, marks)


def _build(plan, rec, marks):
    nc = bass.Bass("TRN2", target_bir_lowering=False)
    dram = lambda name, shape, kind="ExternalInput": nc.dram_tensor(name, list(shape), F32, kind=kind).ap()
    x_d = dram("xin", [4096, D])
    ccol_d = dram("ccol", [128, 8])
    flag_d = dram("flag", [128, 1])
    wada_d = dram("w_ada", [D, 6 * D])
    bada_d = dram("b_ada", [1, 6 * D])
    gmix_d = dram("g_norm_mix", [1, D])
    win_d = dram("w_in", [D, NIN])
    bcol_d = dram("b_col", [128, 56])
    brow_d = dram("b_row", [1, 512 + 1024])
    bgate_d = dram("b_gate", [1, 8])
    cw_d = dram("cw_col", [128, 24])
    ghead_d = dram("g_head", [1, D])
    wpc_d = dram("w_proj_conv", [D, D])
    wpm_d = dram("w_proj_mlstm", [D, D])
    wout_d = dram("w_out", [D, D])
    gffn_d = dram("g_norm_ffn", [1, D])
    wup_d = dram("w_up", [D, 2 * DFF])
    cwf_d = dram("cwf_col", [128, 3 * NFF])
    wdn_d = dram("w_down", [DFF, D])
    gfin_d = dram("g_final", [1, D])
    y_d = dram("y", [2048, D], kind="ExternalOutput")

    with ExitStack() as es:
        S = Sched(nc, es)
        uid = [0]

        def sb(name, shape, dt=F32, st=es):
            uid[0] += 1
            return st.enter_context(nc.sbuf_tensor(f"{name}_{uid[0]}", list(shape), dt))

        identb = sb("identb", [128, 128], BF16)
        identf = sb("identf", [128, 128])
        onesf = sb("onesf", [128, 128])
        onesb = sb("onesb", [1, 512], BF16)
        maskf = sb("maskf", [128, 128])
        flag_t = sb("flag_t", [128, 1])
        ccol = sb("ccol", [128, 8])
        scol = sb("scol", [128, 8], BF16)
        G1bc = sb("G1bc", [128, D])
        G2bc = sb("G2bc", [128, D])
        gt1bc = sb("gt1bc", [128, D], BF16)
        gt2bc = sb("gt2bc", [128, D], BF16)
        sh1row = sb("sh1col", [128, 8])
        sh2row = sb("sh2col", [128, 8])
        gheadbc = sb("gheadbc", [128, D])
        gfinbc = sb("gfinbc", [128, D])
        bcol = sb("bcol", [128, 56])
        bqs = sb("bqs", [128, 4])
        browbc = sb("browbc", [128, 1536], BF16)
        bgatebc = sb("bgatebc", [128, 8])
        cwcol = sb("cwcol", [128, 24])
        cwfcol = sb("cwfcol", [128, 3 * NFF])
        wgate = sb("wgate", [128, 8, 8], BF16)
        C32 = sb("C32", [128, 4, 257])
        Cb = sb("Cb", [128, 4, 257], BF16)
        mcarry = sb("mcarry", [1, 4])
        uhalo = sb("uhalo", [128, 8, 2])
        ahalo = sb("ahalo", [128, NFF, 2])
        wslot = [sb(f"wslot{i}", [128, 8, 512], BF16) for i in range(NSLOT)]
        xt = [sb(f"xt{i}", [128, D]) for i in range(2)]
        xnb = [sb(f"xnb{i}", [128, D], BF16) for i in range(2)]
        sqj = sb("sqj", [128, D], BF16)
        stat = sb("stat", [128, 64])
        psum = es.enter_context(nc.psum_tensor("psum", [128, 8, 512], F32))

        r_wslot = RL("wslot", NSLOT)
        r_xt = RL("xt", 2)
        r_xnb = RL("xnb", 2)
        r_bank = RL("bank", 8)
        r_const = Res("const")
        r_par = Res("par")
        r_par2 = Res("par2")
        r_mod = Res("modrow")
        r_sqj = Res("sqj")
        r_C32 = RL("C32_", 4)
        r_Cb = RL("Cb_", 4)
        r_mc = Res("mcarry")
        r_uh = Res("uhalo")
        r_ah = Res("ahalo")
        r_y = Res("ydram")
        r_in = Res("indram")
        r_wgate = Res("wgate")

        state = {"slot": 0, "bank": 0, "xt": 0, "xnb": 0, "stat": 0, "resv": set()}

        def next_bank():
            b = state["bank"]
            while b in state["resv"]:
                b = (b + 1) % 8
            state["bank"] = (b + 1) % 8
            return b

        def stat_col(n=1):
            c = state["stat"]
            if c + n > 64:
                c = 0
            state["stat"] = c + n
            return stat[:, c:c + n], r_stat[c]

        r_stat = RL("stat", 64)

        dr = {"w_ada": wada_d, "w_in": win_d, "w_proj_conv": wpc_d, "w_proj_mlstm": wpm_d, "w_out": wout_d, "w_up": wup_d, "w_down": wdn_d}
        W = {"issue": 0, "use": 0, "free": list(range(NSLOT)), "slot_of": {}, "fence": 0, "stage": 0}

        def slot_view(i, kc, ncol):
            if ncol <= 512:
                return wslot[i][:, 0:kc, 0:ncol]
            return wslot[i][:].rearrange("p k n -> p (k n)")[:, 0:kc * ncol].rearrange("p (k n) -> p k n", n=ncol)

        def issue_w(i, spec):
            name, r0, c0, kc, ncol = spec
            extra = state.pop("w_gate", [])
            S.dma("pool", slot_view(i, kc, ncol), dr[name][r0:r0 + kc * 128, c0:c0 + ncol].rearrange("(k p) n -> p k n", p=128),
                  reads=[r_in] + extra, writes=[r_wslot[i]])

        if plan is not None:
            W["fence"] = marks[0]

        def pump_w():
            while W["free"] and W["issue"] < min(len(plan), W["fence"]) and W["issue"] - W["use"] < 3:
                i = W["free"].pop(0)
                issue_w(i, plan[W["issue"]])
                W["slot_of"][W["issue"]] = i
                W["issue"] += 1

        def load_w(name, c0, kc=8, ncol=512, r0=0):
            spec = (name, r0, c0, kc, ncol)
            if plan is None:
                rec.append(spec)
                i = state["slot"]
                state["slot"] = (i + 1) % NSLOT
                issue_w(i, spec)
                return i
            pump_w()
            k = W["use"]
            assert plan[k] == spec, (k, plan[k], spec)
            assert k in W["slot_of"], f"weight tile {k} {spec} not issued: missing free_w()?"
            W["use"] += 1
            return W["slot_of"][k]

        def free_w(*slots):
            if plan is None:
                return
            for i in slots:
                assert i not in W["free"]
                W["free"].append(i)
            pump_w()

        S.op("pool", lambda e: e.memset(identf[:], 1.0), writes=[r_const])
        S.op("pool", lambda e: e.affine_select(out=identf[:], in_=identf[:], pattern=[[-1, 128]], compare_op=ALU.is_equal,
                                               fill=0.0, base=0, channel_multiplier=1), reads=[r_const], writes=[r_const])
        S.op("pool", lambda e: e.memset(maskf[:], 1.0), writes=[r_const])
        S.op("pool", lambda e: e.affine_select(out=maskf[:], in_=maskf[:], pattern=[[1, 128]], compare_op=ALU.is_ge,
                                               fill=0.0, base=0, channel_multiplier=-1), reads=[r_const], writes=[r_const])
        S.op("dve", lambda e: e.memset(onesf[:], 1.0), writes=[r_const])
        S.op("dve", lambda e: e.memset(onesb[:], 1.0), writes=[r_const])
        S.op("dve", lambda e: e.tensor_copy(out=identb[:], in_=identf[:]), reads=[r_const], writes=[r_const])
        S.op("dve", lambda e: e.memset(C32[:], 0.0), writes=r_C32)
        S.op("dve", lambda e: e.memset(mcarry[:], 0.0), writes=[r_mc])
        S.op("dve", lambda e: e.memset(uhalo[:], 0.0), writes=[r_uh])
        S.op("dve", lambda e: e.memset(ahalo[:], 0.0), writes=[r_ah])

        def pload(dst, src):
            S.dma("sp", dst, src, reads=[r_in], writes=[r_par])

        r_crit = Res("crit")
        ada = ExitStack()
        modrow = sb("modrow", [1, 6 * D], F32, ada)
        growA = sb("growA", [1, D], F32, ada)
        growB = sb("growB", [1, D], F32, ada)
        browst = sb("browst", [128, 1536], F32, ada)
        S.dma("sp", ccol[:], ccol_d, reads=[r_in], writes=[r_crit])
        S.dma("sp", modrow[:], bada_d, reads=[r_in], writes=[r_mod])
        S.dma("sp", growA[:], gmix_d, reads=[r_in], writes=[r_crit])
        pload(flag_t[:], flag_d)
        pload(growB[:], gffn_d)
        pload(gheadbc[:], ghead_d.partition_broadcast(128))
        pload(gfinbc[:], gfin_d.partition_broadcast(128))
        pload(bcol[:], bcol_d)
        pload(browst[:], brow_d.partition_broadcast(128))
        S.op("dve", lambda e: e.tensor_copy(out=browbc[:], in_=browst[:]), reads=[r_par], writes=[r_par])
        pload(bgatebc[:], bgate_d.partition_broadcast(128))
        pload(cwcol[:], cw_d)
        pload(cwfcol[:], cwf_d)
        state["w_gate"] = [r_crit, r_mod]
        S.dma("pool", wgate[:], win_d[:, O_IG:O_IG + 8].rearrange("(k p) n -> p k n", p=128), reads=[r_in], writes=[r_wgate])

        S.op("act", lambda e: e.activation(out=scol[:], in_=ccol[:], func=AF.Silu), reads=[r_crit], writes=[r_crit])
        QS = 128.0 ** -0.5
        S.op("dve", lambda e: e.tensor_scalar(out=bqs[:], in0=bcol[:, 24:28], scalar1=QS, scalar2=None, op0=ALU.mult),
             reads=[r_par], writes=[r_par])

        def ada_blocks(nlist):
            for n in nlist:
                si = load_w("w_ada", n * 512)
                b = next_bank()
                for k in range(8):
                    S.op("pe", lambda e, k=k: e.matmul(psum[0:1, b, :], lhsT=scol[:, k:k + 1], rhs=wslot[si][:, k, :],
                                                      start=(k == 0), stop=(k == 7)),
                         reads=[r_crit, r_wslot[si]], writes=[r_bank[b]], inc=(k == 7))
                S.op("dve", lambda e: e.tensor_tensor(out=modrow[0:1, n * 512:(n + 1) * 512], in0=psum[0:1, b, :],
                                                      in1=modrow[0:1, n * 512:(n + 1) * 512], op=ALU.add),
                     reads=[r_bank[b], r_mod], writes=[r_mod])
                free_w(si)

        def bcast_row(row_ap, dst, r_dst):
            for h in range(2):
                b = next_bank()
                S.op("pe", lambda e: e.matmul(psum[:, b, :], lhsT=onesf[0:1, :], rhs=row_ap[0:1, h * 512:(h + 1) * 512], start=True, stop=True),
                     reads=[r_mod, r_const], writes=[r_bank[b]])
                S.op("act", lambda e: e.activation(out=dst[:, h * 512:(h + 1) * 512], in_=psum[:, b, :], func=AF.Copy),
                     reads=[r_bank[b]], writes=[r_dst])

        def row_to_col(off, dst, r_dst):
            b = next_bank()
            for k in range(8):
                S.op("pe", lambda e, k=k: e.matmul(psum[:, b, k:k + 1], lhsT=modrow[0:1, off + k * 128:off + (k + 1) * 128], rhs=onesf[0:1, 0:1],
                                                  start=True, stop=True),
                     reads=[r_mod, r_const], writes=[r_bank[b]], inc=(k == 7))
            S.op("act", lambda e: e.activation(out=dst[:], in_=psum[:, b, 0:8], func=AF.Copy), reads=[r_bank[b]], writes=[r_dst])

        def ada_part1():
            ada_blocks([2, 3, 0, 1])
            S.op("dve", lambda e: e.scalar_tensor_tensor(out=growA[:], in0=modrow[0:1, D:2 * D], scalar=1.0, in1=growA[:],
                                                         op0=ALU.add, op1=ALU.mult), reads=[r_mod, r_crit], writes=[r_mod])
            bcast_row(growA, G1bc, r_par)
            row_to_col(0, sh1row, r_par)

        def ada_part2():
            S.op("dve", lambda e: e.scalar_tensor_tensor(out=growB[:], in0=modrow[0:1, 4 * D:5 * D], scalar=1.0, in1=growB[:],
                                                         op0=ALU.add, op1=ALU.mult), reads=[r_mod, r_par], writes=[r_mod])
            bcast_row(growB, G2bc, r_par2)
            bcast_row(modrow[0:1, 2 * D:3 * D], gt1bc, r_par2)
            bcast_row(modrow[0:1, 5 * D:6 * D], gt2bc, r_par2)
            row_to_col(3 * D, sh2row, r_par2)

        ada_part1()

        def rms_rstd(src_ap, src_res, n_feat):
            ssq, r_ssq = stat_col()
            S.op("act", lambda e: e.activation(out=sqj[:, 0:n_feat], in_=src_ap, func=AF.Square, accum_out=ssq),
                 reads=src_res, writes=[r_sqj, r_ssq])
            sd, r_sd = stat_col()
            S.op("act", lambda e: e.activation(out=sd, in_=ssq, func=AF.Sqrt, scale=1.0 / n_feat, bias=EPS),
                 reads=[r_ssq], writes=[r_sd])
            rs, r_rs = stat_col()
            S.op("dve", lambda e: e.reciprocal(out=rs, in_=sd), reads=[r_sd], writes=[r_rs])
            return rs, r_rs

        def norm_front(src_ap, src_res, Gbc):
            rs, r_rs = rms_rstd(src_ap, src_res, D)
            i = state["xnb"]
            state["xnb"] = 1 - i
            S.op("dve", lambda e: e.scalar_tensor_tensor(out=xnb[i][:], in0=src_ap, scalar=rs, in1=Gbc[:], op0=ALU.mult, op1=ALU.mult),
                 reads=list(src_res) + [r_rs, r_par, r_par2], writes=[r_xnb[i]])
            return i

        def norm_back(i, shrow, dstT, col0, dst_res):
            for half in range(2):
                b = next_bank()
                for kk in range(4):
                    k = half * 4 + kk
                    S.op("pe", lambda e, k=k, kk=kk: e.matmul(psum[:, b, kk * 128:(kk + 1) * 128], lhsT=xnb[i][:, k * 128:(k + 1) * 128],
                                                              rhs=identb[:], start=True, stop=True),
                         reads=[r_xnb[i], r_const], writes=[r_bank[b]], inc=(kk == 3))
                for kk in range(4):
                    k = half * 4 + kk
                    if half == 0:
                        S.op("act", lambda e, k=k, kk=kk: e.activation(out=dstT[:, k, col0:col0 + 128], in_=psum[:, b, kk * 128:(kk + 1) * 128],
                                                                       func=AF.Identity, bias=shrow[:, k:k + 1]),
                             reads=[r_bank[b], r_par, r_par2], writes=[dst_res])
                    else:
                        S.op("dve", lambda e, k=k, kk=kk: e.tensor_scalar(out=dstT[:, k, col0:col0 + 128], in0=psum[:, b, kk * 128:(kk + 1) * 128],
                                                                          scalar1=shrow[:, k:k + 1], scalar2=None, op0=ALU.add),
                             reads=[r_bank[b], r_par, r_par2], writes=[dst_res])

        def load_x_tile(row0):
            i = state["xt"]
            state["xt"] = (i + 1) % 2
            S.dma("sp", xt[i][:], x_d[row0:row0 + 128, :], reads=[r_in], writes=[r_xt[i]])
            return i

        def fm_proj(si_list, wcol0, hT, r_h, tbs, evac):
            si = si_list
            for (t0, n) in tbs:
                b = next_bank()
                for k in range(8):
                    S.op("pe", lambda e, k=k: e.matmul(psum[:, b, 0:n], lhsT=wslot[si][:, k, wcol0:wcol0 + 128], rhs=hT[:, k, t0:t0 + n],
                                                      start=(k == 0), stop=(k == 7)),
                         reads=[r_wslot[si]] + r_h, writes=[r_bank[b]], inc=(k == 7))
                evac(b, t0, n)

        def interleave(gen, units):
            for u_ in units:
                next(gen, None)
                u_()
            for _ in gen:
                pass

        def gate_math(scope, graw, r_graw, nch, first_stage_flag, want_dmin, GB, out):
            ng = nch * 4
            T = lambda name, shape: sb(name, shape, F32, scope)
            ig = graw[:, 0:nch, 0:4]
            fg = graw[:, 0:nch, 4:8]
            t_abs = T("g_abs", [128, nch, 4])
            t_e = T("g_e", [128, nch, 4])
            logf = T("g_logf", [128, nch, 4])
            alpha = T("g_alpha", [128, ng])
            b_sb = T("g_b", [128, ng])
            acol = T("g_acol", [128, 1])
            arow = T("g_arow", [1, nch, 4])
            grow = T("g_grow", [1, nch, 4])
            mb = T("g_mb", [1, nch + 1, 4])
            mp = T("g_mp", [1, nch, 4])
            sold_row = T("g_soldrow", [1, nch, 4])
            tmp = T("g_tmp", [128, ng])
            wG = T("g_wG", [128, ng])
            sold = T("g_sold", [128, ng])
            dmin = T("g_dmin", [128, ng])
            rg = Res("gm")
            S.op("dve", lambda e: e.scalar_tensor_tensor(out=t_abs[:], in0=fg, scalar=-1.0, in1=fg, op0=ALU.mult, op1=ALU.max),
                 reads=[r_graw], writes=[rg])
            S.op("act", lambda e: e.activation(out=t_e[:], in_=t_abs[:], func=AF.Exp, scale=-1.0), reads=[rg], writes=[rg])
            S.op("act", lambda e: e.activation(out=t_e[:], in_=t_e[:], func=AF.Ln, bias=1.0), reads=[rg], writes=[rg])
            S.op("dve", lambda e: e.scalar_tensor_tensor(out=logf[:], in0=fg, scalar=0.0, in1=t_e[:], op0=ALU.min, op1=ALU.subtract),
                 reads=[rg, r_graw], writes=[rg])
            logf2 = logf[:].rearrange("p c h -> p (c h)")
            b0, b1, b2, b3 = GB[0], GB[1], GB[0], GB[1]
            yield
            S.op("pe", lambda e: e.matmul(psum[:, b0, 0:ng], lhsT=maskf[:], rhs=logf2, start=True, stop=True),
                 reads=[rg, r_const], writes=[r_bank[b0]])
            S.op("pe", lambda e: e.matmul(psum[0:1, b1, 0:ng], lhsT=onesf[:, 0:1], rhs=logf2, start=True, stop=True),
                 reads=[rg, r_const], writes=[r_bank[b1]])
            S.op("dve", lambda e: e.tensor_tensor(out=alpha[:].rearrange("p (c h) -> p c h", h=4), in0=ig,
                                                  in1=psum[:, b0, 0:ng].rearrange("p (c h) -> p c h", h=4), op=ALU.subtract),
                 reads=[r_graw, r_bank[b0]], writes=[rg])
            S.op("dve", lambda e: e.tensor_copy(out=b_sb[:], in_=psum[:, b0, 0:ng]), reads=[r_bank[b0]], writes=[rg])
            S.op("act", lambda e: e.activation(out=grow[:].rearrange("p c h -> p (c h)"), in_=psum[0:1, b1, 0:ng], func=AF.Copy),
                 reads=[r_bank[b1]], writes=[rg])
            yield
            S.op("pe", lambda e: e.matmul(psum[0:ng, b2, 0:128], lhsT=alpha[:], rhs=identf[:], start=True, stop=True),
                 reads=[rg, r_const], writes=[r_bank[b2]])
            S.op("dve", lambda e: e.tensor_reduce(out=acol[0:ng, :], in_=psum[0:ng, b2, 0:128], axis=AX.X, op=ALU.max),
                 reads=[r_bank[b2]], writes=[rg])
            yield
            S.op("pe", lambda e: e.matmul(psum[0:1, b3, 0:ng], lhsT=acol[0:ng, 0:1], rhs=identf[0:ng, 0:ng], start=True, stop=True),
                 reads=[rg, r_const], writes=[r_bank[b3]])
            S.op("act", lambda e: e.activation(out=arow[:].rearrange("p c h -> p (c h)"), in_=psum[0:1, b3, 0:ng], func=AF.Copy),
                 reads=[r_bank[b3]], writes=[rg])
            yield
            S.op("dve", lambda e: e.tensor_copy(out=mb[0:1, 0, :], in_=mcarry[:]), reads=[r_mc], writes=[rg])
            for c in range(nch):
                S.op("dve", lambda e, c=c: e.tensor_tensor(out=mp[0:1, c, :], in0=mb[0:1, c, :], in1=arow[0:1, c, :], op=ALU.max),
                     reads=[rg], writes=[rg])
                S.op("dve", lambda e, c=c: e.tensor_tensor(out=mb[0:1, c + 1, :], in0=mp[0:1, c, :], in1=grow[0:1, c, :], op=ALU.add),
                     reads=[rg], writes=[rg])
                if first_stage_flag and c == 0:
                    S.op("dve", lambda e, c=c: e.tensor_scalar(out=mb[0:1, c + 1, :], in0=mb[0:1, c + 1, :], scalar1=flag_t[0:1, 0:1],
                                                               scalar2=None, op0=ALU.mult), reads=[rg, r_par], writes=[rg])
            S.op("dve", lambda e: e.tensor_copy(out=mcarry[:], in_=mb[0:1, nch, :]), reads=[rg], writes=[r_mc])
            S.op("dve", lambda e: e.tensor_tensor(out=sold_row[:], in0=mb[0:1, 0:nch, :], in1=mp[:], op=ALU.subtract),
                 reads=[rg], writes=[rg])
            S.op("act", lambda e: e.activation(out=sold_row[:], in_=sold_row[:], func=AF.Exp), reads=[rg], writes=[rg])
            if first_stage_flag:
                S.op("dve", lambda e: e.tensor_scalar(out=sold_row[0:1, 1, :], in0=sold_row[0:1, 1, :], scalar1=flag_t[0:1, 0:1],
                                                      scalar2=None, op0=ALU.mult), reads=[rg, r_par], writes=[rg])
            b4, b5 = GB[0], GB[1]
            yield
            yield
            S.op("pe", lambda e: e.matmul(psum[:, b4, 0:ng], lhsT=onesf[0:1, :], rhs=mp[:].rearrange("p c h -> p (c h)"), start=True, stop=True),
                 reads=[rg, r_const], writes=[r_bank[b4]])
            S.op("pe", lambda e: e.matmul(psum[:, b5, 0:ng], lhsT=onesf[0:1, :], rhs=sold_row[:].rearrange("p c h -> p (c h)"),
                                          start=True, stop=True), reads=[rg, r_const], writes=[r_bank[b5]])
            S.op("act", lambda e: e.activation(out=sold[:], in_=psum[:, b5, 0:ng], func=AF.Copy), reads=[r_bank[b5]], writes=[rg])
            S.op("dve", lambda e: e.tensor_tensor(out=tmp[:], in0=alpha[:], in1=psum[:, b4, 0:ng], op=ALU.subtract),
                 reads=[rg, r_bank[b4]], writes=[rg])
            S.op("act", lambda e: e.activation(out=wG[:], in_=tmp[:], func=AF.Exp), reads=[rg], writes=[rg])
            if want_dmin:
                S.op("dve", lambda e: e.tensor_tensor(out=tmp[:], in0=b_sb[:], in1=psum[:, b4, 0:ng], op=ALU.add),
                     reads=[rg, r_bank[b4]], writes=[rg])
                S.op("act", lambda e: e.activation(out=dmin[:], in_=tmp[:], func=AF.Exp, scale=-1.0), reads=[rg], writes=[rg])
            out.update(wG=wG, sold=sold, dmin=dmin, rg=rg)

        def kv_gate_proj(c, hTt, r_hT, col0, si_k, si_v, ktok, r_ktok, vaug, r_vaug, gbank, ridx, kT_src=None):
            b = next_bank()
            if kT_src is None:
                for k in range(8):
                    S.op("pe", lambda e, k=k: e.matmul(psum[:, b, :], lhsT=hTt[:, k, col0:col0 + 128], rhs=wslot[si_k][:, k, :],
                                                      start=(k == 0), stop=(k == 7)),
                         reads=[r_hT, r_wslot[si_k]], writes=[r_bank[b]], inc=(k == 7))
                S.op("dve", lambda e: e.tensor_tensor(out=ktok[:, ridx, :], in0=psum[:, b, :], in1=browbc[:, 0:512], op=ALU.add),
                     reads=[r_bank[b], r_par], writes=[r_ktok[ridx]])
            else:
                kT_, r_kT_c = kT_src
                for h in range(4):
                    S.op("pe", lambda e, h=h: e.matmul(psum[:, b, h * 128:(h + 1) * 128], lhsT=kT_[:, h, col0:col0 + 128], rhs=identb[:],
                                                      start=True, stop=True),
                         reads=[r_kT_c, r_const], writes=[r_bank[b]], inc=(h == 3))
                S.op("dve", lambda e: e.tensor_copy(out=ktok[:, ridx, :], in_=psum[:, b, :]), reads=[r_bank[b]], writes=[r_ktok[ridx]])
            for half in range(2):
                b = next_bank()
                for k in range(8):
                    S.op("pe", lambda e, k=k: e.matmul(psum[:, b, :], lhsT=hTt[:, k, col0:col0 + 128], rhs=wslot[si_v[half]][:, k, :],
                                                      start=(k == 0), stop=(k == 7)),
                         reads=[r_hT, r_wslot[si_v[half]]], writes=[r_bank[b]], inc=(k == 7))
                S.op("dve", lambda e: e.tensor_tensor(out=vaug[:, ridx, 2 * half:2 * half + 2, 0:256],
                                                      in0=psum[:, b, :].rearrange("p (h v) -> p h v", h=2),
                                                      in1=browbc[:, 512 + half * 512:512 + (half + 1) * 512].rearrange("p (h v) -> p h v", h=2),
                                                      op=ALU.add),
                     reads=[r_bank[b], r_par], writes=[r_vaug[ridx]])
            for k in range(8):
                S.op("pe", lambda e, k=k: e.matmul(psum[:, gbank, ridx * 8:ridx * 8 + 8], lhsT=hTt[:, k, col0:col0 + 128], rhs=wgate[:, k, :],
                                                  start=(k == 0), stop=(k == 7)),
                     reads=[r_hT, r_wgate], writes=[r_bank[gbank]], inc=(k == 7))

        def state_update(ridx, gi0, ktok, r_ktok, vw, r_vw, sold, rg, pbank):
            for h in range(4):
                pb = pbank[h % 2]
                S.op("pe", lambda e: e.matmul(psum[:, pb, 0:257], lhsT=ktok[:, ridx, h * 128:(h + 1) * 128], rhs=vw[:, h, :],
                                              start=True, stop=True),
                     reads=[r_ktok[ridx], r_vw], writes=[r_bank[pb]])
                S.op("dve", lambda e: e.scalar_tensor_tensor(out=C32[:, h, :], in0=C32[:, h, :], scalar=sold[:, gi0 + h:gi0 + h + 1],
                                                             in1=psum[:, pb, 0:257], op0=ALU.mult, op1=ALU.add),
                     reads=[r_C32[h], rg, r_bank[pb]], writes=[r_C32[h]])

        NPF = 15
        with ExitStack() as ps_:
            ktok_p = sb("ktok_p", [128, NPF, 512], BF16, ps_)
            vaug_p = sb("vaug_p", [128, NPF, 4, 257], BF16, ps_)
            graw_p = sb("graw_p", [128, NPF, 8], F32, ps_)
            hTp = [sb(f"hTp{i}", [128, 8, 128], BF16, ps_) for i in range(2)]
            r_ktok_p = RL("ktok_p", NPF)
            r_vaug_p = RL("vaug_p", NPF)
            r_hTp = RL("hTp", 2)
            r_graw_p = Res("graw_p")
            S.op("dve", lambda e: e.memset(vaug_p[:, :, :, 256:257], 1.0), writes=r_vaug_p)
            si_k = load_w("w_in", O_K)
            si_v = [load_w("w_in", O_V + i * 512) for i in range(2)]
            gbank = 7
            state["resv"] = {gbank, 5, 6}
            def pfront(c):
                xi = load_x_tile(c * 128)
                return norm_front(xt[xi][:], [r_xt[xi]], G1bc)

            nf = {0: pfront(0)}
            if NPF > 1:
                nf[1] = pfront(1)
            norm_back(nf[0], sh1row, hTp[0], 0, r_hTp[0])
            for c in range(NPF):
                if c + 2 < NPF:
                    nf[c + 2] = pfront(c + 2)
                if c + 1 < NPF:
                    norm_back(nf[c + 1], sh1row, hTp[(c + 1) % 2], 0, r_hTp[(c + 1) % 2])
                j = c % 2
                kv_gate_proj(c, hTp[j], r_hTp[j], 0, si_k, si_v, ktok_p, r_ktok_p, vaug_p, r_vaug_p, gbank, c)
            free_w(si_k, *si_v)
            S.op("dve", lambda e: e.tensor_tensor(out=graw_p[:], in0=psum[:, gbank, 0:NPF * 8].rearrange("p (c g) -> p c g", g=8),
                                                  in1=bgatebc[:].rearrange("p (o g) -> p o g", o=1).to_broadcast([128, NPF, 8]), op=ALU.add),
                 reads=[r_bank[gbank], r_par], writes=[r_graw_p])
            state["resv"] = {4, 5, 6, 7}
            gm = {}
            interleave(gate_math(ps_, graw_p, r_graw_p, NPF, False, False, (4, 7), gm),
                       [(lambda n=n: ada_blocks([n])) for n in range(4, 12)])
            ada_part2()
            wG, sold, dmin, rg = gm["wG"], gm["sold"], gm["dmin"], gm["rg"]
            def prescale_p(c):
                for h in range(4):
                    S.op("act", lambda e, h=h: e.activation(out=vaug_p[:, c, h, :], in_=vaug_p[:, c, h, :], func=AF.Identity,
                                                            scale=wG[:, c * 4 + h:c * 4 + h + 1]),
                         reads=[r_vaug_p[c], rg], writes=[r_vaug_p[c]])

            prescale_p(0)
            for c in range(NPF):
                if c + 1 < NPF:
                    prescale_p(c + 1)
                state_update(c, c * 4, ktok_p, r_ktok_p, vaug_p[:, c], r_vaug_p[c], sold, rg, (5, 6))
            state["resv"] = set()
            S.barrier()
        ada.close()

        prefront = {}

        def run_stage(ci0, nch, first, next_ci0=None):
            if plan is None:
                marks.append(len(rec))
            else:
                W["stage"] += 1
                W["fence"] = marks[W["stage"]] if W["stage"] < len(marks) else len(plan)
            if not first:
                state["w_gate"] = [r_y]
            NT = nch * 128
            tbs = []
            t = 0
            if first:
                tbs.append((0, 128))
                t = 128
            while t < NT:
                n = min(512, NT - t)
                tbs.append((t, n))
                t += n
            x_row0 = ci0 * 128
            with ExitStack() as st:
                x1 = sb("x1", [128, nch, D], F32, st)
                r_x1 = RL("x1_", nch)
                hT = sb("hT", [128, 8, NT], BF16, st)
                sgoT = sb("sgoT", [128, 8, NT], BF16, st)
                hmT = sgoT
                r_hT = RL("hT", nch)
                r_hmT = RL("hmT", nch)

                def blk(rl, t0, n):
                    return rl[t0 // 128:(t0 + n) // 128]

                def sfront(c):
                    xi = load_x_tile(x_row0 + c * 128)
                    return norm_front(xt[xi][:], [r_xt[xi]], G1bc)

                nf = prefront.pop("nf", None) or {0: sfront(0)}
                for c in range(nch):
                    if c + 1 < nch and (c + 1) not in nf:
                        nf[c + 1] = sfront(c + 1)
                    norm_back(nf[c], sh1row, hT, c * 128, r_hT[c])

                with ExitStack() as ms:
                    qT = sb("qT", [128, 4, NT], BF16, ms)
                    kT = sb("kT", [128, 4, NT], BF16, ms)
                    ktok = sb("ktok", [128, nch, 512], BF16, ms)
                    vaug = sb("vaug", [128, nch, 4, 257], BF16, ms)
                    graw = sb("graw", [128, nch, 8], F32, ms)
                    Sm = [sb(f"Sm{i}", [128, 4, 128], BF16, ms) for i in range(3)]
                    hm = [sb(f"hm{i}", [128, D], BF16, ms) for i in range(3)]
                    r_qT = RL("qT", nch)
                    r_kT = RL("kT", nch)
                    r_ktok = RL("ktok", nch)
                    r_vaug = RL("vaug", nch)
                    r_sgoT = r_hmT
                    r_graw = Res("graw")
                    r_Sm = RL("Sm", 3)
                    r_hm = RL("hm", 3)
                    S.op("dve", lambda e: e.memset(vaug[:, :, :, 256:257], 1.0), writes=r_vaug)
                    si_q = load_w("w_in", O_Q)
                    si_k = load_w("w_in", O_K)
                    for h in range(4):
                        def ev_q(b, t0, n, h=h):
                            S.op("act", lambda e: e.activation(out=qT[:, h, t0:t0 + n], in_=psum[:, b, 0:n], func=AF.Identity,
                                                               bias=bqs[:, h:h + 1], scale=QS),
                                 reads=[r_bank[b], r_par], writes=blk(r_qT, t0, n))
                        fm_proj(si_q, h * 128, hT, r_hT, tbs, ev_q)
                    free_w(si_q)
                    for h in range(4):
                        def ev_k(b, t0, n, h=h):
                            S.op("act", lambda e: e.activation(out=kT[:, h, t0:t0 + n], in_=psum[:, b, 0:n], func=AF.Identity,
                                                               bias=bcol[:, 28 + h:29 + h]),
                                 reads=[r_bank[b], r_par], writes=blk(r_kT, t0, n))
                        fm_proj(si_k, h * 128, hT, r_hT, tbs, ev_k)
                    free_w(si_k)
                    si_v = [load_w("w_in", O_V + i * 512) for i in range(2)]
                    gbank = 7
                    state["resv"] = {gbank}
                    for c in range(nch):
                        kv_gate_proj(c, hT, r_hT[c], c * 128, si_k, si_v, ktok, r_ktok, vaug, r_vaug, gbank, c, kT_src=(kT, r_kT[c]))
                    free_w(*si_v)
                    S.op("dve", lambda e: e.tensor_tensor(out=graw[:], in0=psum[:, gbank, 0:nch * 8].rearrange("p (c g) -> p c g", g=8),
                                                          in1=bgatebc[:].rearrange("p (o g) -> p o g", o=1).to_broadcast([128, nch, 8]), op=ALU.add),
                         reads=[r_bank[gbank], r_par], writes=[r_graw])
                    state["resv"] = {6, 7}
                    units = []
                    for i in range(2):
                        for jj in range(4):
                            def unit(i=i, jj=jj):
                                if jj == 0:
                                    state["si_o"] = load_w("w_in", O_O + i * 512)
                                j = i * 4 + jj
                                def ev_o(b, t0, n, j=j):
                                    S.op("act", lambda e: e.activation(out=sgoT[:, j, t0:t0 + n], in_=psum[:, b, 0:n], func=AF.Sigmoid,
                                                                       bias=bcol[:, 48 + j:49 + j]),
                                         reads=[r_bank[b], r_par], writes=blk(r_sgoT, t0, n))
                                fm_proj(state["si_o"], jj * 128, hT, r_hT, tbs, ev_o)
                                if jj == 3:
                                    free_w(state["si_o"])
                            units.append(unit)
                    gm = {}
                    interleave(gate_math(ms, graw, r_graw, nch, first, True, (6, 7), gm), units)
                    wG, sold, dmin, rg = gm["wG"], gm["sold"], gm["dmin"], gm["rg"]
                    state["resv"] = set()
                    S.barrier()
                    ng_ = nch * 4
                    def prescale(c):
                        for h in range(4):
                            S.op("act", lambda e, h=h: e.activation(out=vaug[:, c, h, :], in_=vaug[:, c, h, :], func=AF.Identity,
                                                                    scale=wG[:, c * 4 + h:c * 4 + h + 1]),
                                 reads=[r_vaug[c], rg], writes=[r_vaug[c]])

                    prescale(0)
                    r_X = RL("X", 4)
                    r_STb = Res("STb")
                    Xs = [sb(f"Xs{i}", [128, 4, 257], F32, ms) for i in range(3)]
                    den4 = [sb(f"den4{i}", [128, 4], F32, ms) for i in range(3)]
                    ssq4 = [sb(f"ssq4{i}", [128, 4], F32, ms) for i in range(3)]
                    fac4 = [sb(f"fac4{i}", [128, 4], F32, ms) for i in range(3)]
                    r_Xs = RL("Xs", 3)
                    r_d4 = RL("den4", 3)
                    r_s4 = RL("ssq4", 3)
                    r_f4 = RL("fac4", 3)
                    def st_A(c):
                        j = c % 3
                        cs = slice(c * 128, (c + 1) * 128)
                        for h in range(4):
                            S.op("pe", lambda e, h=h: e.matmul(psum[:, 0, h * 128:(h + 1) * 128], lhsT=kT[:, h, cs], rhs=qT[:, h, cs],
                                                               start=True, stop=True),
                                 reads=[r_kT[c], r_qT[c]], writes=[r_STb], inc=(h == 3))
                        S.op("dve", lambda e: e.tensor_tensor(out=Sm[j][:], in0=psum[:, 0, :].rearrange("p (h t) -> p h t", h=4),
                                                              in1=maskf[:].rearrange("p (o t) -> p o t", o=1).to_broadcast([128, 4, 128]), op=ALU.mult),
                             reads=[r_STb, r_const], writes=[r_Sm[j]])
                        for h in range(4):
                            S.op("act", lambda e, h=h: e.activation(out=Cb[:, h, :], in_=C32[:, h, :], func=AF.Identity,
                                                                    scale=sold[:, c * 4 + h:c * 4 + h + 1]),
                                 reads=[r_C32[h], rg], writes=[r_Cb[h]])

                    def st_B(c):
                        j = c % 3
                        cs = slice(c * 128, (c + 1) * 128)
                        for h in range(4):
                            S.op("pe", lambda e, h=h: e.matmul(psum[:, 1 + h, 0:257], lhsT=Sm[j][:, h, :], rhs=vaug[:, c, h, :], start=True, stop=False),
                                 reads=[r_Sm[j], r_vaug[c]], writes=[r_X[h]], inc=False)
                            S.op("pe", lambda e, h=h: e.matmul(psum[:, 1 + h, 0:257], lhsT=qT[:, h, cs], rhs=Cb[:, h, :], start=False, stop=True),
                                 reads=[r_qT[c], r_Cb[h]], writes=[r_X[h]], inc=(h == 3))
                        S.op("act", lambda e: e.activation(out=Xs[j][:], in_=psum[:, 1:5, 0:257], func=AF.Copy), reads=r_X, writes=[r_Xs[j]])
                        state_update(c, c * 4, ktok, r_ktok, vaug[:, c], r_vaug[c], sold, rg, (5, 6))

                    def st_C1(c):
                        j = c % 3
                        S.op("dve", lambda e: e.scalar_tensor_tensor(out=den4[j][:].rearrange("p (h o) -> p h o", o=1), in0=Xs[j][:, :, 256:257],
                                                                     scalar=-1.0, in1=Xs[j][:, :, 256:257], op0=ALU.mult, op1=ALU.max),
                             reads=[r_Xs[j]], writes=[r_d4[j]])
                        S.op("dve", lambda e: e.tensor_tensor(out=den4[j][:], in0=den4[j][:], in1=dmin[:, c * 4:c * 4 + 4], op=ALU.max),
                             reads=[r_d4[j], rg], writes=[r_d4[j]])
                        S.op("dve", lambda e: e.reciprocal(out=den4[j][:], in_=den4[j][:]), reads=[r_d4[j]], writes=[r_d4[j]])
                        for h in range(4):
                            S.op("act", lambda e, h=h: e.activation(out=sqj[:, h * 256:(h + 1) * 256], in_=Xs[j][:, h, 0:256], func=AF.Square,
                                                                    scale=den4[j][:, h:h + 1], accum_out=ssq4[j][:, h:h + 1]),
                                 reads=[r_Xs[j], r_d4[j]], writes=[r_sqj, r_s4[j]])
                        S.op("act", lambda e: e.activation(out=ssq4[j][:], in_=ssq4[j][:], func=AF.Sqrt, scale=1.0 / 256, bias=EPS),
                             reads=[r_s4[j]], writes=[r_s4[j]])

                    def st_C2(c):
                        j = c % 3
                        S.op("dve", lambda e: e.reciprocal(out=fac4[j][:], in_=ssq4[j][:]), reads=[r_s4[j]], writes=[r_f4[j]])
                        S.op("dve", lambda e: e.tensor_tensor(out=fac4[j][:], in0=fac4[j][:], in1=den4[j][:], op=ALU.mult),
                             reads=[r_f4[j], r_d4[j]], writes=[r_f4[j]])
                        for h in range(4):
                            S.op("dve", lambda e, h=h: e.scalar_tensor_tensor(out=hm[j][:, h * 256:(h + 1) * 256], in0=Xs[j][:, h, 0:256],
                                                                              scalar=fac4[j][:, h:h + 1], in1=gheadbc[:, h * 256:(h + 1) * 256],
                                                                              op0=ALU.mult, op1=ALU.mult),
                                 reads=[r_Xs[j], r_f4[j], r_par], writes=[r_hm[j]])

                    def st_D(c):
                        j = c % 3
                        cs = slice(c * 128, (c + 1) * 128)
                        for half in range(2):
                            tb_ = 7
                            for kk in range(4):
                                k = half * 4 + kk
                                S.op("pe", lambda e, k=k, kk=kk: e.matmul(psum[:, tb_, kk * 128:(kk + 1) * 128], lhsT=hm[j][:, k * 128:(k + 1) * 128],
                                                                          rhs=identb[:], start=True, stop=True),
                                     reads=[r_hm[j], r_const], writes=[r_bank[tb_]], inc=(kk == 3))
                            S.op("dve", lambda e: e.tensor_tensor(out=hmT[:, half * 4:half * 4 + 4, cs],
                                                                  in0=psum[:, tb_, :].rearrange("p (k t) -> p k t", k=4),
                                                                  in1=sgoT[:, half * 4:half * 4 + 4, cs], op=ALU.mult),
                                 reads=[r_bank[tb_], r_hmT[c]], writes=[r_hmT[c]])

                    for i in range(nch + 2):
                        if i + 1 < nch:
                            prescale(i + 1)
                        if i < nch:
                            st_A(i)
                        if 0 <= i - 2 < nch:
                            st_C2(i - 2)
                        if 0 <= i - 1 < nch:
                            st_C1(i - 1)
                        if i < nch:
                            st_B(i)
                        if 0 <= i - 2 < nch:
                            st_D(i - 2)
                    S.barrier()
                with ExitStack() as cs_:
                    ycT = sb("ycT", [128, 8, NT], BF16, cs_)
                    zT = sb("zT", [128, 8, NT], BF16, cs_)
                    xs2 = [sb(f"xs{i}", [128, NT], F32, cs_) for i in range(2)]
                    u2 = [sb(f"u{i}", [128, NT + 2], F32, cs_) for i in range(2)]
                    t32 = [sb(f"t3{i}", [128, NT], F32, cs_) for i in range(2)]
                    sg2 = [sb(f"sg{i}", [128, NT], BF16, cs_) for i in range(2)]
                    zc2 = [sb(f"zc{i}", [128, NT], F32, cs_) for i in range(2)]
                    r_ycT = Res("ycT")
                    r_zT = RL("zT", nch)
                    r_xs2, r_u2, r_t32, r_sg2, r_zc2 = RL("xs", 2), RL("u", 2), RL("t3", 2), RL("sg", 2), RL("zc", 2)
                    cv = {}

                    def conv_front(j):
                        xs, u, t3 = xs2[j % 2], u2[j % 2], t32[j % 2]
                        r_xs, r_u, r_t3 = r_xs2[j % 2], r_u2[j % 2], r_t32[j % 2]
                        if j % 4 == 0:
                            cv["x"] = load_w("w_in", O_XIN + (j // 4) * 512)
                            cv["c"] = load_w("w_in", O_CG + (j // 4) * 512)
                            cv[("b", j // 4)] = load_w("w_in", O_BG + (j // 4) * 512)
                        si_x, si_c = cv["x"], cv["c"]
                        wc = (j % 4) * 128
                        def ev_x(b, t0, n, j=j):
                            S.op("act", lambda e: e.activation(out=xs[:, t0:t0 + n], in_=psum[:, b, 0:n], func=AF.Identity, bias=bcol[:, j:j + 1]),
                                 reads=[r_bank[b], r_par], writes=[r_xs])
                        fm_proj(si_x, wc, hT, r_hT, tbs, ev_x)
                        S.op("dve", lambda e, j=j: e.tensor_copy(out=u[:, 0:2], in_=uhalo[:, j, :]), reads=[r_uh], writes=[r_u])
                        def ev_c(b, t0, n, j=j):
                            S.op("dve", lambda e: e.scalar_tensor_tensor(out=u[:, 2 + t0:2 + t0 + n], in0=psum[:, b, 0:n], scalar=bcol[:, 16 + j:17 + j],
                                                                         in1=xs[:, t0:t0 + n], op0=ALU.add, op1=ALU.mult),
                                 reads=[r_bank[b], r_par, r_xs], writes=[r_u])
                        fm_proj(si_c, wc, hT, r_hT, tbs, ev_c)
                        if first:
                            S.op("dve", lambda e: e.tensor_scalar(out=u[:, 128:130], in0=u[:, 128:130], scalar1=flag_t[:, 0:1], scalar2=None, op0=ALU.mult),
                                 reads=[r_u, r_par], writes=[r_u])
                        S.op("dve", lambda e, j=j: e.tensor_copy(out=uhalo[:, j, :], in_=u[:, NT:NT + 2]), reads=[r_u], writes=[r_uh])
                        S.op("act", lambda e, j=j: e.activation(out=t3[:], in_=u[:, 2:NT + 2], func=AF.Identity, scale=cwcol[:, j * 3 + 2:j * 3 + 3]),
                             reads=[r_u, r_par], writes=[r_t3])
                        S.op("dve", lambda e, j=j: e.scalar_tensor_tensor(out=t3[:], in0=u[:, 1:NT + 1], scalar=cwcol[:, j * 3 + 1:j * 3 + 2], in1=t3[:],
                                                                          op0=ALU.mult, op1=ALU.add), reads=[r_u, r_par, r_t3], writes=[r_t3])
                        S.op("dve", lambda e, j=j: e.scalar_tensor_tensor(out=t3[:], in0=u[:, 0:NT], scalar=cwcol[:, j * 3:j * 3 + 1], in1=t3[:],
                                                                          op0=ALU.mult, op1=ALU.add), reads=[r_u, r_par, r_t3], writes=[r_t3])
                        if j % 4 == 3:
                            free_w(si_x, si_c)

                    def conv_back(j):
                        t3, r_t3 = t32[j % 2], r_t32[j % 2]
                        si_b = cv[("b", j // 4)]
                        def ev_b(b, t0, n, j=j):
                            S.op("dve", lambda e: e.scalar_tensor_tensor(out=ycT[:, j, t0:t0 + n], in0=psum[:, b, 0:n], scalar=bcol[:, 8 + j:9 + j],
                                                                         in1=t3[:, t0:t0 + n], op0=ALU.add, op1=ALU.mult),
                                 reads=[r_bank[b], r_par, r_t3], writes=[r_ycT])
                        fm_proj(si_b, (j % 4) * 128, hT, r_hT, tbs, ev_b)
                        if j % 4 == 3:
                            free_w(si_b)

                    conv_front(0)
                    for j in range(1, 8):
                        conv_front(j)
                        conv_back(j - 1)
                    conv_back(7)
                    r_hmT_all = r_hmT
                    for j in range(8):
                        sg, zc = sg2[j % 2], zc2[j % 2]
                        r_sg, r_zc = r_sg2[j % 2], r_zc2[j % 2]
                        if j % 4 == 0:
                            si_gc = load_w("w_in", O_GC + (j // 4) * 512)
                            si_pc = load_w("w_proj_conv", (j // 4) * 512)
                            si_gm = load_w("w_in", O_GM + (j // 4) * 512)
                            si_pm = load_w("w_proj_mlstm", (j // 4) * 512)
                        wc = (j % 4) * 128
                        def ev_g(b, t0, n, col):
                            S.op("act", lambda e: e.activation(out=sg[:, t0:t0 + n], in_=psum[:, b, 0:n], func=AF.Sigmoid, bias=bcol[:, col:col + 1]),
                                 reads=[r_bank[b], r_par], writes=[r_sg])
                        fm_proj(si_gc, wc, hT, r_hT, tbs, lambda b, t0, n, j=j: ev_g(b, t0, n, 32 + j))
                        def ev_pc(b, t0, n):
                            S.op("dve", lambda e: e.tensor_tensor(out=zc[:, t0:t0 + n], in0=psum[:, b, 0:n], in1=sg[:, t0:t0 + n], op=ALU.mult),
                                 reads=[r_bank[b], r_sg], writes=[r_zc])
                        fm_proj(si_pc, wc, ycT, [r_ycT], tbs, ev_pc)
                        fm_proj(si_gm, wc, hT, r_hT, tbs, lambda b, t0, n, j=j: ev_g(b, t0, n, 40 + j))
                        def ev_pm(b, t0, n, j=j):
                            S.op("dve", lambda e: e.tensor_tensor(out=sg[:, t0:t0 + n], in0=psum[:, b, 0:n], in1=sg[:, t0:t0 + n], op=ALU.mult),
                                 reads=[r_bank[b], r_sg], writes=[r_sg])
                            S.op("dve", lambda e: e.tensor_tensor(out=zT[:, j, t0:t0 + n], in0=sg[:, t0:t0 + n], in1=zc[:, t0:t0 + n], op=ALU.add),
                                 reads=[r_sg, r_zc], writes=blk(r_zT, t0, n))
                        fm_proj(si_pm, wc, hmT, r_hmT_all, tbs, ev_pm)
                        if j % 4 == 3:
                            free_w(si_gc, si_pc, si_gm, si_pm)
                    si_o2 = [load_w("w_out", i * 512) for i in range(2)]
                    for i in range(2):
                        S.op("dve", lambda e, i=i: e.tensor_tensor(out=wslot[si_o2[i]][:], in0=wslot[si_o2[i]][:],
                                                                   in1=gt1bc[:, i * 512:(i + 1) * 512].rearrange("p (o n) -> p o n", o=1).to_broadcast([128, 8, 512]),
                                                                   op=ALU.mult),
                             reads=[r_wslot[si_o2[i]], r_par2], writes=[r_wslot[si_o2[i]]])
                    nf2 = {}
                    for c in range(nch):
                        xi = load_x_tile(x_row0 + c * 128)
                        for i in range(2):
                            b = next_bank()
                            for k in range(8):
                                S.op("pe", lambda e, k=k: e.matmul(psum[:, b, :], lhsT=zT[:, k, c * 128:(c + 1) * 128], rhs=wslot[si_o2[i]][:, k, :],
                                                                  start=(k == 0), stop=(k == 7)),
                                     reads=[r_zT[c], r_wslot[si_o2[i]]], writes=[r_bank[b]], inc=(k == 7))
                            S.op("dve", lambda e: e.tensor_tensor(out=x1[:, c, i * 512:(i + 1) * 512], in0=psum[:, b, :],
                                                                  in1=xt[xi][:, i * 512:(i + 1) * 512], op=ALU.add),
                                 reads=[r_bank[b], r_xt[xi]], writes=[r_x1[c]])
                        nf2[c] = norm_front(x1[:, c, :], [r_x1[c]], G2bc)
                        if c >= 1:
                            norm_back(nf2[c - 1], sh2row, hT, (c - 1) * 128, r_hT[c - 1])
                    norm_back(nf2[nch - 1], sh2row, hT, (nch - 1) * 128, r_hT[nch - 1])
                    free_w(*si_o2)
                    S.barrier()
                with ExitStack() as fs:
                    actT2 = [sb(f"actT{i}", [128, 4, NT], BF16, fs) for i in range(2)]
                    a_s2 = [sb(f"a_s{i}", [128, NT + 2], F32, fs) for i in range(2)]
                    t3f2 = [sb(f"t3f{i}", [128, NT], F32, fs) for i in range(2)]
                    sl2 = [sb(f"sl{i}", [128, NT], F32, fs) for i in range(2)]
                    r_actT2 = [RL("actTa", nch), RL("actTb", nch)]
                    r_as2, r_t3f2, r_sl2 = RL("a_s", 2), RL("t3f", 2), RL("sl", 2)
                    groups = [(g * 4, min(4, NFF - g * 4)) for g in range((NFF + 3) // 4)]

                    def final_norm_store(c):
                        ci = ci0 + c
                        if ci < 16:
                            return
                        rs, r_rs = rms_rstd(x1[:, c, :], [r_x1[c]], D)
                        S.op("dve", lambda e: e.scalar_tensor_tensor(out=x1[:, c, :], in0=x1[:, c, :], scalar=rs, in1=gfinbc[:], op0=ALU.mult, op1=ALU.mult),
                             reads=[r_x1[c], r_rs, r_par], writes=[r_x1[c]])
                        S.dma("sp", y_d[(ci - 16) * 128:(ci - 15) * 128, :], x1[:, c, :], reads=[r_x1[c]], writes=[r_y])

                    def emit_down(gi, gn, si_d, wd, last=False):
                        actT, r_actT = actT2[gi % 2], r_actT2[gi % 2]
                        for c in range(nch):
                            for i in range(2):
                                b = next_bank()
                                for jj in range(gn):
                                    S.op("pe", lambda e, jj=jj: e.matmul(psum[:, b, :], lhsT=actT[:, jj, c * 128:(c + 1) * 128], rhs=wd[:, jj, i * 512:(i + 1) * 512],
                                                                        start=(jj == 0), stop=(jj == gn - 1)),
                                         reads=[r_actT[c], r_wslot[si_d]], writes=[r_bank[b]], inc=(jj == gn - 1))
                                S.op("dve", lambda e: e.tensor_tensor(out=x1[:, c, i * 512:(i + 1) * 512], in0=psum[:, b, :],
                                                                      in1=x1[:, c, i * 512:(i + 1) * 512], op=ALU.add),
                                     reads=[r_bank[b], r_x1[c]], writes=[r_x1[c]])
                            if last and c >= 1:
                                final_norm_store(c - 1)
                        if last:
                            final_norm_store(nch - 1)
                        free_w(si_d)

                    pending = None
                    pend_gt = None

                    def up_a(j, jj, si_a):
                        a_s, t3, sl = a_s2[j % 2], t3f2[j % 2], sl2[j % 2]
                        r_as, r_t3, r_sl = r_as2[j % 2], r_t3f2[j % 2], r_sl2[j % 2]
                        S.op("dve", lambda e: e.tensor_copy(out=a_s[:, 0:2], in_=ahalo[:, j, :]), reads=[r_ah], writes=[r_as])
                        def ev_a(b, t0, n):
                            S.op("act", lambda e: e.activation(out=a_s[:, 2 + t0:2 + t0 + n], in_=psum[:, b, 0:n], func=AF.Copy),
                                 reads=[r_bank[b]], writes=[r_as])
                        fm_proj(si_a, jj * 128, hT, r_hT, tbs, ev_a)
                        if first:
                            S.op("dve", lambda e: e.tensor_scalar(out=a_s[:, 128:130], in0=a_s[:, 128:130], scalar1=flag_t[:, 0:1], scalar2=None,
                                                                  op0=ALU.mult), reads=[r_as, r_par], writes=[r_as])
                        S.op("dve", lambda e: e.tensor_copy(out=ahalo[:, j, :], in_=a_s[:, NT:NT + 2]), reads=[r_as], writes=[r_ah])
                        S.op("act", lambda e: e.activation(out=t3[:], in_=a_s[:, 2:NT + 2], func=AF.Identity,
                                                           scale=cwfcol[:, j * 3 + 2:j * 3 + 3]), reads=[r_as, r_par], writes=[r_t3])
                        S.op("dve", lambda e: e.scalar_tensor_tensor(out=t3[:], in0=a_s[:, 1:NT + 1], scalar=cwfcol[:, j * 3 + 1:j * 3 + 2], in1=t3[:],
                                                                     op0=ALU.mult, op1=ALU.add), reads=[r_as, r_par, r_t3], writes=[r_t3])
                        S.op("dve", lambda e: e.scalar_tensor_tensor(out=t3[:], in0=a_s[:, 0:NT], scalar=cwfcol[:, j * 3:j * 3 + 1], in1=t3[:],
                                                                     op0=ALU.mult, op1=ALU.add), reads=[r_as, r_par, r_t3], writes=[r_t3])
                        S.op("act", lambda e: e.activation(out=sl[:], in_=t3[:], func=AF.Silu), reads=[r_t3], writes=[r_sl])

                    def up_g(j, jj, si_g, actT, r_actT, release):
                        sl, r_sl = sl2[j % 2], r_sl2[j % 2]
                        def ev_gt(b, t0, n):
                            S.op("dve", lambda e: e.tensor_tensor(out=actT[:, jj, t0:t0 + n], in0=psum[:, b, 0:n], in1=sl[:, t0:t0 + n], op=ALU.mult),
                                 reads=[r_bank[b], r_sl], writes=blk(r_actT, t0, n))
                        fm_proj(si_g, jj * 128, hT, r_hT, tbs, ev_gt)
                        if release:
                            free_w(si_g)

                    for gi, (j0, gn) in enumerate(groups):
                        actT, r_actT = actT2[gi % 2], r_actT2[gi % 2]
                        ncol = gn * 128
                        si_a = load_w("w_up", j0 * 128, 8, ncol)
                        si_g = load_w("w_up", DFF + j0 * 128, 8, ncol)
                        si_d = load_w("w_down", 0, gn, 1024, r0=j0 * 128)
                        wd = slot_view(si_d, gn, 1024)
                        S.op("dve", lambda e: e.tensor_tensor(out=wd, in0=wd,
                                                              in1=gt2bc[:].rearrange("p (o n) -> p o n", o=1).to_broadcast([128, gn, 1024]), op=ALU.mult),
                             reads=[r_wslot[si_d], r_par2], writes=[r_wslot[si_d]])
                        for jj in range(gn):
                            j = j0 + jj
                            up_a(j, jj, si_a)
                            if pend_gt is not None:
                                up_g(*pend_gt)
                            pend_gt = (j, jj, si_g, actT, r_actT, jj == gn - 1)
                            if pending is not None and jj == min(2, gn - 1):
                                emit_down(*pending)
                                pending = None
                        free_w(si_a)
                        pending = (gi, gn, si_d, wd)
                    up_g(*pend_gt)
                    if next_ci0 is not None:
                        pf = {}
                        for c in range(2):
                            xi = load_x_tile((next_ci0 + c) * 128)
                            pf[c] = norm_front(xt[xi][:], [r_xt[xi]], G1bc)
                        prefront["nf"] = pf
                    emit_down(*pending, last=True)
                    S.barrier()

        run_stage(15, 5, True, next_ci0=20)
        run_stage(20, 6, False, next_ci0=26)
        run_stage(26, 6, False)
        S.barrier()
        build_nc.stats = (S.ninstr, dict(S.cnt))
        if plan is not None:
            assert W["use"] == len(plan), (W["use"], len(plan))
    return nc


def _col(v, nchunk):
    return np.ascontiguousarray(np.asarray(v, np.float32).reshape(nchunk, 128).T)


def make_in_maps(x, c, w_ada, b_ada, g_norm_mix, w_in, b_in, conv_mix_w, mlstm_head_g, w_proj_conv, w_proj_mlstm,
                 w_out, g_norm_ffn, w_up, conv_ffn_w, w_down, g_final):
    f = lambda a: np.ascontiguousarray(np.asarray(a, dtype=np.float32))
    x = f(x)
    c = f(c)
    b = f(b_in)[0]
    b_col = np.concatenate([_col(b[O_XIN:O_XIN + 1024], 8), _col(b[O_BG:O_BG + 1024], 8), _col(b[O_CG:O_CG + 1024], 8),
                            _col(b[O_Q:O_Q + 512], 4), _col(b[O_K:O_K + 512], 4), _col(b[O_GC:O_GC + 1024], 8),
                            _col(b[O_GM:O_GM + 1024], 8), _col(b[O_O:O_O + 1024], 8)], axis=1)
    b_row = np.concatenate([b[O_K:O_K + 512], b[O_V:O_V + 1024]])[None, :]
    b_gate = b[O_IG:O_IG + 8][None, :]
    cw_col = f(conv_mix_w)[0].reshape(3, 8, 128).transpose(2, 1, 0).reshape(128, 24)
    cwf_col = f(conv_ffn_w)[0].reshape(3, NFF, 128).transpose(2, 1, 0).reshape(128, 3 * NFF)
    shared = {
        "w_ada": f(w_ada)[0], "b_ada": f(b_ada)[0][None, :], "g_norm_mix": f(g_norm_mix)[0][None, :],
        "w_in": f(w_in)[0], "b_col": np.ascontiguousarray(b_col), "b_row": np.ascontiguousarray(b_row),
        "b_gate": np.ascontiguousarray(b_gate), "cw_col": np.ascontiguousarray(cw_col),
        "g_head": f(mlstm_head_g)[0].reshape(1, D), "w_proj_conv": f(w_proj_conv)[0], "w_proj_mlstm": f(w_proj_mlstm)[0],
        "w_out": f(w_out)[0], "g_norm_ffn": f(g_norm_ffn)[0][None, :], "w_up": f(w_up)[0],
        "cwf_col": np.ascontiguousarray(cwf_col), "w_down": f(w_down)[0], "g_final": f(g_final)[None, :],
    }
    in_maps = []
    for core in range(8):
        bi, hf = core // 2, core % 2
        if hf == 1:
            xin = x[bi]
        else:
            xin = np.concatenate([x[bi, 0:2048], x[bi, 0:2048]], axis=0)
        m = dict(shared)
        m["xin"] = np.ascontiguousarray(xin)
        m["ccol"] = _col(c[bi], 8)
        m["flag"] = np.full((128, 1), float(hf), np.float32)
        in_maps.append(m)
    return in_maps


_NC_CACHE = {}


def kernel(**inputs):
    if "nc" not in _NC_CACHE:
        _NC_CACHE["nc"] = build_nc()
    nc = _NC_CACHE["nc"]
    in_maps = make_in_maps(**inputs)
    res = run_bass_kernel_spmd(nc, in_maps, core_ids=list(range(8)))
    out = np.empty((4, 4096, D), np.float32)
    for core in range(8):
        bi, hf = core // 2, core % 2
        out[bi, hf * 2048:(hf + 1) * 2048] = res.results[core]["y"]
    return out
```

```python
import numpy as np
import concourse.bass as bass
import concourse.mybir as mybir
from concourse.bass_utils import run_bass_kernel_spmd
from contextlib import ExitStack

F32 = mybir.dt.float32
BF16 = mybir.dt.bfloat16
AF = mybir.ActivationFunctionType
ALU = mybir.AluOpType
AX = mybir.AxisListType

D = 1024
NIN = 8200
DFF = 2816
EPS = 1e-6
O_XIN, O_BG, O_CG, O_Q, O_K, O_V, O_O, O_IG, O_FG, O_GC, O_GM = 0, 1024, 2048, 3072, 3584, 4096, 5120, 6144, 6148, 6152, 7176
NSLOT = 6
NFF = 22


class Res:
    __slots__ = ("name", "w", "r")

    def __init__(self, name):
        self.name = name
        self.w = None
        self.r = {}


def RL(name, n):
    return [Res(f"{name}{i}") for i in range(n)]


class Sched:
    def __init__(self, nc, es):
        self.nc = nc
        self.engs = {"pe": nc.tensor, "act": nc.scalar, "dve": nc.vector, "pool": nc.gpsimd, "sp": nc.sync}
        self.sem = {k: es.enter_context(nc.semaphore("s_" + k)) for k in self.engs}
        self.cnt = {k: 0 for k in self.engs}
        self.pending = {k: False for k in self.engs}
        self.seen = {k: {} for k in self.engs}
        self.dq = {}
        for q, n in (("sp", 12), ("pool", 10)):
            self.dq[q] = {"sems": [es.enter_context(nc.semaphore(f"d_{q}{i}")) for i in range(n)],
                          "vals": [0] * n, "next": 0}
        self.semobj = dict(self.sem)
        for q in self.dq:
            for i, s in enumerate(self.dq[q]["sems"]):
                self.semobj[(q, i)] = s
        self.ninstr = 0

    def _deps(self, e, reads, writes):
        deps = {}

        def add(ev, same_ok):
            if ev is None:
                return
            k, v = ev
            if k == e and not same_ok:
                return
            if deps.get(k, 0) < v:
                deps[k] = v

        same_raw = e in ("act", "dve", "pool")
        for r in reads:
            add(r.w, same_raw)
        for w in writes:
            add(w.w, False)
            for k, v in w.r.items():
                add((k, v), False)
        return deps

    def _wait(self, e, deps):
        eng = self.engs[e]
        seen = self.seen[e]
        for k, v in deps.items():
            if seen.get(k, 0) >= v:
                continue
            eng.wait_ge(self.semobj[k], v)
            seen[k] = v
            self.ninstr += 1

    def _mark(self, ev, reads, writes):
        k, v = ev
        for r in reads:
            if r.r.get(k, 0) < v:
                r.r[k] = v
        for w in writes:
            w.w = ev
            w.r = {}

    def op(self, e, fn, reads=(), writes=(), inc=True):
        deps = self._deps(e, reads, writes)
        self._wait(e, deps)
        ins = fn(self.engs[e])
        self.ninstr += 1
        if inc:
            ins.then_inc(self.sem[e], 1)
            self.cnt[e] += 1
            self.pending[e] = False
            ev = (e, self.cnt[e])
        else:
            self.pending[e] = True
            ev = (e, self.cnt[e] + 1)
        self._mark(ev, reads, writes)
        return ins

    def dma(self, q, out, in_, reads=(), writes=()):
        dq = self.dq[q]
        i = dq["next"]
        dq["next"] = (i + 1) % len(dq["sems"])
        deps = self._deps(q, reads, writes)
        if dq["vals"][i] > 0:
            k = (q, i)
            if deps.get(k, 0) < dq["vals"][i]:
                deps[k] = dq["vals"][i]
        self._wait(q, deps)
        ins = self.engs[q].dma_start(out=out, in_=in_)
        ins.then_inc(dq["sems"][i], 16)
        dq["vals"][i] += 16
        self.ninstr += 1
        ev = ((q, i), dq["vals"][i])
        self._mark(ev, reads, writes)
        return ins

    def flush_pe(self):
        if self.pending["pe"]:
            self.nc.tensor.nop().then_inc(self.sem["pe"], 1)
            self.cnt["pe"] += 1
            self.pending["pe"] = False

    def barrier(self, engines=("pe", "act", "dve"), queues=("sp",)):
        self.flush_pe()
        deps = {k: self.cnt[k] for k in self.engs if self.cnt[k] > 0}
        for q in queues:
            for i, v in enumerate(self.dq[q]["vals"]):
                if v > 0:
                    deps[(q, i)] = v
        for e in engines:
            d = {k: v for k, v in deps.items() if k != e or e in ("act", "dve")}
            self._wait(e, d)


def build_nc(debug=False):
    rec = []
    marks = []
    _build(None, rec, marks)
    return _build(rec, None, marks)


def _build(plan, rec, marks):
    nc = bass.Bass("TRN2", target_bir_lowering=False)
    dram = lambda name, shape, kind="ExternalInput": nc.dram_tensor(name, list(shape), F32, kind=kind).ap()
    x_d = dram("xin", [4096, D])
    ccol_d = dram("ccol", [128, 8])
    flag_d = dram("flag", [128, 1])
    wada_d = dram("w_ada", [D, 6 * D])
    bada_d = dram("b_ada", [1, 6 * D])
    gmix_d = dram("g_norm_mix", [1, D])
    win_d = dram("w_in", [D, NIN])
    bcol_d = dram("b_col", [128, 56])
    brow_d = dram("b_row", [1, 512 + 1024])
    bgate_d = dram("b_gate", [1, 8])
    cw_d = dram("cw_col", [128, 24])
    ghead_d = dram("g_head", [1, D])
    wpc_d = dram("w_proj_conv", [D, D])
    wpm_d = dram("w_proj_mlstm", [D, D])
    wout_d = dram("w_out", [D, D])
    gffn_d = dram("g_norm_ffn", [1, D])
    wup_d = dram("w_up", [D, 2 * DFF])
    cwf_d = dram("cwf_col", [128, 3 * NFF])
    wdn_d = dram("w_down", [DFF, D])
    gfin_d = dram("g_final", [1, D])
    y_d = dram("y", [2048, D], kind="ExternalOutput")

    with ExitStack() as es:
        S = Sched(nc, es)
        uid = [0]

        def sb(name, shape, dt=F32, st=es):
            uid[0] += 1
            return st.enter_context(nc.sbuf_tensor(f"{name}_{uid[0]}", list(shape), dt))

        identb = sb("identb", [128, 128], BF16)
        identf = sb("identf", [128, 128])
        onesf = sb("onesf", [128, 128])
        onesb = sb("onesb", [1, 512], BF16)
        maskf = sb("maskf", [128, 128])
        flag_t = sb("flag_t", [128, 1])
        ccol = sb("ccol", [128, 8])
        scol = sb("scol", [128, 8], BF16)
        G1bc = sb("G1bc", [128, D])
        G2bc = sb("G2bc", [128, D])
        gt1bc = sb("gt1bc", [128, D], BF16)
        gt2bc = sb("gt2bc", [128, D], BF16)
        sh1row = sb("sh1col", [128, 8])
        sh2row = sb("sh2col", [128, 8])
        gheadbc = sb("gheadbc", [128, D])
        gfinbc = sb("gfinbc", [128, D])
        bcol = sb("bcol", [128, 56])
        bqs = sb("bqs", [128, 4])
        browbc = sb("browbc", [128, 1536], BF16)
        bgatebc = sb("bgatebc", [128, 8])
        cwcol = sb("cwcol", [128, 24])
        cwfcol = sb("cwfcol", [128, 3 * NFF])
        wgate = sb("wgate", [128, 8, 8], BF16)
        C32 = sb("C32", [128, 4, 257])
        Cb = sb("Cb", [128, 4, 257], BF16)
        mcarry = sb("mcarry", [1, 4])
        uhalo = sb("uhalo", [128, 8, 2])
        ahalo = sb("ahalo", [128, NFF, 2])
        wslot = [sb(f"wslot{i}", [128, 8, 512], BF16) for i in range(NSLOT)]
        xt = [sb(f"xt{i}", [128, D]) for i in range(2)]
        xnb = [sb(f"xnb{i}", [128, D], BF16) for i in range(2)]
        sqj = sb("sqj", [128, D], BF16)
        stat = sb("stat", [128, 64])
        psum = es.enter_context(nc.psum_tensor("psum", [128, 8, 512], F32))

        r_wslot = RL("wslot", NSLOT)
        r_xt = RL("xt", 2)
        r_xnb = RL("xnb", 2)
        r_bank = RL("bank", 8)
        r_const = Res("const")
        r_par = Res("par")
        r_par2 = Res("par2")
        r_mod = Res("modrow")
        r_sqj = Res("sqj")
        r_C32 = RL("C32_", 4)
        r_Cb = RL("Cb_", 4)
        r_mc = Res("mcarry")
        r_uh = Res("uhalo")
        r_ah = Res("ahalo")
        r_y = Res("ydram")
        r_in = Res("indram")
        r_wgate = Res("wgate")

        state = {"slot": 0, "bank": 0, "xt": 0, "xnb": 0, "stat": 0, "resv": set()}

        def next_bank():
            b = state["bank"]
            while b in state["resv"]:
                b = (b + 1) % 8
            state["bank"] = (b + 1) % 8
            return b

        def stat_col(n=1):
            c = state["stat"]
            if c + n > 64:
                c = 0
            state["stat"] = c + n
            return stat[:, c:c + n], r_stat[c]

        r_stat = RL("stat", 64)

        dr = {"w_ada": wada_d, "w_in": win_d, "w_proj_conv": wpc_d, "w_proj_mlstm": wpm_d, "w_out": wout_d, "w_up": wup_d, "w_down": wdn_d}
        W = {"issue": 0, "use": 0, "free": list(range(NSLOT)), "slot_of": {}, "fence": 0, "stage": 0}

        def slot_view(i, kc, ncol):
            if ncol <= 512:
                return wslot[i][:, 0:kc, 0:ncol]
            return wslot[i][:].rearrange("p k n -> p (k n)")[:, 0:kc * ncol].rearrange("p (k n) -> p k n", n=ncol)

        def issue_w(i, spec):
            name, r0, c0, kc, ncol = spec
            extra = state.pop("w_gate", [])
            S.dma("pool", slot_view(i, kc, ncol), dr[name][r0:r0 + kc * 128, c0:c0 + ncol].rearrange("(k p) n -> p k n", p=128),
                  reads=[r_in] + extra, writes=[r_wslot[i]])

        if plan is not None:
            W["fence"] = marks[0]

        def pump_w():
            while W["free"] and W["issue"] < min(len(plan), W["fence"]) and W["issue"] - W["use"] < 3:
                i = W["free"].pop(0)
                issue_w(i, plan[W["issue"]])
                W["slot_of"][W["issue"]] = i
                W["issue"] += 1

        def load_w(name, c0, kc=8, ncol=512, r0=0):
            spec = (name, r0, c0, kc, ncol)
            if plan is None:
                rec.append(spec)
                i = state["slot"]
                state["slot"] = (i + 1) % NSLOT
                issue_w(i, spec)
                return i
            pump_w()
            k = W["use"]
            assert plan[k] == spec, (k, plan[k], spec)
            assert k in W["slot_of"], f"weight tile {k} {spec} not issued: missing free_w()?"
            W["use"] += 1
            return W["slot_of"][k]

        def free_w(*slots):
            if plan is None:
                return
            for i in slots:
                assert i not in W["free"]
                W["free"].append(i)
            pump_w()

        S.op("pool", lambda e: e.memset(identf[:], 1.0), writes=[r_const])
        S.op("pool", lambda e: e.affine_select(out=identf[:], in_=identf[:], pattern=[[-1, 128]], compare_op=ALU.is_equal,
                                               fill=0.0, base=0, channel_multiplier=1), reads=[r_const], writes=[r_const])
        S.op("pool", lambda e: e.memset(maskf[:], 1.0), writes=[r_const])
        S.op("pool", lambda e: e.affine_select(out=maskf[:], in_=maskf[:], pattern=[[1, 128]], compare_op=ALU.is_ge,
                                               fill=0.0, base=0, channel_multiplier=-1), reads=[r_const], writes=[r_const])
        S.op("dve", lambda e: e.memset(onesf[:], 1.0), writes=[r_const])
        S.op("dve", lambda e: e.memset(onesb[:], 1.0), writes=[r_const])
        S.op("dve", lambda e: e.tensor_copy(out=identb[:], in_=identf[:]), reads=[r_const], writes=[r_const])
        S.op("dve", lambda e: e.memset(C32[:], 0.0), writes=r_C32)
        S.op("dve", lambda e: e.memset(mcarry[:], 0.0), writes=[r_mc])
        S.op("dve", lambda e: e.memset(uhalo[:], 0.0), writes=[r_uh])
        S.op("dve", lambda e: e.memset(ahalo[:], 0.0), writes=[r_ah])

        def pload(dst, src):
            S.dma("sp", dst, src, reads=[r_in], writes=[r_par])

        r_crit = Res("crit")
        ada = ExitStack()
        modrow = sb("modrow", [1, 6 * D], F32, ada)
        growA = sb("growA", [1, D], F32, ada)
        growB = sb("growB", [1, D], F32, ada)
        browst = sb("browst", [128, 1536], F32, ada)
        S.dma("sp", ccol[:], ccol_d, reads=[r_in], writes=[r_crit])
        S.dma("sp", modrow[:], bada_d, reads=[r_in], writes=[r_mod])
        S.dma("sp", growA[:], gmix_d, reads=[r_in], writes=[r_crit])
        pload(flag_t[:], flag_d)
        pload(growB[:], gffn_d)
        pload(gheadbc[:], ghead_d.partition_broadcast(128))
        pload(gfinbc[:], gfin_d.partition_broadcast(128))
        pload(bcol[:], bcol_d)
        pload(browst[:], brow_d.partition_broadcast(128))
        S.op("dve", lambda e: e.tensor_copy(out=browbc[:], in_=browst[:]), reads=[r_par], writes=[r_par])
        pload(bgatebc[:], bgate_d.partition_broadcast(128))
        pload(cwcol[:], cw_d)
        pload(cwfcol[:], cwf_d)
        state["w_gate"] = [r_crit, r_mod]
        S.dma("pool", wgate[:], win_d[:, O_IG:O_IG + 8].rearrange("(k p) n -> p k n", p=128), reads=[r_in], writes=[r_wgate])

        S.op("act", lambda e: e.activation(out=scol[:], in_=ccol[:], func=AF.Silu), reads=[r_crit], writes=[r_crit])
        QS = 128.0 ** -0.5
        S.op("dve", lambda e: e.tensor_scalar(out=bqs[:], in0=bcol[:, 24:28], scalar1=QS, scalar2=None, op0=ALU.mult),
             reads=[r_par], writes=[r_par])

        def ada_blocks(nlist):
            for n in nlist:
                si = load_w("w_ada", n * 512)
                b = next_bank()
                for k in range(8):
                    S.op("pe", lambda e, k=k: e.matmul(psum[0:1, b, :], lhsT=scol[:, k:k + 1], rhs=wslot[si][:, k, :],
                                                      start=(k == 0), stop=(k == 7)),
                         reads=[r_crit, r_wslot[si]], writes=[r_bank[b]], inc=(k == 7))
                S.op("dve", lambda e: e.tensor_tensor(out=modrow[0:1, n * 512:(n + 1) * 512], in0=psum[0:1, b, :],
                                                      in1=modrow[0:1, n * 512:(n + 1) * 512], op=ALU.add),
                     reads=[r_bank[b], r_mod], writes=[r_mod])
                free_w(si)

        def bcast_row(row_ap, dst, r_dst):
            for h in range(2):
                b = next_bank()
                S.op("pe", lambda e: e.matmul(psum[:, b, :], lhsT=onesf[0:1, :], rhs=row_ap[0:1, h * 512:(h + 1) * 512], start=True, stop=True),
                     reads=[r_mod, r_const], writes=[r_bank[b]])
                S.op("act", lambda e: e.activation(out=dst[:, h * 512:(h + 1) * 512], in_=psum[:, b, :], func=AF.Copy),
                     reads=[r_bank[b]], writes=[r_dst])

        def row_to_col(off, dst, r_dst):
            b = next_bank()
            for k in range(8):
                S.op("pe", lambda e, k=k: e.matmul(psum[:, b, k:k + 1], lhsT=modrow[0:1, off + k * 128:off + (k + 1) * 128], rhs=onesf[0:1, 0:1],
                                                  start=True, stop=True),
                     reads=[r_mod, r_const], writes=[r_bank[b]], inc=(k == 7))
            S.op("act", lambda e: e.activation(out=dst[:], in_=psum[:, b, 0:8], func=AF.Copy), reads=[r_bank[b]], writes=[r_dst])

        def ada_part1():
            ada_blocks([2, 3, 0, 1])
            S.op("dve", lambda e: e.scalar_tensor_tensor(out=growA[:], in0=modrow[0:1, D:2 * D], scalar=1.0, in1=growA[:],
                                                         op0=ALU.add, op1=ALU.mult), reads=[r_mod, r_crit], writes=[r_mod])
            bcast_row(growA, G1bc, r_par)
            row_to_col(0, sh1row, r_par)

        def ada_part2():
            S.op("dve", lambda e: e.scalar_tensor_tensor(out=growB[:], in0=modrow[0:1, 4 * D:5 * D], scalar=1.0, in1=growB[:],
                                                         op0=ALU.add, op1=ALU.mult), reads=[r_mod, r_par], writes=[r_mod])
            bcast_row(growB, G2bc, r_par2)
            bcast_row(modrow[0:1, 2 * D:3 * D], gt1bc, r_par2)
            bcast_row(modrow[0:1, 5 * D:6 * D], gt2bc, r_par2)
            row_to_col(3 * D, sh2row, r_par2)

        ada_part1()

        def rms_rstd(src_ap, src_res, n_feat):
            ssq, r_ssq = stat_col()
            S.op("act", lambda e: e.activation(out=sqj[:, 0:n_feat], in_=src_ap, func=AF.Square, accum_out=ssq),
                 reads=src_res, writes=[r_sqj, r_ssq])
            sd, r_sd = stat_col()
            S.op("act", lambda e: e.activation(out=sd, in_=ssq, func=AF.Sqrt, scale=1.0 / n_feat, bias=EPS),
                 reads=[r_ssq], writes=[r_sd])
            rs, r_rs = stat_col()
            S.op("dve", lambda e: e.reciprocal(out=rs, in_=sd), reads=[r_sd], writes=[r_rs])
            return rs, r_rs

        def norm_front(src_ap, src_res, Gbc):
            rs, r_rs = rms_rstd(src_ap, src_res, D)
            i = state["xnb"]
            state["xnb"] = 1 - i
            S.op("dve", lambda e: e.scalar_tensor_tensor(out=xnb[i][:], in0=src_ap, scalar=rs, in1=Gbc[:], op0=ALU.mult, op1=ALU.mult),
                 reads=list(src_res) + [r_rs, r_par, r_par2], writes=[r_xnb[i]])
            return i

        def norm_back(i, shrow, dstT, col0, dst_res):
            for half in range(2):
                b = next_bank()
                for kk in range(4):
                    k = half * 4 + kk
                    S.op("pe", lambda e, k=k, kk=kk: e.matmul(psum[:, b, kk * 128:(kk + 1) * 128], lhsT=xnb[i][:, k * 128:(k + 1) * 128],
                                                              rhs=identb[:], start=True, stop=True),
                         reads=[r_xnb[i], r_const], writes=[r_bank[b]], inc=(kk == 3))
                for kk in range(4):
                    k = half * 4 + kk
                    if half == 0:
                        S.op("act", lambda e, k=k, kk=kk: e.activation(out=dstT[:, k, col0:col0 + 128], in_=psum[:, b, kk * 128:(kk + 1) * 128],
                                                                       func=AF.Identity, bias=shrow[:, k:k + 1]),
                             reads=[r_bank[b], r_par, r_par2], writes=[dst_res])
                    else:
                        S.op("dve", lambda e, k=k, kk=kk: e.tensor_scalar(out=dstT[:, k, col0:col0 + 128], in0=psum[:, b, kk * 128:(kk + 1) * 128],
                                                                          scalar1=shrow[:, k:k + 1], scalar2=None, op0=ALU.add),
                             reads=[r_bank[b], r_par, r_par2], writes=[dst_res])

        def load_x_tile(row0):
            i = state["xt"]
            state["xt"] = (i + 1) % 2
            S.dma("sp", xt[i][:], x_d[row0:row0 + 128, :], reads=[r_in], writes=[r_xt[i]])
            return i

        def fm_proj(si_list, wcol0, hT, r_h, tbs, evac):
            si = si_list
            for (t0, n) in tbs:
                b = next_bank()
                for k in range(8):
                    S.op("pe", lambda e, k=k: e.matmul(psum[:, b, 0:n], lhsT=wslot[si][:, k, wcol0:wcol0 + 128], rhs=hT[:, k, t0:t0 + n],
                                                      start=(k == 0), stop=(k == 7)),
                         reads=[r_wslot[si]] + r_h, writes=[r_bank[b]], inc=(k == 7))
                evac(b, t0, n)

        def interleave(gen, units):
            for u_ in units:
                next(gen, None)
                u_()
            for _ in gen:
                pass

        def gate_math(scope, graw, r_graw, nch, first_stage_flag, want_dmin, GB, out):
            ng = nch * 4
            T = lambda name, shape: sb(name, shape, F32, scope)
            ig = graw[:, 0:nch, 0:4]
            fg = graw[:, 0:nch, 4:8]
            t_abs = T("g_abs", [128, nch, 4])
            t_e = T("g_e", [128, nch, 4])
            logf = T("g_logf", [128, nch, 4])
            alpha = T("g_alpha", [128, ng])
            b_sb = T("g_b", [128, ng])
            acol = T("g_acol", [128, 1])
            arow = T("g_arow", [1, nch, 4])
            grow = T("g_grow", [1, nch, 4])
            mb = T("g_mb", [1, nch + 1, 4])
            mp = T("g_mp", [1, nch, 4])
            sold_row = T("g_soldrow", [1, nch, 4])
            tmp = T("g_tmp", [128, ng])
            wG = T("g_wG", [128, ng])
            sold = T("g_sold", [128, ng])
            dmin = T("g_dmin", [128, ng])
            rg = Res("gm")
            S.op("dve", lambda e: e.scalar_tensor_tensor(out=t_abs[:], in0=fg, scalar=-1.0, in1=fg, op0=ALU.mult, op1=ALU.max),
                 reads=[r_graw], writes=[rg])
            S.op("act", lambda e: e.activation(out=t_e[:], in_=t_abs[:], func=AF.Exp, scale=-1.0), reads=[rg], writes=[rg])
            S.op("act", lambda e: e.activation(out=t_e[:], in_=t_e[:], func=AF.Ln, bias=1.0), reads=[rg], writes=[rg])
            S.op("dve", lambda e: e.scalar_tensor_tensor(out=logf[:], in0=fg, scalar=0.0, in1=t_e[:], op0=ALU.min, op1=ALU.subtract),
                 reads=[rg, r_graw], writes=[rg])
            logf2 = logf[:].rearrange("p c h -> p (c h)")
            b0, b1, b2, b3 = GB[0], GB[1], GB[0], GB[1]
            yield
            S.op("pe", lambda e: e.matmul(psum[:, b0, 0:ng], lhsT=maskf[:], rhs=logf2, start=True, stop=True),
                 reads=[rg, r_const], writes=[r_bank[b0]])
            S.op("pe", lambda e: e.matmul(psum[0:1, b1, 0:ng], lhsT=onesf[:, 0:1], rhs=logf2, start=True, stop=True),
                 reads=[rg, r_const], writes=[r_bank[b1]])
            S.op("dve", lambda e: e.tensor_tensor(out=alpha[:].rearrange("p (c h) -> p c h", h=4), in0=ig,
                                                  in1=psum[:, b0, 0:ng].rearrange("p (c h) -> p c h", h=4), op=ALU.subtract),
                 reads=[r_graw, r_bank[b0]], writes=[rg])
            S.op("dve", lambda e: e.tensor_copy(out=b_sb[:], in_=psum[:, b0, 0:ng]), reads=[r_bank[b0]], writes=[rg])
            S.op("act", lambda e: e.activation(out=grow[:].rearrange("p c h -> p (c h)"), in_=psum[0:1, b1, 0:ng], func=AF.Copy),
                 reads=[r_bank[b1]], writes=[rg])
            yield
            S.op("pe", lambda e: e.matmul(psum[0:ng, b2, 0:128], lhsT=alpha[:], rhs=identf[:], start=True, stop=True),
                 reads=[rg, r_const], writes=[r_bank[b2]])
            S.op("dve", lambda e: e.tensor_reduce(out=acol[0:ng, :], in_=psum[0:ng, b2, 0:128], axis=AX.X, op=ALU.max),
                 reads=[r_bank[b2]], writes=[rg])
            yield
            S.op("pe", lambda e: e.matmul(psum[0:1, b3, 0:ng], lhsT=acol[0:ng, 0:1], rhs=identf[0:ng, 0:ng], start=True, stop=True),
                 reads=[rg, r_const], writes=[r_bank[b3]])
            S.op("act", lambda e: e.activation(out=arow[:].rearrange("p c h -> p (c h)"), in_=psum[0:1, b3, 0:ng], func=AF.Copy),
                 reads=[r_bank[b3]], writes=[rg])
            yield
            S.op("dve", lambda e: e.tensor_copy(out=mb[0:1, 0, :], in_=mcarry[:]), reads=[r_mc], writes=[rg])
            for c in range(nch):
                S.op("dve", lambda e, c=c: e.tensor_tensor(out=mp[0:1, c, :], in0=mb[0:1, c, :], in1=arow[0:1, c, :], op=ALU.max),
                     reads=[rg], writes=[rg])
                S.op("dve", lambda e, c=c: e.tensor_tensor(out=mb[0:1, c + 1, :], in0=mp[0:1, c, :], in1=grow[0:1, c, :], op=ALU.add),
                     reads=[rg], writes=[rg])
                if first_stage_flag and c == 0:
                    S.op("dve", lambda e, c=c: e.tensor_scalar(out=mb[0:1, c + 1, :], in0=mb[0:1, c + 1, :], scalar1=flag_t[0:1, 0:1],
                                                               scalar2=None, op0=ALU.mult), reads=[rg, r_par], writes=[rg])
            S.op("dve", lambda e: e.tensor_copy(out=mcarry[:], in_=mb[0:1, nch, :]), reads=[rg], writes=[r_mc])
            S.op("dve", lambda e: e.tensor_tensor(out=sold_row[:], in0=mb[0:1, 0:nch, :], in1=mp[:], op=ALU.subtract),
                 reads=[rg], writes=[rg])
            S.op("act", lambda e: e.activation(out=sold_row[:], in_=sold_row[:], func=AF.Exp), reads=[rg], writes=[rg])
            if first_stage_flag:
                S.op("dve", lambda e: e.tensor_scalar(out=sold_row[0:1, 1, :], in0=sold_row[0:1, 1, :], scalar1=flag_t[0:1, 0:1],
                                                      scalar2=None, op0=ALU.mult), reads=[rg, r_par], writes=[rg])
            b4, b5 = GB[0], GB[1]
            yield
            yield
            S.op("pe", lambda e: e.matmul(psum[:, b4, 0:ng], lhsT=onesf[0:1, :], rhs=mp[:].rearrange("p c h -> p (c h)"), start=True, stop=True),
                 reads=[rg, r_const], writes=[r_bank[b4]])
            S.op("pe", lambda e: e.matmul(psum[:, b5, 0:ng], lhsT=onesf[0:1, :], rhs=sold_row[:].rearrange("p c h -> p (c h)"),
                                          start=True, stop=True), reads=[rg, r_const], writes=[r_bank[b5]])
            S.op("act", lambda e: e.activation(out=sold[:], in_=psum[:, b5, 0:ng], func=AF.Copy), reads=[r_bank[b5]], writes=[rg])
            S.op("dve", lambda e: e.tensor_tensor(out=tmp[:], in0=alpha[:], in1=psum[:, b4, 0:ng], op=ALU.subtract),
                 reads=[rg, r_bank[b4]], writes=[rg])
            S.op("act", lambda e: e.activation(out=wG[:], in_=tmp[:], func=AF.Exp), reads=[rg], writes=[rg])
            if want_dmin:
                S.op("dve", lambda e: e.tensor_tensor(out=tmp[:], in0=b_sb[:], in1=psum[:, b4, 0:ng], op=ALU.add),
                     reads=[rg, r_bank[b4]], writes=[rg])
                S.op("act", lambda e: e.activation(out=dmin[:], in_=tmp[:], func=AF.Exp, scale=-1.0), reads=[rg], writes=[rg])
            out.update(wG=wG, sold=sold, dmin=dmin, rg=rg)

        def kv_gate_proj(c, hTt, r_hT, col0, si_k, si_v, ktok, r_ktok, vaug, r_vaug, gbank, ridx, kT_src=None):
            b = next_bank()
            if kT_src is None:
                for k in range(8):
                    S.op("pe", lambda e, k=k: e.matmul(psum[:, b, :], lhsT=hTt[:, k, col0:col0 + 128], rhs=wslot[si_k][:, k, :],
                                                      start=(k == 0), stop=(k == 7)),
                         reads=[r_hT, r_wslot[si_k]], writes=[r_bank[b]], inc=(k == 7))
                S.op("dve", lambda e: e.tensor_tensor(out=ktok[:, ridx, :], in0=psum[:, b, :], in1=browbc[:, 0:512], op=ALU.add),
                     reads=[r_bank[b], r_par], writes=[r_ktok[ridx]])
            else:
                kT_, r_kT_c = kT_src
                for h in range(4):
                    S.op("pe", lambda e, h=h: e.matmul(psum[:, b, h * 128:(h + 1) * 128], lhsT=kT_[:, h, col0:col0 + 128], rhs=identb[:],
                                                      start=True, stop=True),
                         reads=[r_kT_c, r_const], writes=[r_bank[b]], inc=(h == 3))
                S.op("dve", lambda e: e.tensor_copy(out=ktok[:, ridx, :], in_=psum[:, b, :]), reads=[r_bank[b]], writes=[r_ktok[ridx]])
            for half in range(2):
                b = next_bank()
                for k in range(8):
                    S.op("pe", lambda e, k=k: e.matmul(psum[:, b, :], lhsT=hTt[:, k, col0:col0 + 128], rhs=wslot[si_v[half]][:, k, :],
                                                      start=(k == 0), stop=(k == 7)),
                         reads=[r_hT, r_wslot[si_v[half]]], writes=[r_bank[b]], inc=(k == 7))
                S.op("dve", lambda e: e.tensor_tensor(out=vaug[:, ridx, 2 * half:2 * half + 2, 0:256],
                                                      in0=psum[:, b, :].rearrange("p (h v) -> p h v", h=2),
                                                      in1=browbc[:, 512 + half * 512:512 + (half + 1) * 512].rearrange("p (h v) -> p h v", h=2),
                                                      op=ALU.add),
                     reads=[r_bank[b], r_par], writes=[r_vaug[ridx]])
            for k in range(8):
                S.op("pe", lambda e, k=k: e.matmul(psum[:, gbank, ridx * 8:ridx * 8 + 8], lhsT=hTt[:, k, col0:col0 + 128], rhs=wgate[:, k, :],
                                                  start=(k == 0), stop=(k == 7)),
                     reads=[r_hT, r_wgate], writes=[r_bank[gbank]], inc=(k == 7))

        def state_update(ridx, gi0, ktok, r_ktok, vw, r_vw, sold, rg, pbank):
            for h in range(4):
                pb = pbank[h % 2]
                S.op("pe", lambda e: e.matmul(psum[:, pb, 0:257], lhsT=ktok[:, ridx, h * 128:(h + 1) * 128], rhs=vw[:, h, :],
                                              start=True, stop=True),
                     reads=[r_ktok[ridx], r_vw], writes=[r_bank[pb]])
                S.op("dve", lambda e: e.scalar_tensor_tensor(out=C32[:, h, :], in0=C32[:, h, :], scalar=sold[:, gi0 + h:gi0 + h + 1],
                                                             in1=psum[:, pb, 0:257], op0=ALU.mult, op1=ALU.add),
                     reads=[r_C32[h], rg, r_bank[pb]], writes=[r_C32[h]])

        NPF = 15
        with ExitStack() as ps_:
            ktok_p = sb("ktok_p", [128, NPF, 512], BF16, ps_)
            vaug_p = sb("vaug_p", [128, NPF, 4, 257], BF16, ps_)
            graw_p = sb("graw_p", [128, NPF, 8], F32, ps_)
            hTp = [sb(f"hTp{i}", [128, 8, 128], BF16, ps_) for i in range(2)]
            r_ktok_p = RL("ktok_p", NPF)
            r_vaug_p = RL("vaug_p", NPF)
            r_hTp = RL("hTp", 2)
            r_graw_p = Res("graw_p")
            S.op("dve", lambda e: e.memset(vaug_p[:, :, :, 256:257], 1.0), writes=r_vaug_p)
            si_k = load_w("w_in", O_K)
            si_v = [load_w("w_in", O_V + i * 512) for i in range(2)]
            gbank = 7
            state["resv"] = {gbank, 5, 6}
            def pfront(c):
                xi = load_x_tile(c * 128)
                return norm_front(xt[xi][:], [r_xt[xi]], G1bc)

            nf = {0: pfront(0)}
            if NPF > 1:
                nf[1] = pfront(1)
            norm_back(nf[0], sh1row, hTp[0], 0, r_hTp[0])
            for c in range(NPF):
                if c + 2 < NPF:
                    nf[c + 2] = pfront(c + 2)
                if c + 1 < NPF:
                    norm_back(nf[c + 1], sh1row, hTp[(c + 1) % 2], 0, r_hTp[(c + 1) % 2])
                j = c % 2
                kv_gate_proj(c, hTp[j], r_hTp[j], 0, si_k, si_v, ktok_p, r_ktok_p, vaug_p, r_vaug_p, gbank, c)
            free_w(si_k, *si_v)
            S.op("dve", lambda e: e.tensor_tensor(out=graw_p[:], in0=psum[:, gbank, 0:NPF * 8].rearrange("p (c g) -> p c g", g=8),
                                                  in1=bgatebc[:].rearrange("p (o g) -> p o g", o=1).to_broadcast([128, NPF, 8]), op=ALU.add),
                 reads=[r_bank[gbank], r_par], writes=[r_graw_p])
            state["resv"] = {4, 5, 6, 7}
            gm = {}
            interleave(gate_math(ps_, graw_p, r_graw_p, NPF, False, False, (4, 7), gm),
                       [(lambda n=n: ada_blocks([n])) for n in range(4, 12)])
            ada_part2()
            wG, sold, dmin, rg = gm["wG"], gm["sold"], gm["dmin"], gm["rg"]
            def prescale_p(c):
                for h in range(4):
                    S.op("act", lambda e, h=h: e.activation(out=vaug_p[:, c, h, :], in_=vaug_p[:, c, h, :], func=AF.Identity,
                                                            scale=wG[:, c * 4 + h:c * 4 + h + 1]),
                         reads=[r_vaug_p[c], rg], writes=[r_vaug_p[c]])

            prescale_p(0)
            for c in range(NPF):
                if c + 1 < NPF:
                    prescale_p(c + 1)
                state_update(c, c * 4, ktok_p, r_ktok_p, vaug_p[:, c], r_vaug_p[c], sold, rg, (5, 6))
            state["resv"] = set()
            S.barrier()
        ada.close()

        prefront = {}

        def run_stage(ci0, nch, first, next_ci0=None):
            if plan is None:
                marks.append(len(rec))
            else:
                W["stage"] += 1
                W["fence"] = marks[W["stage"]] if W["stage"] < len(marks) else len(plan)
            if not first:
                state["w_gate"] = [r_y]
            NT = nch * 128
            tbs = []
            t = 0
            if first:
                tbs.append((0, 128))
                t = 128
            while t < NT:
                n = min(512, NT - t)
                tbs.append((t, n))
                t += n
            x_row0 = ci0 * 128
            with ExitStack() as st:
                x1 = sb("x1", [128, nch, D], F32, st)
                r_x1 = RL("x1_", nch)
                hT = sb("hT", [128, 8, NT], BF16, st)
                sgoT = sb("sgoT", [128, 8, NT], BF16, st)
                hmT = sgoT
                r_hT = RL("hT", nch)
                r_hmT = RL("hmT", nch)

                def blk(rl, t0, n):
                    return rl[t0 // 128:(t0 + n) // 128]

                def sfront(c):
                    xi = load_x_tile(x_row0 + c * 128)
                    return norm_front(xt[xi][:], [r_xt[xi]], G1bc)

                nf = prefront.pop("nf", None) or {0: sfront(0)}
                for c in range(nch):
                    if c + 1 < nch and (c + 1) not in nf:
                        nf[c + 1] = sfront(c + 1)
                    norm_back(nf[c], sh1row, hT, c * 128, r_hT[c])

                with ExitStack() as ms:
                    qT = sb("qT", [128, 4, NT], BF16, ms)
                    kT = sb("kT", [128, 4, NT], BF16, ms)
                    ktok = sb("ktok", [128, nch, 512], BF16, ms)
                    vaug = sb("vaug", [128, nch, 4, 257], BF16, ms)
                    graw = sb("graw", [128, nch, 8], F32, ms)
                    Sm = [sb(f"Sm{i}", [128, 4, 128], BF16, ms) for i in range(3)]
                    hm = [sb(f"hm{i}", [128, D], BF16, ms) for i in range(3)]
                    r_qT = RL("qT", nch)
                    r_kT = RL("kT", nch)
                    r_ktok = RL("ktok", nch)
                    r_vaug = RL("vaug", nch)
                    r_sgoT = r_hmT
                    r_graw = Res("graw")
                    r_Sm = RL("Sm", 3)
                    r_hm = RL("hm", 3)
                    S.op("dve", lambda e: e.memset(vaug[:, :, :, 256:257], 1.0), writes=r_vaug)
                    si_q = load_w("w_in", O_Q)
                    si_k = load_w("w_in", O_K)
                    for h in range(4):
                        def ev_q(b, t0, n, h=h):
                            S.op("act", lambda e: e.activation(out=qT[:, h, t0:t0 + n], in_=psum[:, b, 0:n], func=AF.Identity,
                                                               bias=bqs[:, h:h + 1], scale=QS),
                                 reads=[r_bank[b], r_par], writes=blk(r_qT, t0, n))
                        fm_proj(si_q, h * 128, hT, r_hT, tbs, ev_q)
                    free_w(si_q)
                    for h in range(4):
                        def ev_k(b, t0, n, h=h):
                            S.op("act", lambda e: e.activation(out=kT[:, h, t0:t0 + n], in_=psum[:, b, 0:n], func=AF.Identity,
                                                               bias=bcol[:, 28 + h:29 + h]),
                                 reads=[r_bank[b], r_par], writes=blk(r_kT, t0, n))
                        fm_proj(si_k, h * 128, hT, r_hT, tbs, ev_k)
                    free_w(si_k)
                    si_v = [load_w("w_in", O_V + i * 512) for i in range(2)]
                    gbank = 7
                    state["resv"] = {gbank}
                    for c in range(nch):
                        kv_gate_proj(c, hT, r_hT[c], c * 128, si_k, si_v, ktok, r_ktok, vaug, r_vaug, gbank, c, kT_src=(kT, r_kT[c]))
                    free_w(*si_v)
                    S.op("dve", lambda e: e.tensor_tensor(out=graw[:], in0=psum[:, gbank, 0:nch * 8].rearrange("p (c g) -> p c g", g=8),
                                                          in1=bgatebc[:].rearrange("p (o g) -> p o g", o=1).to_broadcast([128, nch, 8]), op=ALU.add),
                         reads=[r_bank[gbank], r_par], writes=[r_graw])
                    state["resv"] = {6, 7}
                    units = []
                    for i in range(2):
                        for jj in range(4):
                            def unit(i=i, jj=jj):
                                if jj == 0:
                                    state["si_o"] = load_w("w_in", O_O + i * 512)
                                j = i * 4 + jj
                                def ev_o(b, t0, n, j=j):
                                    S.op("act", lambda e: e.activation(out=sgoT[:, j, t0:t0 + n], in_=psum[:, b, 0:n], func=AF.Sigmoid,
                                                                       bias=bcol[:, 48 + j:49 + j]),
                                         reads=[r_bank[b], r_par], writes=blk(r_sgoT, t0, n))
                                fm_proj(state["si_o"], jj * 128, hT, r_hT, tbs, ev_o)
                                if jj == 3:
                                    free_w(state["si_o"])
                            units.append(unit)
                    gm = {}
                    interleave(gate_math(ms, graw, r_graw, nch, first, True, (6, 7), gm), units)
                    wG, sold, dmin, rg = gm["wG"], gm["sold"], gm["dmin"], gm["rg"]
                    state["resv"] = set()
                    S.barrier()
                    ng_ = nch * 4
                    def prescale(c):
                        for h in range(4):
                            S.op("act", lambda e, h=h: e.activation(out=vaug[:, c, h, :], in_=vaug[:, c, h, :], func=AF.Identity,
                                                                    scale=wG[:, c * 4 + h:c * 4 + h + 1]),
                                 reads=[r_vaug[c], rg], writes=[r_vaug[c]])

                    prescale(0)
                    r_X = RL("X", 4)
                    r_STb = Res("STb")
                    Xs = [sb(f"Xs{i}", [128, 4, 257], F32, ms) for i in range(3)]
                    den4 = [sb(f"den4{i}", [128, 4], F32, ms) for i in range(3)]
                    ssq4 = [sb(f"ssq4{i}", [128, 4], F32, ms) for i in range(3)]
                    fac4 = [sb(f"fac4{i}", [128, 4], F32, ms) for i in range(3)]
                    r_Xs = RL("Xs", 3)
                    r_d4 = RL("den4", 3)
                    r_s4 = RL("ssq4", 3)
                    r_f4 = RL("fac4", 3)
                    def st_A(c):
                        j = c % 3
                        cs = slice(c * 128, (c + 1) * 128)
                        for h in range(4):
                            S.op("pe", lambda e, h=h: e.matmul(psum[:, 0, h * 128:(h + 1) * 128], lhsT=kT[:, h, cs], rhs=qT[:, h, cs],
                                                               start=True, stop=True),
                                 reads=[r_kT[c], r_qT[c]], writes=[r_STb], inc=(h == 3))
                        S.op("dve", lambda e: e.tensor_tensor(out=Sm[j][:], in0=psum[:, 0, :].rearrange("p (h t) -> p h t", h=4),
                                                              in1=maskf[:].rearrange("p (o t) -> p o t", o=1).to_broadcast([128, 4, 128]), op=ALU.mult),
                             reads=[r_STb, r_const], writes=[r_Sm[j]])
                        for h in range(4):
                            S.op("act", lambda e, h=h: e.activation(out=Cb[:, h, :], in_=C32[:, h, :], func=AF.Identity,
                                                                    scale=sold[:, c * 4 + h:c * 4 + h + 1]),
                                 reads=[r_C32[h], rg], writes=[r_Cb[h]])

                    def st_B(c):
                        j = c % 3
                        cs = slice(c * 128, (c + 1) * 128)
                        for h in range(4):
                            S.op("pe", lambda e, h=h: e.matmul(psum[:, 1 + h, 0:257], lhsT=Sm[j][:, h, :], rhs=vaug[:, c, h, :], start=True, stop=False),
                                 reads=[r_Sm[j], r_vaug[c]], writes=[r_X[h]], inc=False)
                            S.op("pe", lambda e, h=h: e.matmul(psum[:, 1 + h, 0:257], lhsT=qT[:, h, cs], rhs=Cb[:, h, :], start=False, stop=True),
                                 reads=[r_qT[c], r_Cb[h]], writes=[r_X[h]], inc=(h == 3))
                        S.op("act", lambda e: e.activation(out=Xs[j][:], in_=psum[:, 1:5, 0:257], func=AF.Copy), reads=r_X, writes=[r_Xs[j]])
                        state_update(c, c * 4, ktok, r_ktok, vaug[:, c], r_vaug[c], sold, rg, (5, 6))

                    def st_C1(c):
                        j = c % 3
                        S.op("dve", lambda e: e.scalar_tensor_tensor(out=den4[j][:].rearrange("p (h o) -> p h o", o=1), in0=Xs[j][:, :, 256:257],
                                                                     scalar=-1.0, in1=Xs[j][:, :, 256:257], op0=ALU.mult, op1=ALU.max),
                             reads=[r_Xs[j]], writes=[r_d4[j]])
                        S.op("dve", lambda e: e.tensor_tensor(out=den4[j][:], in0=den4[j][:], in1=dmin[:, c * 4:c * 4 + 4], op=ALU.max),
                             reads=[r_d4[j], rg], writes=[r_d4[j]])
                        S.op("dve", lambda e: e.reciprocal(out=den4[j][:], in_=den4[j][:]), reads=[r_d4[j]], writes=[r_d4[j]])
                        for h in range(4):
                            S.op("act", lambda e, h=h: e.activation(out=sqj[:, h * 256:(h + 1) * 256], in_=Xs[j][:, h, 0:256], func=AF.Square,
                                                                    scale=den4[j][:, h:h + 1], accum_out=ssq4[j][:, h:h + 1]),
                                 reads=[r_Xs[j], r_d4[j]], writes=[r_sqj, r_s4[j]])
                        S.op("act", lambda e: e.activation(out=ssq4[j][:], in_=ssq4[j][:], func=AF.Sqrt, scale=1.0 / 256, bias=EPS),
                             reads=[r_s4[j]], writes=[r_s4[j]])

                    def st_C2(c):
                        j = c % 3
                        S.op("dve", lambda e: e.reciprocal(out=fac4[j][:], in_=ssq4[j][:]), reads=[r_s4[j]], writes=[r_f4[j]])
                        S.op("dve", lambda e: e.tensor_tensor(out=fac4[j][:], in0=fac4[j][:], in1=den4[j][:], op=ALU.mult),
                             reads=[r_f4[j], r_d4[j]], writes=[r_f4[j]])
                        for h in range(4):
                            S.op("dve", lambda e, h=h: e.scalar_tensor_tensor(out=hm[j][:, h * 256:(h + 1) * 256], in0=Xs[j][:, h, 0:256],
                                                                              scalar=fac4[j][:, h:h + 1], in1=gheadbc[:, h * 256:(h + 1) * 256],
                                                                              op0=ALU.mult, op1=ALU.mult),
                                 reads=[r_Xs[j], r_f4[j], r_par], writes=[r_hm[j]])

                    def st_D(c):
                        j = c % 3
                        cs = slice(c * 128, (c + 1) * 128)
                        for half in range(2):
                            tb_ = 7
                            for kk in range(4):
                                k = half * 4 + kk
                                S.op("pe", lambda e, k=k, kk=kk: e.matmul(psum[:, tb_, kk * 128:(kk + 1) * 128], lhsT=hm[j][:, k * 128:(k + 1) * 128],
                                                                          rhs=identb[:], start=True, stop=True),
                                     reads=[r_hm[j], r_const], writes=[r_bank[tb_]], inc=(kk == 3))
                            S.op("dve", lambda e: e.tensor_tensor(out=hmT[:, half * 4:half * 4 + 4, cs],
                                                                  in0=psum[:, tb_, :].rearrange("p (k t) -> p k t", k=4),
                                                                  in1=sgoT[:, half * 4:half * 4 + 4, cs], op=ALU.mult),
                                 reads=[r_bank[tb_], r_hmT[c]], writes=[r_hmT[c]])

                    for i in range(nch + 2):
                        if i + 1 < nch:
                            prescale(i + 1)
                        if i < nch:
                            st_A(i)
                        if 0 <= i - 2 < nch:
                            st_C2(i - 2)
                        if 0 <= i - 1 < nch:
                            st_C1(i - 1)
                        if i < nch:
                            st_B(i)
                        if 0 <= i - 2 < nch:
                            st_D(i - 2)
                    S.barrier()
                with ExitStack() as cs_:
                    ycT = sb("ycT", [128, 8, NT], BF16, cs_)
                    zT = sb("zT", [128, 8, NT], BF16, cs_)
                    xs2 = [sb(f"xs{i}", [128, NT], F32, cs_) for i in range(2)]
                    u2 = [sb(f"u{i}", [128, NT + 2], F32, cs_) for i in range(2)]
                    t32 = [sb(f"t3{i}", [128, NT], F32, cs_) for i in range(2)]
                    sg2 = [sb(f"sg{i}", [128, NT], BF16, cs_) for i in range(2)]
                    zc2 = [sb(f"zc{i}", [128, NT], F32, cs_) for i in range(2)]
                    r_ycT = Res("ycT")
                    r_zT = RL("zT", nch)
                    r_xs2, r_u2, r_t32, r_sg2, r_zc2 = RL("xs", 2), RL("u", 2), RL("t3", 2), RL("sg", 2), RL("zc", 2)
                    cv = {}

                    def conv_front(j):
                        xs, u, t3 = xs2[j % 2], u2[j % 2], t32[j % 2]
                        r_xs, r_u, r_t3 = r_xs2[j % 2], r_u2[j % 2], r_t32[j % 2]
                        if j % 4 == 0:
                            cv["x"] = load_w("w_in", O_XIN + (j // 4) * 512)
                            cv["c"] = load_w("w_in", O_CG + (j // 4) * 512)
                            cv[("b", j // 4)] = load_w("w_in", O_BG + (j // 4) * 512)
                        si_x, si_c = cv["x"], cv["c"]
                        wc = (j % 4) * 128
                        def ev_x(b, t0, n, j=j):
                            S.op("act", lambda e: e.activation(out=xs[:, t0:t0 + n], in_=psum[:, b, 0:n], func=AF.Identity, bias=bcol[:, j:j + 1]),
                                 reads=[r_bank[b], r_par], writes=[r_xs])
                        fm_proj(si_x, wc, hT, r_hT, tbs, ev_x)
                        S.op("dve", lambda e, j=j: e.tensor_copy(out=u[:, 0:2], in_=uhalo[:, j, :]), reads=[r_uh], writes=[r_u])
                        def ev_c(b, t0, n, j=j):
                            S.op("dve", lambda e: e.scalar_tensor_tensor(out=u[:, 2 + t0:2 + t0 + n], in0=psum[:, b, 0:n], scalar=bcol[:, 16 + j:17 + j],
                                                                         in1=xs[:, t0:t0 + n], op0=ALU.add, op1=ALU.mult),
                                 reads=[r_bank[b], r_par, r_xs], writes=[r_u])
                        fm_proj(si_c, wc, hT, r_hT, tbs, ev_c)
                        if first:
                            S.op("dve", lambda e: e.tensor_scalar(out=u[:, 128:130], in0=u[:, 128:130], scalar1=flag_t[:, 0:1], scalar2=None, op0=ALU.mult),
                                 reads=[r_u, r_par], writes=[r_u])
                        S.op("dve", lambda e, j=j: e.tensor_copy(out=uhalo[:, j, :], in_=u[:, NT:NT + 2]), reads=[r_u], writes=[r_uh])
                        S.op("act", lambda e, j=j: e.activation(out=t3[:], in_=u[:, 2:NT + 2], func=AF.Identity, scale=cwcol[:, j * 3 + 2:j * 3 + 3]),
                             reads=[r_u, r_par], writes=[r_t3])
                        S.op("dve", lambda e, j=j: e.scalar_tensor_tensor(out=t3[:], in0=u[:, 1:NT + 1], scalar=cwcol[:, j * 3 + 1:j * 3 + 2], in1=t3[:],
                                                                          op0=ALU.mult, op1=ALU.add), reads=[r_u, r_par, r_t3], writes=[r_t3])
                        S.op("dve", lambda e, j=j: e.scalar_tensor_tensor(out=t3[:], in0=u[:, 0:NT], scalar=cwcol[:, j * 3:j * 3 + 1], in1=t3[:],
                                                                          op0=ALU.mult, op1=ALU.add), reads=[r_u, r_par, r_t3], writes=[r_t3])
                        if j % 4 == 3:
                            free_w(si_x, si_c)

                    def conv_back(j):
                        t3, r_t3 = t32[j % 2], r_t32[j % 2]
                        si_b = cv[("b", j // 4)]
                        def ev_b(b, t0, n, j=j):
                            S.op("dve", lambda e: e.scalar_tensor_tensor(out=ycT[:, j, t0:t0 + n], in0=psum[:, b, 0:n], scalar=bcol[:, 8 + j:9 + j],
                                                                         in1=t3[:, t0:t0 + n], op0=ALU.add, op1=ALU.mult),
                                 reads=[r_bank[b], r_par, r_t3], writes=[r_ycT])
                        fm_proj(si_b, (j % 4) * 128, hT, r_hT, tbs, ev_b)
                        if j % 4 == 3:
                            free_w(si_b)

                    conv_front(0)
                    for j in range(1, 8):
                        conv_front(j)
                        conv_back(j - 1)
                    conv_back(7)
                    r_hmT_all = r_hmT
                    for j in range(8):
                        sg, zc = sg2[j % 2], zc2[j % 2]
                        r_sg, r_zc = r_sg2[j % 2], r_zc2[j % 2]
                        if j % 4 == 0:
                            si_gc = load_w("w_in", O_GC + (j // 4) * 512)
                            si_pc = load_w("w_proj_conv", (j // 4) * 512)
                            si_gm = load_w("w_in", O_GM + (j // 4) * 512)
                            si_pm = load_w("w_proj_mlstm", (j // 4) * 512)
                        wc = (j % 4) * 128
                        def ev_g(b, t0, n, col):
                            S.op("act", lambda e: e.activation(out=sg[:, t0:t0 + n], in_=psum[:, b, 0:n], func=AF.Sigmoid, bias=bcol[:, col:col + 1]),
                                 reads=[r_bank[b], r_par], writes=[r_sg])
                        fm_proj(si_gc, wc, hT, r_hT, tbs, lambda b, t0, n, j=j: ev_g(b, t0, n, 32 + j))
                        def ev_pc(b, t0, n):
                            S.op("dve", lambda e: e.tensor_tensor(out=zc[:, t0:t0 + n], in0=psum[:, b, 0:n], in1=sg[:, t0:t0 + n], op=ALU.mult),
                                 reads=[r_bank[b], r_sg], writes=[r_zc])
                        fm_proj(si_pc, wc, ycT, [r_ycT], tbs, ev_pc)
                        fm_proj(si_gm, wc, hT, r_hT, tbs, lambda b, t0, n, j=j: ev_g(b, t0, n, 40 + j))
                        def ev_pm(b, t0, n, j=j):
                            S.op("dve", lambda e: e.tensor_tensor(out=sg[:, t0:t0 + n], in0=psum[:, b, 0:n], in1=sg[:, t0:t0 + n], op=ALU.mult),
                                 reads=[r_bank[b], r_sg], writes=[r_sg])
                            S.op("dve", lambda e: e.tensor_tensor(out=zT[:, j, t0:t0 + n], in0=sg[:, t0:t0 + n], in1=zc[:, t0:t0 + n], op=ALU.add),
                                 reads=[r_sg, r_zc], writes=blk(r_zT, t0, n))
                        fm_proj(si_pm, wc, hmT, r_hmT_all, tbs, ev_pm)
                        if j % 4 == 3:
                            free_w(si_gc, si_pc, si_gm, si_pm)
                    si_o2 = [load_w("w_out", i * 512) for i in range(2)]
                    for i in range(2):
                        S.op("dve", lambda e, i=i: e.tensor_tensor(out=wslot[si_o2[i]][:], in0=wslot[si_o2[i]][:],
                                                                   in1=gt1bc[:, i * 512:(i + 1) * 512].rearrange("p (o n) -> p o n", o=1).to_broadcast([128, 8, 512]),
                                                                   op=ALU.mult),
                             reads=[r_wslot[si_o2[i]], r_par2], writes=[r_wslot[si_o2[i]]])
                    nf2 = {}
                    for c in range(nch):
                        xi = load_x_tile(x_row0 + c * 128)
                        for i in range(2):
                            b = next_bank()
                            for k in range(8):
                                S.op("pe", lambda e, k=k: e.matmul(psum[:, b, :], lhsT=zT[:, k, c * 128:(c + 1) * 128], rhs=wslot[si_o2[i]][:, k, :],
                                                                  start=(k == 0), stop=(k == 7)),
                                     reads=[r_zT[c], r_wslot[si_o2[i]]], writes=[r_bank[b]], inc=(k == 7))
                            S.op("dve", lambda e: e.tensor_tensor(out=x1[:, c, i * 512:(i + 1) * 512], in0=psum[:, b, :],
                                                                  in1=xt[xi][:, i * 512:(i + 1) * 512], op=ALU.add),
                                 reads=[r_bank[b], r_xt[xi]], writes=[r_x1[c]])
                        nf2[c] = norm_front(x1[:, c, :], [r_x1[c]], G2bc)
                        if c >= 1:
                            norm_back(nf2[c - 1], sh2row, hT, (c - 1) * 128, r_hT[c - 1])
                    norm_back(nf2[nch - 1], sh2row, hT, (nch - 1) * 128, r_hT[nch - 1])
                    free_w(*si_o2)
                    S.barrier()
                with ExitStack() as fs:
                    actT2 = [sb(f"actT{i}", [128, 4, NT], BF16, fs) for i in range(2)]
                    a_s2 = [sb(f"a_s{i}", [128, NT + 2], F32, fs) for i in range(2)]
                    t3f2 = [sb(f"t3f{i}", [128, NT], F32, fs) for i in range(2)]
                    sl2 = [sb(f"sl{i}", [128, NT], F32, fs) for i in range(2)]
                    r_actT2 = [RL("actTa", nch), RL("actTb", nch)]
                    r_as2, r_t3f2, r_sl2 = RL("a_s", 2), RL("t3f", 2), RL("sl", 2)
                    groups = [(g * 4, min(4, NFF - g * 4)) for g in range((NFF + 3) // 4)]

                    def final_norm_store(c):
                        ci = ci0 + c
                        if ci < 16:
                            return
                        rs, r_rs = rms_rstd(x1[:, c, :], [r_x1[c]], D)
                        S.op("dve", lambda e: e.scalar_tensor_tensor(out=x1[:, c, :], in0=x1[:, c, :], scalar=rs, in1=gfinbc[:], op0=ALU.mult, op1=ALU.mult),
                             reads=[r_x1[c], r_rs, r_par], writes=[r_x1[c]])
                        S.dma("sp", y_d[(ci - 16) * 128:(ci - 15) * 128, :], x1[:, c, :], reads=[r_x1[c]], writes=[r_y])

                    def emit_down(gi, gn, si_d, wd, last=False):
                        actT, r_actT = actT2[gi % 2], r_actT2[gi % 2]
                        for c in range(nch):
                            for i in range(2):
                                b = next_bank()
                                for jj in range(gn):
                                    S.op("pe", lambda e, jj=jj: e.matmul(psum[:, b, :], lhsT=actT[:, jj, c * 128:(c + 1) * 128], rhs=wd[:, jj, i * 512:(i + 1) * 512],
                                                                        start=(jj == 0), stop=(jj == gn - 1)),
                                         reads=[r_actT[c], r_wslot[si_d]], writes=[r_bank[b]], inc=(jj == gn - 1))
                                S.op("dve", lambda e: e.tensor_tensor(out=x1[:, c, i * 512:(i + 1) * 512], in0=psum[:, b, :],
                                                                      in1=x1[:, c, i * 512:(i + 1) * 512], op=ALU.add),
                                     reads=[r_bank[b], r_x1[c]], writes=[r_x1[c]])
                            if last and c >= 1:
                                final_norm_store(c - 1)
                        if last:
                            final_norm_store(nch - 1)
                        free_w(si_d)

                    pending = None
                    pend_gt = None

                    def up_a(j, jj, si_a):
                        a_s, t3, sl = a_s2[j % 2], t3f2[j % 2], sl2[j % 2]
                        r_as, r_t3, r_sl = r_as2[j % 2], r_t3f2[j % 2], r_sl2[j % 2]
                        S.op("dve", lambda e: e.tensor_copy(out=a_s[:, 0:2], in_=ahalo[:, j, :]), reads=[r_ah], writes=[r_as])
                        def ev_a(b, t0, n):
                            S.op("act", lambda e: e.activation(out=a_s[:, 2 + t0:2 + t0 + n], in_=psum[:, b, 0:n], func=AF.Copy),
                                 reads=[r_bank[b]], writes=[r_as])
                        fm_proj(si_a, jj * 128, hT, r_hT, tbs, ev_a)
                        if first:
                            S.op("dve", lambda e: e.tensor_scalar(out=a_s[:, 128:130], in0=a_s[:, 128:130], scalar1=flag_t[:, 0:1], scalar2=None,
                                                                  op0=ALU.mult), reads=[r_as, r_par], writes=[r_as])
                        S.op("dve", lambda e: e.tensor_copy(out=ahalo[:, j, :], in_=a_s[:, NT:NT + 2]), reads=[r_as], writes=[r_ah])
                        S.op("act", lambda e: e.activation(out=t3[:], in_=a_s[:, 2:NT + 2], func=AF.Identity,
                                                           scale=cwfcol[:, j * 3 + 2:j * 3 + 3]), reads=[r_as, r_par], writes=[r_t3])
                        S.op("dve", lambda e: e.scalar_tensor_tensor(out=t3[:], in0=a_s[:, 1:NT + 1], scalar=cwfcol[:, j * 3 + 1:j * 3 + 2], in1=t3[:],
                                                                     op0=ALU.mult, op1=ALU.add), reads=[r_as, r_par, r_t3], writes=[r_t3])
                        S.op("dve", lambda e: e.scalar_tensor_tensor(out=t3[:], in0=a_s[:, 0:NT], scalar=cwfcol[:, j * 3:j * 3 + 1], in1=t3[:],
                                                                     op0=ALU.mult, op1=ALU.add), reads=[r_as, r_par, r_t3], writes=[r_t3])
                        S.op("act", lambda e: e.activation(out=sl[:], in_=t3[:], func=AF.Silu), reads=[r_t3], writes=[r_sl])

                    def up_g(j, jj, si_g, actT, r_actT, release):
                        sl, r_sl = sl2[j % 2], r_sl2[j % 2]
                        def ev_gt(b, t0, n):
                            S.op("dve", lambda e: e.tensor_tensor(out=actT[:, jj, t0:t0 + n], in0=psum[:, b, 0:n], in1=sl[:, t0:t0 + n], op=ALU.mult),
                                 reads=[r_bank[b], r_sl], writes=blk(r_actT, t0, n))
                        fm_proj(si_g, jj * 128, hT, r_hT, tbs, ev_gt)
                        if release:
                            free_w(si_g)

                    for gi, (j0, gn) in enumerate(groups):
                        actT, r_actT = actT2[gi % 2], r_actT2[gi % 2]
                        ncol = gn * 128
                        si_a = load_w("w_up", j0 * 128, 8, ncol)
                        si_g = load_w("w_up", DFF + j0 * 128, 8, ncol)
                        si_d = load_w("w_down", 0, gn, 1024, r0=j0 * 128)
                        wd = slot_view(si_d, gn, 1024)
                        S.op("dve", lambda e: e.tensor_tensor(out=wd, in0=wd,
                                                              in1=gt2bc[:].rearrange("p (o n) -> p o n", o=1).to_broadcast([128, gn, 1024]), op=ALU.mult),
                             reads=[r_wslot[si_d], r_par2], writes=[r_wslot[si_d]])
                        for jj in range(gn):
                            j = j0 + jj
                            up_a(j, jj, si_a)
                            if pend_gt is not None:
                                up_g(*pend_gt)
                            pend_gt = (j, jj, si_g, actT, r_actT, jj == gn - 1)
                            if pending is not None and jj == min(1, gn - 1):
                                emit_down(*pending)
                                pending = None
                        free_w(si_a)
                        pending = (gi, gn, si_d, wd)
                    up_g(*pend_gt)
                    if next_ci0 is not None:
                        pf = {}
                        for c in range(2):
                            xi = load_x_tile((next_ci0 + c) * 128)
                            pf[c] = norm_front(xt[xi][:], [r_xt[xi]], G1bc)
                        prefront["nf"] = pf
                    emit_down(*pending, last=True)
                    S.barrier()

        run_stage(15, 5, True, next_ci0=20)
        run_stage(20, 6, False, next_ci0=26)
        run_stage(26, 6, False)
        S.barrier()
        build_nc.stats = (S.ninstr, dict(S.cnt))
        if plan is not None:
            assert W["use"] == len(plan), (W["use"], len(plan))
    return nc


def _col(v, nchunk):
    return np.ascontiguousarray(np.asarray(v, np.float32).reshape(nchunk, 128).T)


def make_in_maps(x, c, w_ada, b_ada, g_norm_mix, w_in, b_in, conv_mix_w, mlstm_head_g, w_proj_conv, w_proj_mlstm,
                 w_out, g_norm_ffn, w_up, conv_ffn_w, w_down, g_final):
    f = lambda a: np.ascontiguousarray(np.asarray(a, dtype=np.float32))
    x = f(x)
    c = f(c)
    b = f(b_in)[0]
    b_col = np.concatenate([_col(b[O_XIN:O_XIN + 1024], 8), _col(b[O_BG:O_BG + 1024], 8), _col(b[O_CG:O_CG + 1024], 8),
                            _col(b[O_Q:O_Q + 512], 4), _col(b[O_K:O_K + 512], 4), _col(b[O_GC:O_GC + 1024], 8),
                            _col(b[O_GM:O_GM + 1024], 8), _col(b[O_O:O_O + 1024], 8)], axis=1)
    b_row = np.concatenate([b[O_K:O_K + 512], b[O_V:O_V + 1024]])[None, :]
    b_gate = b[O_IG:O_IG + 8][None, :]
    cw_col = f(conv_mix_w)[0].reshape(3, 8, 128).transpose(2, 1, 0).reshape(128, 24)
    cwf_col = f(conv_ffn_w)[0].reshape(3, NFF, 128).transpose(2, 1, 0).reshape(128, 3 * NFF)
    shared = {
        "w_ada": f(w_ada)[0], "b_ada": f(b_ada)[0][None, :], "g_norm_mix": f(g_norm_mix)[0][None, :],
        "w_in": f(w_in)[0], "b_col": np.ascontiguousarray(b_col), "b_row": np.ascontiguousarray(b_row),
        "b_gate": np.ascontiguousarray(b_gate), "cw_col": np.ascontiguousarray(cw_col),
        "g_head": f(mlstm_head_g)[0].reshape(1, D), "w_proj_conv": f(w_proj_conv)[0], "w_proj_mlstm": f(w_proj_mlstm)[0],
        "w_out": f(w_out)[0], "g_norm_ffn": f(g_norm_ffn)[0][None, :], "w_up": f(w_up)[0],
        "cwf_col": np.ascontiguousarray(cwf_col), "w_down": f(w_down)[0], "g_final": f(g_final)[None, :],
    }
    in_maps = []
    for core in range(8):
        bi, hf = core // 2, core % 2
        if hf == 1:
            xin = x[bi]
        else:
            xin = np.concatenate([x[bi, 0:2048], x[bi, 0:2048]], axis=0)
        m = dict(shared)
        m["xin"] = np.ascontiguousarray(xin)
        m["ccol"] = _col(c[bi], 8)
        m["flag"] = np.full((128, 1), float(hf), np.float32)
        in_maps.append(m)
    return in_maps


_NC_CACHE = {}


def kernel(**inputs):
    if "nc" not in _NC_CACHE:
        _NC_CACHE["nc"] = build_nc()
    nc = _NC_CACHE["nc"]
    in_maps = make_in_maps(**inputs)
    res = run_bass_kernel_spmd(nc, in_maps, core_ids=list(range(8)))
    out = np.empty((4, 4096, D), np.float32)
    for core in range(8):
        bi, hf = core // 2, core % 2
        out[bi, hf * 2048:(hf + 1) * 2048] = res.results[core]["y"]
    return out
```
